# Optimizing a Trainium2 kernel written in Bass

```python
import functools
import jax, jax.numpy as jnp
from jax import lax
import numpy as np

D_MODEL = 1024
BATCH = 2
SEQ = 8192
DEPTH = 1
DEC_BATCH = 32
DEC_SEQ = 8
PAST_LEN = 16384
PAGE_SIZE = 128

N_HEADS_SB = 8
HEAD_DIM = 64
D_SB = N_HEADS_SB * HEAD_DIM
D_CONV = D_MODEL - D_SB
D_IN = 3 * D_SB + 2 * D_CONV
CONV_W = 31
FFN_CONV_W = 3
D_FF = 2816
Q_BLOCK = 128
LN_EPS = 1e-5
ALPHA = (2.0 * DEPTH) ** 0.25
BETA_INIT = (8.0 * DEPTH) ** -0.25
SB_BIAS_HI = -1.0
SB_BIAS_LO = -8.0

kernel_name = 'stickbreak_conformer_hybrid_step'


def _layernorm(x, g, b):
    xf = x.astype(jnp.float32)
    mu = jnp.mean(xf, axis=-1, keepdims=True)
    var = jnp.mean(jnp.square(xf - mu), axis=-1, keepdims=True)
    return ((xf - mu) * lax.rsqrt(var + LN_EPS) * g + b).astype(x.dtype)


def _rmsnorm(x, g):
    xf = x.astype(jnp.float32)
    ms = jnp.mean(jnp.square(xf), axis=-1, keepdims=True)
    return (xf * lax.rsqrt(ms + LN_EPS) * g).astype(x.dtype)


def _causal_dwconv(u, prev, w, b):
    xp = jnp.concatenate([prev.astype(u.dtype), u], axis=1)
    out = lax.conv_general_dilated(
        xp, w[:, None, :].astype(u.dtype), window_strides=(1,), padding='VALID',
        dimension_numbers=('NWC', 'WIO', 'NWC'), feature_group_count=u.shape[-1])
    return out + b, xp[:, -(w.shape[0] - 1):]


def _sb_block(q, k, v, bias, q_pos, k_pos):
    z = jnp.einsum('bqhd,bkhd->bhqk', q.astype(jnp.float32),
                   k.astype(jnp.float32)) * (HEAD_DIM ** -0.5)
    z = z + bias.astype(jnp.float32)[None, :, None, None]
    mask = k_pos[None, :] < q_pos[:, None]
    log_1mb = jnp.where(mask, jax.nn.log_sigmoid(-z), 0.0)
    after = lax.cumsum(log_1mb, axis=3, reverse=True) - log_1mb
    a = jnp.where(mask, jnp.exp(jax.nn.log_sigmoid(z) + after), 0.0)
    return jnp.einsum('bhqk,bkhd->bqhd', a, v.astype(jnp.float32))


def _attend_prompt(q, k, v, bias):
    B, T, H, Dh = q.shape
    nb = T // Q_BLOCK
    qb = q.reshape(B, nb, Q_BLOCK, H, Dh).swapaxes(0, 1)
    q_pos = jnp.arange(T).reshape(nb, Q_BLOCK)
    k_pos = jnp.arange(T)
    o = lax.map(lambda args: _sb_block(args[0], k, v, bias, args[1], k_pos), (qb, q_pos))
    return o.swapaxes(0, 1).reshape(B, T, H, Dh)


def _attend_with_past(q, k, v, bias, k_past, v_past):
    past = k_past.shape[1]
    T = q.shape[1]
    kk = jnp.concatenate([k_past.astype(k.dtype), k], axis=1)
    vv = jnp.concatenate([v_past.astype(v.dtype), v], axis=1)
    q_pos = past + jnp.arange(T)
    k_pos = jnp.arange(past + T)
    return _sb_block(q, kk, vv, bias, q_pos, k_pos)


def _decoder_layer(x, c, conv_prev, ffn_prev, attend, w_ada, b_ada, w_in, sb_bias, w_dw,
                   b_dw, cln_g, cln_b, gn_sb, gn_conv, w_out, ln1_g, ln1_b, w_up, w_fdw,
                   b_fdw, w_down, ln2_g, ln2_b):
    B, T, _ = x.shape
    mod = jax.nn.silu(c) @ w_ada + b_ada
    sh1, sc1, g1, sh2, sc2, g2 = jnp.split(mod[:, None, :], 6, axis=-1)
    h = x * (1 + sc1) + sh1
    proj = h @ w_in
    q, k, v, a, g = jnp.split(
        proj, [D_SB, 2 * D_SB, 3 * D_SB, 3 * D_SB + D_CONV], axis=-1)
    q = q.reshape(B, T, N_HEADS_SB, HEAD_DIM)
    k = k.reshape(B, T, N_HEADS_SB, HEAD_DIM)
    v = v.reshape(B, T, N_HEADS_SB, HEAD_DIM)
    attn = attend(q, k, v, sb_bias).astype(x.dtype)
    attn = _rmsnorm(attn, gn_sb.reshape(N_HEADS_SB, HEAD_DIM)).reshape(B, T, D_SB)
    u = a * jax.nn.sigmoid(g)
    cv, conv_state = _causal_dwconv(u, conv_prev, w_dw, b_dw)
    cv = jax.nn.silu(_layernorm(cv, cln_g, cln_b))
    mixed = jnp.concatenate([attn, _rmsnorm(cv, gn_conv)], axis=-1) @ w_out
    x = _layernorm(ALPHA * x + (1 + g1) * mixed, ln1_g, ln1_b)
    h2 = x * (1 + sc2) + sh2
    val, gt = jnp.split(h2 @ w_up, 2, axis=-1)
    gt_c, ffn_state = _causal_dwconv(gt, ffn_prev, w_fdw, b_fdw)
    f = (jax.nn.gelu(gt_c, approximate=False) * val) @ w_down
    x = _layernorm(ALPHA * x + (1 + g2) * f, ln2_g, ln2_b)
    return x, k, v, conv_state, ffn_state


def setup_inputs(seed: int = 0) -> dict:
    key = jax.random.key(seed)
    ks = jax.random.split(key, 32)
    f32 = jnp.float32
    n_pages = PAST_LEN // PAGE_SIZE
    n_used = DEC_BATCH * n_pages
    n_pool = n_used + max(1, n_used // 4)

    def nrm(k, shape, s):
        return jax.random.normal(k, shape, f32) * s

    def gain(k, n):
        return 1.0 + 0.02 * jax.random.normal(k, (DEPTH, n), f32)

    def bias(k, n):
        return 0.01 * jax.random.normal(k, (DEPTH, n), f32)

    w_in = nrm(ks[10], (DEPTH, D_MODEL, D_IN), D_MODEL ** -0.5)
    w_in = w_in.at[:, :, 2 * D_SB:3 * D_SB].multiply(BETA_INIT)
    sb_bias = (jnp.linspace(SB_BIAS_HI, SB_BIAS_LO, N_HEADS_SB, dtype=f32)[None, :]
               + 0.05 * jax.random.normal(ks[27], (DEPTH, N_HEADS_SB), f32))
    page_table = jax.random.permutation(ks[6], n_pool)[:n_used].reshape(
        DEC_BATCH, n_pages).astype(jnp.int32)
    return {
        'x_prompt': nrm(ks[0], (BATCH, SEQ, D_MODEL), 1.0),
        'x_sample': nrm(ks[1], (DEC_BATCH, DEC_SEQ, D_MODEL), 1.0),
        'cache_k': nrm(ks[2], (DEPTH, n_pool, PAGE_SIZE, N_HEADS_SB, HEAD_DIM), 1.0),
        'cache_v': nrm(ks[3], (DEPTH, n_pool, PAGE_SIZE, N_HEADS_SB, HEAD_DIM), 1.0),
        'state_conv': nrm(ks[4], (DEPTH, DEC_BATCH, CONV_W - 1, D_CONV), 0.5),
        'state_ffn': nrm(ks[5], (DEPTH, DEC_BATCH, FFN_CONV_W - 1, D_FF), 1.0),
        'page_table': page_table,
        'c_prompt': nrm(ks[7], (BATCH, D_MODEL), 1.0),
        'c_sample': nrm(ks[8], (DEC_BATCH, D_MODEL), 1.0),
        'w_ada': nrm(ks[9], (DEPTH, D_MODEL, 6 * D_MODEL), 0.2 * D_MODEL ** -0.5),
        'b_ada': bias(ks[11], 6 * D_MODEL),
        'w_in': w_in,
        'sb_bias': sb_bias,
        'w_dw': nrm(ks[12], (DEPTH, CONV_W, D_CONV), CONV_W ** -0.5),
        'b_dw': bias(ks[13], D_CONV),
        'cln_g': gain(ks[14], D_CONV),
        'cln_b': bias(ks[15], D_CONV),
        'gn_sb': gain(ks[16], D_SB),
        'gn_conv': gain(ks[17], D_CONV),
        'w_out': nrm(ks[18], (DEPTH, D_MODEL, D_MODEL), BETA_INIT * D_MODEL ** -0.5),
        'ln1_g': gain(ks[19], D_MODEL),
        'ln1_b': bias(ks[20], D_MODEL),
        'w_up': nrm(ks[21], (DEPTH, D_MODEL, 2 * D_FF), D_MODEL ** -0.5),
        'w_fdw': nrm(ks[22], (DEPTH, FFN_CONV_W, D_FF), FFN_CONV_W ** -0.5),
        'b_fdw': bias(ks[23], D_FF),
        'w_down': nrm(ks[24], (DEPTH, D_FF, D_MODEL), BETA_INIT * D_FF ** -0.5),
        'ln2_g': gain(ks[25], D_MODEL),
        'ln2_b': bias(ks[26], D_MODEL),
    }


def reference(x_prompt, x_sample, cache_k, cache_v, state_conv, state_ffn, page_table,
              c_prompt, c_sample, w_ada, b_ada, w_in, sb_bias, w_dw, b_dw, cln_g, cln_b,
              gn_sb, gn_conv, w_out, ln1_g, ln1_b, w_up, w_fdw, b_fdw, w_down, ln2_g,
              ln2_b):
    n_seq, n_pages = page_table.shape
    past_len = n_pages * PAGE_SIZE
    xp, xs = x_prompt, x_sample
    kp_l, vp_l, cp_l, fp_l, ks_l, vs_l, cs_l, fs_l = [], [], [], [], [], [], [], []
    for layer in range(DEPTH):
        params = (w_ada[layer], b_ada[layer], w_in[layer], sb_bias[layer], w_dw[layer],
                  b_dw[layer], cln_g[layer], cln_b[layer], gn_sb[layer], gn_conv[layer],
                  w_out[layer], ln1_g[layer], ln1_b[layer], w_up[layer], w_fdw[layer],
                  b_fdw[layer], w_down[layer], ln2_g[layer], ln2_b[layer])
        conv0 = jnp.zeros((xp.shape[0], CONV_W - 1, D_CONV), xp.dtype)
        ffn0 = jnp.zeros((xp.shape[0], FFN_CONV_W - 1, D_FF), xp.dtype)
        xp, kp, vp, cp, fp = _decoder_layer(xp, c_prompt, conv0, ffn0, _attend_prompt, *params)
        k_past = cache_k[layer, page_table].reshape(n_seq, past_len, N_HEADS_SB, HEAD_DIM)
        v_past = cache_v[layer, page_table].reshape(n_seq, past_len, N_HEADS_SB, HEAD_DIM)
        attend_s = functools.partial(_attend_with_past, k_past=k_past, v_past=v_past)
        xs, ks_, vs_, cs_, fs_ = _decoder_layer(xs, c_sample, state_conv[layer],
                                                state_ffn[layer], attend_s, *params)
        kp_l.append(kp); vp_l.append(vp); cp_l.append(cp); fp_l.append(fp)
        ks_l.append(ks_); vs_l.append(vs_); cs_l.append(cs_); fs_l.append(fs_)
    k_prompt = jnp.stack(kp_l)
    v_prompt = jnp.stack(vp_l)
    conv_prompt = jnp.stack(cp_l)
    ffn_prompt = jnp.stack(fp_l)
    k_sample = jnp.stack(ks_l)
    v_sample = jnp.stack(vs_l)
    conv_sample = jnp.stack(cs_l)
    ffn_sample = jnp.stack(fs_l)
    return (xp, xs, k_prompt, v_prompt, conv_prompt, ffn_prompt,
            k_sample, v_sample, conv_sample, ffn_sample)
```

```python
import numpy as np
import os
from contextlib import ExitStack
import concourse.bass as bass
import concourse.mybir as mybir
from concourse.bass_utils import run_bass_kernel_spmd

F32 = mybir.dt.float32; BF16 = mybir.dt.bfloat16; I32 = mybir.dt.int32
ALU = mybir.AluOpType; AF = mybir.ActivationFunctionType

D = 1024; KC = 8; NSLOT = 8192; NBLK = 64; OWN = 2048; DFF = 2816; NFC = 22
ALPHA = 2.0 ** 0.25; EPS = 1e-5
NQ = 128 + OWN
UOFF = 32
NCOLP = 4 * 5 + 124 + 22 + 66


class Buf:
    def __init__(self, t, name):
        self.t = t; self.name = name
        self.w = None; self.r = []
        self.dsem = None; self.dcount = 0
    def __getitem__(self, idx):
        return self.t[idx]


class Dep:
    def __init__(self, nc):
        self.nc = nc
        self.stacks = [ExitStack()]
        self.eng = {'pe': nc.tensor, 'act': nc.scalar, 'dve': nc.vector, 'pool': nc.gpsimd, 'sp': nc.sync}
        self.sem = {k: nc.alloc_semaphore(name=f"sem_{k}") for k in self.eng}
        self.cnt = {k: 0 for k in self.eng}
        self.seen = {k: {} for k in self.eng}
        self.out_tokens = []
        self.dsems = []
        self.dsem_pool = []
        self.scope_bufs = [[]]
        self.retired = {}

    def push(self):
        self.stacks.append(ExitStack())
        self.scope_bufs.append([])
    def pop(self):
        self.barrier()
        for b in self.scope_bufs.pop():
            if b.dsem is not None:
                self.dsem_pool.append((b.dsem, b.dcount))
                self.dsems = [o for o in self.dsems if o is not b]
                self.retired[id(b.dsem)] = (b.dsem, b.dcount)
                b.dsem = None
        self.stacks.pop().close()
    def sb(self, name, shape, dt):
        b = Buf(self.stacks[-1].enter_context(self.nc.sbuf_tensor(name, shape, dt)), name)
        self.scope_bufs[-1].append(b)
        return b
    def ps(self, name, shape, dt):
        b = Buf(self.stacks[-1].enter_context(self.nc.psum_tensor(name, shape, dt)), name)
        b.psum = True
        return b
    def dram(self, name, shape, dt, kind="Internal"):
        return Buf(self.nc.dram_tensor(name, shape, dt, kind=kind).ap(), name)

    def _wait(self, e, s, v):
        k = id(s)
        if self.seen[e].get(k, 0) >= v: return
        self.eng[e].wait_ge(s, v)
        self.seen[e][k] = v

    def _waits(self, e, reads, writes):
        need = {}
        def add(tok):
            if tok is None: return
            s, v = tok
            k = id(s)
            if k not in need or need[k][1] < v: need[k] = (s, v)
        for b in reads: add(b.w)
        for b in writes:
            add(b.w)
            for t in b.r: add(t)
        for k, (s, v) in need.items():
            self._wait(e, s, v)

    def _record(self, tok, reads, writes):
        for b in reads:
            b.r.append(tok)
            if len(b.r) > 24:
                m = {}
                for s, v in b.r:
                    if id(s) not in m or m[id(s)][1] < v: m[id(s)] = (s, v)
                b.r = list(m.values())
        for b in writes:
            b.w = tok; b.r = []

    def op(self, e, fn, reads=(), writes=()):
        pr = [b for b in reads if getattr(b, 'psum', False)]
        if pr:
            writes = list(writes) + [b for b in pr if b not in writes]
            reads = [b for b in reads if not getattr(b, 'psum', False)]
        self._waits(e, reads, writes)
        inst = fn(self.eng[e])
        self.cnt[e] += 1
        inst.then_inc(self.sem[e], 1)
        self._record((self.sem[e], self.cnt[e]), reads, writes)
        return inst

    def dma(self, q, out, in_, reads, writes, owner, is_output=False, indirect=None, **kw):
        self._waits(q, reads, writes)
        eng = self.eng[q]
        if indirect is not None:
            inst = eng.indirect_dma_start(out=out, out_offset=None, in_=in_, in_offset=indirect, **kw)
        else:
            inst = eng.dma_start(out=out, in_=in_, **kw)
        if owner.dsem is None:
            if self.dsem_pool:
                owner.dsem, owner.dcount = self.dsem_pool.pop()
                self.retired.pop(id(owner.dsem), None)
            else:
                owner.dsem = self.nc.alloc_semaphore(name=f"dsem_{owner.name}")
            self.dsems.append(owner)
        owner.dcount += 16
        inst.then_inc(owner.dsem, 16)
        tok = (owner.dsem, owner.dcount)
        self._record(tok, reads, writes)
        if is_output: self.out_tokens.append(tok)
        return inst

    def barrier(self):
        for e in self.eng:
            for e2 in self.eng:
                if self.cnt[e2] > 0: self._wait(e, self.sem[e2], self.cnt[e2])
            for o in self.dsems:
                self._wait(e, o.dsem, o.dcount)
            for (sm, v) in self.retired.values():
                self._wait(e, sm, v)

    def finish(self):
        self.barrier()


def build_nc(NPOOL=5120, STOP=None):
    nc = bass.Bass("TRN2", target_bir_lowering=False)
    d = Dep(nc)
    op = d.op; dma = d.dma
    def bail():
        d.finish()
        while d.stacks: d.stacks.pop().close()
        return nc
    IN = "ExternalInput"; OUT = "ExternalOutput"
    xs = d.dram("xs", [NSLOT, D], F32, IN)
    vmask = d.dram("vmask", [128, NBLK], F32, IN)
    hflag = d.dram("hflag", [128, 1], F32, IN)
    c5T = d.dram("c5T", [128, KC, 32], F32, IN)
    x_s = d.dram("x_s", [32, D], F32, IN)
    cache_k = d.dram("cache_k", [NPOOL * 128, 512], F32, IN)
    cache_v = d.dram("cache_v", [NPOOL * 128, 512], F32, IN)
    ptab = d.dram("ptab", [4 * 128], I32, IN)
    arange = d.dram("arange", [128, 1], F32, IN)
    st_conv = d.dram("st_conv", [4, 30, 512], F32, IN)
    st_ffn = d.dram("st_ffn", [4, 2, DFF], F32, IN)
    w_ada = d.dram("w_ada", [D, 6 * D], F32, IN)
    b_ada = d.dram("b_ada", [1, 6 * D], F32, IN)
    w_in = d.dram("w_in", [D, 2560], F32, IN)
    sbbias = d.dram("sbbias", [1, 8], F32, IN)
    brow = d.dram("brow", [1, 256], F32, IN)
    colp_d = d.dram("colp", [128, NCOLP], F32, IN)
    gsb_hd = d.dram("gsb_hd", [64, 8], F32, IN)
    w_out = d.dram("w_out", [D, D], F32, IN)
    lnp = d.dram("lnp", [4, D], F32, IN)
    w_up = d.dram("w_up", [D, 2 * DFF], F32, IN)
    w_down = d.dram("w_down", [DFF, D], F32, IN)

    y_p = d.dram("y_p", [OWN, D], F32, OUT)
    k_p = d.dram("k_p", [OWN, 512], F32, OUT)
    v_p = d.dram("v_p", [OWN, 512], F32, OUT)
    conv_p = d.dram("conv_p", [30, 512], F32, OUT)
    ffn_p = d.dram("ffn_p", [2, DFF], F32, OUT)
    y_s = d.dram("y_s", [32, D], F32, OUT)
    k_s = d.dram("k_s", [32, 512], F32, OUT)
    v_s = d.dram("v_s", [32, 512], F32, OUT)
    conv_s = d.dram("conv_s", [4, 30, 512], F32, OUT)
    ffn_s = d.dram("ffn_s", [4, 2, DFF], F32, OUT)

    mod_d = d.dram("mod_d", [5, 6 * D], F32)
    knT_d = d.dram("knT_d", [4, 128, NSLOT], BF16)
    v_d = d.dram("v_d", [NSLOT, 512], BF16)
    qT_d = d.dram("qT_d", [4, 128, NQ], BF16)
    mixT_d = d.dram("mixT_d", [8, 128, NQ], BF16)

    P = [d.ps(f"pb{i}", [128, 512], F32) for i in range(7)]
    PT = d.ps("ptr", [128, 1024], BF16)

    ones_f = d.sb("ones_f", [128, 128], F32)
    ident_f = d.sb("ident_f", [128, 128], F32)
    ident_b = d.sb("ident_b", [128, 128], BF16)
    tri_b = d.sb("tri_b", [128, 128], BF16)
    ones_b = d.sb("ones_b", [128, 128], BF16)
    m512 = d.sb("m512", [128, 128], F32)
    bones = d.sb("bones", [128, 128], BF16)
    shiftI = d.sb("shiftI", [64, 128], BF16)
    Mi = [d.sb(f"Mi{i}", [128, 512], BF16) for i in range(4)]
    Ms = d.sb("Ms", [128, 256], BF16)
    ones512 = d.sb("ones512", [128, 512], BF16)
    mhalf = d.sb("mhalf", [128, 512], F32)
    colp = d.sb("colp_sb", [128, NCOLP], F32)
    sbb = d.sb("sbb", [128, 8], F32)
    vm = d.sb("vm", [128, NBLK], F32)
    hfl = d.sb("hfl", [128, 1], F32)
    G1 = d.sb("G1", [128, D], F32); A2 = d.sb("A2", [128, D], F32)
    B2 = d.sb("B2", [128, D], F32); G2 = d.sb("G2", [128, D], F32)
    gthalo = d.sb("gthalo", [128, NFC, 2], F32)

    op('pool', lambda e: e.memset(ones_f[:], 1.0), [], [ones_f])
    op('pool', lambda e: e.memset(ones_b[:], 1.0), [], [ones_b])
    op('pool', lambda e: e.memset(ones512[:], 1.0), [], [ones512])
    op('pool', lambda e: e.memset(m512[:], 1.0 / 512.0), [], [m512])
    op('pool', lambda e: e.memset(mhalf[:], -0.5), [], [mhalf])
    op('pool', lambda e: e.memset(bones[:], 0.0), [], [bones])
    op('pool', lambda e: e.memset(bones[0:64, 0:64], 1.0 / 64.0), [], [bones])
    op('pool', lambda e: e.memset(bones[64:128, 64:128], 1.0 / 64.0), [], [bones])
    op('pool', lambda e: e.affine_select(out=ident_f[:], in_=ones_f[:], pattern=[[-1, 128]], compare_op=ALU.is_equal, fill=0.0, base=0, channel_multiplier=1), [ones_f], [ident_f])
    op('pool', lambda e: e.tensor_copy(out=ident_b[:], in_=ident_f[:]), [ident_f], [ident_b])
    op('pool', lambda e: e.affine_select(out=tri_b[:], in_=ones_b[:], pattern=[[-1, 128]], compare_op=ALU.is_ge, fill=0.0, base=0, channel_multiplier=1), [ones_b], [tri_b])
    op('pool', lambda e: e.affine_select(out=shiftI[:], in_=ones_b[0:64, :], pattern=[[-1, 128]], compare_op=ALU.is_equal, fill=0.0, base=64, channel_multiplier=1), [ones_b], [shiftI])
    for i in range(4):
        op('pool', lambda e, i=i: e.affine_select(out=Mi[i][:], in_=ones512[:], pattern=[[1, 512]], compare_op=ALU.is_gt, fill=0.0, base=-128 * i, channel_multiplier=-1), [ones512], [Mi[i]])
    op('pool', lambda e: e.affine_select(out=Ms[:], in_=ones512[:, 0:256], pattern=[[0, 32], [1, 8]], compare_op=ALU.is_gt, fill=0.0, base=0, channel_multiplier=-1), [ones512], [Ms])
    dma('sp', colp[:], colp_d[:], [colp_d], [colp], colp)
    dma('sp', sbb[:], sbbias.t.broadcast_to([128, 8]), [sbbias], [sbb], sbb)
    dma('sp', vm[:], vmask[:], [vmask], [vm], vm)
    dma('sp', hfl[:], hflag[:], [hflag], [hfl], hfl)
    C_GNSB, C_GNCV, C_CLNG, C_CLNB, C_BDW, C_WDW, C_BFDW, C_WFDW = 0, 4, 8, 12, 16, 20, 144, 166
    def col(c): return colp[:, c:c + 1]

    if STOP == 'C': return bail()
    d.push()
    cT = d.sb("cT", [128, KC * 32], F32)
    sT = d.sb("sT", [128, KC * 32], F32)
    e5 = d.sb("e5", [128, KC * 32], F32)
    modsb = d.sb("modsb", [5, 6 * D], F32)
    bada = d.sb("bada", [5, 6 * D], F32)
    wa = [d.sb(f"wa{i}", [128, KC, 512], F32) for i in range(2)]
    dma('sp', cT[:], c5T.t.rearrange("p k r -> p (k r)"), [c5T], [cT], cT)
    dma('sp', bada[:], b_ada.t.broadcast_to([5, 6 * D]), [b_ada], [bada], bada)
    op('act', lambda e: e.activation(out=e5[:], in_=cT[:], func=AF.Exp, scale=-1.0), [cT], [e5])
    op('dve', lambda e: e.tensor_scalar(out=e5[:], in0=e5[:], scalar1=1.0, scalar2=None, op0=ALU.add), [e5], [e5])
    op('dve', lambda e: e.reciprocal(out=e5[:], in_=e5[:]), [e5], [e5])
    op('dve', lambda e: e.tensor_tensor(out=sT[:], in0=cT[:], in1=e5[:], op=ALU.mult), [cT, e5], [sT])
    w_ada_v = w_ada.t.rearrange("(k p) n -> p k n", p=128)
    for n in range(12):
        w = wa[n % 2]
        dma('sp', w[:], w_ada_v[:, :, n * 512:(n + 1) * 512], [w_ada], [w], w)
        pb = P[n % 2]
        for k in range(KC):
            op('pe', lambda e, k=k, w=w, pb=pb: e.matmul(pb[0:32, :], lhsT=sT[:, k * 32:(k + 1) * 32], rhs=w[:, k, :], start=(k == 0), stop=(k == KC - 1)), [sT, w], [pb])
        op('dve', lambda e, n=n, pb=pb: e.tensor_tensor(out=modsb[:, n * 512:(n + 1) * 512], in0=pb[0:5, :], in1=bada[:, n * 512:(n + 1) * 512], op=ALU.add), [pb, bada], [modsb])
    for a, b_ in ((1024, 3072), (4096, 6144)):
        op('dve', lambda e, a=a, b_=b_: e.tensor_scalar(out=modsb[:, a:b_], in0=modsb[:, a:b_], scalar1=1.0, scalar2=None, op0=ALU.add), [modsb], [modsb])
    dma('sp', mod_d[:], modsb[:], [modsb], [mod_d], modsb)
    d.pop()

    if STOP == 'A': return bail()
    def load_mod(tile, row, lo, npart=128, p0=0):
        dma('sp', tile[p0:p0 + npart, :], mod_d.t[row:row + 1, lo:lo + D].broadcast_to([npart, D]), [mod_d], [tile], tile)
    load_mod(G1, 0, 2048); load_mod(B2, 0, 3072); load_mod(A2, 0, 4096); load_mod(G2, 0, 5120)

    def modulate_transpose(xt, ntok, A, B, vcol, hb, t1, hT, c0):
        op('dve', lambda e: e.tensor_tensor(out=t1[0:ntok, :], in0=xt[0:ntok, :], in1=A[0:ntok, :], op=ALU.mult), [xt, A], [t1])
        if vcol is not None:
            op('dve', lambda e: e.scalar_tensor_tensor(out=hb[0:ntok, :], in0=B[0:ntok, :], scalar=vcol, in1=t1[0:ntok, :], op0=ALU.mult, op1=ALU.add), [B, t1, vm], [hb])
        else:
            op('dve', lambda e: e.tensor_tensor(out=hb[0:ntok, :], in0=t1[0:ntok, :], in1=B[0:ntok, :], op=ALU.add), [B, t1], [hb])
        for k in range(KC):
            op('pe', lambda e, k=k: e.transpose(out=PT[:, k * 128:k * 128 + ntok], in_=hb[0:ntok, k * 128:(k + 1) * 128], identity=ident_b[0:ntok, 0:ntok]), [hb, ident_b], [PT])
        op('act', lambda e: e.copy(out=hT[:, :, c0:c0 + ntok], in_=PT[:].rearrange("p (k t) -> p k t", k=KC)[:, :, 0:ntok]), [PT], [hT])

    def mm_acc(pb, pslice, lhs_fn, rhs_fn, nk, reads):
        for k in range(nk):
            op('pe', lambda e, k=k: e.matmul(pslice, lhsT=lhs_fn(k), rhs=rhs_fn(k), start=(k == 0), stop=(k == nk - 1)), reads, [pb])

    def layernorm_rows(src, ntok, gam, bet, dst, stats, mv, rstd):
        for hf in range(2):
            op('dve', lambda e, hf=hf: e.bn_stats(out=stats[0:ntok, hf * 6:(hf + 1) * 6], in_=src[0:ntok, hf * 512:(hf + 1) * 512]), [src], [stats])
        op('dve', lambda e: e.bn_aggr(out=mv[0:ntok, :], in_=stats[0:ntok, :]), [stats], [mv])
        op('dve', lambda e: e.tensor_scalar(out=rstd[0:ntok, :], in0=mv[0:ntok, 1:2], scalar1=EPS, scalar2=None, op0=ALU.add), [mv], [rstd])
        op('pool', lambda e: e.tensor_tensor(out=rstd[0:ntok, :], in0=rstd[0:ntok, :], in1=mhalf[0:ntok, 0:1], op=ALU.pow), [rstd, mhalf], [rstd])
        op('dve', lambda e: e.tensor_scalar(out=dst[0:ntok, :], in0=src[0:ntok, :], scalar1=mv[0:ntok, 0:1], scalar2=rstd[0:ntok, 0:1], op0=ALU.subtract, op1=ALU.mult), [src, mv, rstd], [dst])
        op('dve', lambda e: e.tensor_tensor(out=dst[0:ntok, :], in0=dst[0:ntok, :], in1=gam[0:ntok, :], op=ALU.mult), [dst, gam], [dst])
        op('dve', lambda e: e.tensor_tensor(out=dst[0:ntok, :], in0=dst[0:ntok, :], in1=bet[0:ntok, :], op=ALU.add), [dst, bet], [dst])

    def colstat_rstd(srcs, n, lhs, npart, pb, tmp, wk):
        for i, (b, ap) in enumerate(srcs):
            op('dve', lambda e, ap=ap, i=i: e.tensor_tensor(out=wk[i][0:npart, 0:n], in0=ap, in1=ap, op=ALU.mult), [b], [wk[i]])
        for i in range(len(srcs)):
            op('pe', lambda e, i=i: e.matmul(pb[0:npart, 0:n], lhsT=lhs, rhs=wk[i][0:npart, 0:n], start=(i == 0), stop=(i == len(srcs) - 1)), [wk[i], m512, bones], [pb])
        op('dve', lambda e: e.tensor_scalar(out=tmp[0:npart, 0:n], in0=pb[0:npart, 0:n], scalar1=EPS, scalar2=None, op0=ALU.add), [pb], [tmp])
        op('pool', lambda e: e.tensor_tensor(out=tmp[0:npart, 0:n], in0=tmp[0:npart, 0:n], in1=mhalf[0:npart, 0:n], op=ALU.pow), [tmp, mhalf], [tmp])

    d.push()
    dwd = d.sb("dwd", [128, 124, 128], BF16)
    for i in range(124):
        op('dve' if i % 2 else 'pool', lambda e, i=i: e.tensor_scalar(out=dwd[:, i, :], in0=ident_f[:], scalar1=col(C_WDW + i), scalar2=None, op0=ALU.mult), [ident_f, colp], [dwd])
    wkf = dict(sq=[d.sb(f"csq{i}", [128, 512], F32) for i in range(4)], tmp=d.sb("ctmp", [128, 512], F32), sg=d.sb("csg", [128, 512], F32))
    d.push()
    uT_b = d.sb("uT_b", [128, 4, UOFF + NQ], BF16)
    op('pool', lambda e: e.memset(uT_b[:], 0.0), [], [uT_b])
    d.push()
    A1 = d.sb("A1", [128, D], F32); B1 = d.sb("B1", [128, D], F32)
    load_mod(B1, 0, 0); load_mod(A1, 0, 1024)
    winb = d.sb("winb", [128, KC, 2560], BF16)
    for k in range(KC):
        dma('pool', winb[:, k, :], w_in.t[k * 128:(k + 1) * 128, :], [w_in], [winb], winb)
    xt = [d.sb(f"xt{i}", [128, D], F32) for i in range(3)]
    t1 = d.sb("t1", [128, D], F32)
    hb = [d.sb(f"hb{i}", [128, D], BF16) for i in range(2)]
    hT = [d.sb(f"hT{i}", [128, KC, 512], BF16) for i in range(2)]
    kn_sb = [d.sb(f"kn_sb{i}", [128, 512], BF16) for i in range(2)]
    q_sb = [d.sb(f"q_sb{i}", [128, 512], BF16) for i in range(2)]
    v_sb = [d.sb(f"v_sb{i}", [128, 512], BF16) for i in range(2)]
    vf_sb = [d.sb(f"vf_sb{i}", [128, 512], F32) for i in range(2)]
    kf_sb = [d.sb(f"kf_sb{i}", [128, 512], F32) for i in range(2)]
    sig = d.sb("sig", [128, 512], F32)
    uf = [d.sb(f"uf{i}", [128, 512], F32) for i in range(4)]

    def proj_group(hTg, ncol, sbi, is_own, is_halo, c_lo):
        cs = slice(c_lo, c_lo + ncol)
        for pc in range(4):
            pb = P[pc % 2]
            mm_acc(pb, pb[:, 0:ncol], lambda k, pc=pc: winb[:, k, 512 + pc * 128:512 + (pc + 1) * 128], lambda k: hTg[:, k, cs], KC, [winb, hTg])
            ks = kn_sb[pc % 2]
            op('act', lambda e, pb=pb, ks=ks: e.activation(out=ks[:, 0:ncol], in_=pb[:, 0:ncol], func=AF.Identity, scale=-0.125), [pb], [ks])
            yield ('kn', pc, ks)
        if is_own or is_halo:
            for pc in range(0 if os.environ.get('SKIP_Q') else 4):
                pb = P[2 + pc % 2]
                mm_acc(pb, pb[:, 0:ncol], lambda k, pc=pc: winb[:, k, pc * 128:(pc + 1) * 128], lambda k: hTg[:, k, cs], KC, [winb, hTg])
                qs = q_sb[pc % 2]
                op('act' if os.environ.get('E1') else 'dve', lambda e, pb=pb, qs=qs: (e.copy if os.environ.get('E1') else e.tensor_copy)(out=qs[:, 0:ncol], in_=pb[:, 0:ncol]), [pb], [qs])
                yield ('q', pc, qs)
            for cc in range(0 if os.environ.get('SKIP_U') else 4):
                pa, pg = P[4], P[5]
                mm_acc(pa, pa[:, 0:ncol], lambda k, cc=cc: winb[:, k, 1536 + cc * 128:1536 + (cc + 1) * 128], lambda k: hTg[:, k, cs], KC, [winb, hTg])
                mm_acc(pg, pg[:, 0:ncol], lambda k, cc=cc: winb[:, k, 2048 + cc * 128:2048 + (cc + 1) * 128], lambda k: hTg[:, k, cs], KC, [winb, hTg])
                op('act', lambda e: e.activation(out=sig[:, 0:ncol], in_=pg[:, 0:ncol], func=AF.Sigmoid), [pg], [sig])
                op('dve', lambda e, cc=cc: e.tensor_tensor(out=uf[cc][:, 0:ncol], in0=pa[:, 0:ncol], in1=sig[:, 0:ncol], op=ALU.mult), [pa, sig], [uf[cc]])
                yield ('u', cc, uf[cc])

    xi = 0
    for sbi in range(16):
        if STOP == 'P1c' and sbi >= 1: break
        if STOP == 'P1d' and sbi not in (0, 11): continue
        if STOP == 'P1e' and sbi not in (0, 12): continue
        if STOP == 'P1f' and sbi not in (0, 15): continue
        is_own = sbi >= 12
        is_halo_sb = sbi == 11
        hTg = hT[sbi % 2]
        for bl in range(4):
            blk = sbi * 4 + bl
            x_t = xt[xi % 3]; xi += 1
            dma('sp', x_t[:], xs.t[blk * 128:(blk + 1) * 128, :], [xs], [x_t], x_t)
            modulate_transpose(x_t, 128, A1, B1, vm[:, blk:blk + 1], hb[blk % 2], t1, hTg, bl * 128)
        if STOP == 'P1a': break
        for bl in range(4):
            blk = sbi * 4 + bl
            pb = P[6]
            mm_acc(pb, pb[:, :], lambda k, bl=bl: hTg[:, k, bl * 128:(bl + 1) * 128], lambda k: winb[:, k, 1024:1536], KC, [winb, hTg])
            vs_ = v_sb[blk % 2]
            op('act', lambda e, vs_=vs_: e.copy(out=vs_[:], in_=pb[:]), [pb], [vs_])
            dma('sp', v_d.t[blk * 128:(blk + 1) * 128, :], vs_[:], [vs_], [v_d], vs_)
            if is_own and not os.environ.get('SKIP_OWNOUT'):
                vf = vf_sb[blk % 2]
                op('dve', lambda e, vf=vf: e.tensor_copy(out=vf[:], in_=pb[:]), [pb], [vf])
                r0 = (blk - 48) * 128
                dma('sp', v_p.t[r0:r0 + 128, :], vf[:], [vf], [v_p], vf, is_output=True)
                pk = P[3]
                mm_acc(pk, pk[:, :], lambda k, bl=bl: hTg[:, k, bl * 128:(bl + 1) * 128], lambda k: winb[:, k, 512:1024], KC, [winb, hTg])
                kf = kf_sb[blk % 2]
                op('dve', lambda e, kf=kf: e.tensor_copy(out=kf[:], in_=pk[:]), [pk], [kf])
                dma('sp', k_p.t[r0:r0 + 128, :], kf[:], [kf], [k_p], kf, is_output=True)
        if STOP == 'P1b': break
        if is_halo_sb:
            groups = [(512, 0, False, False), (128, 384, False, True)]
        else:
            groups = [(512, 0, is_own and not os.environ.get('SKIP_OWNPROJ'), False)]
        for gi, (ncol, c_lo, own_, halo_) in enumerate(groups):
            only_qu = (gi == 1)
            for kind, idx, buf in proj_group(hTg, ncol, sbi, own_, halo_, c_lo):
                if kind == 'kn':
                    if only_qu: continue
                    dma('sp', knT_d.t[idx, :, sbi * 512:sbi * 512 + ncol], buf[:, 0:ncol], [buf], [knT_d], buf)
                elif kind == 'q':
                    qc0 = 0 if halo_ else 128 + (sbi - 12) * 512
                    dma('sp', qT_d.t[idx, :, qc0:qc0 + ncol], buf[:, 0:ncol], [buf], [qT_d], buf)
                else:
                    uc0 = UOFF + (0 if halo_ else 128 + (sbi - 12) * 512)
                    op('dve' if os.environ.get('E2') else 'pool', lambda e, idx=idx, buf=buf, uc0=uc0, ncol=ncol: e.tensor_copy(out=uT_b[:, idx, uc0:uc0 + ncol], in_=buf[:, 0:ncol]), [buf], [uT_b])
                    if sbi == 15:
                        dma('sp', conv_p.t[:, idx * 128:(idx + 1) * 128].rearrange("t p -> p t"), buf[:, 482:512], [buf], [conv_p], buf, is_output=True, allow_slow_non_contiguous=True)
    d.pop()

    if STOP in ('P1', 'P1a', 'P1b', 'P1c', 'P1d', 'P1e', 'P1f'): return bail()
    def attn_chain(steps, nq, zmms, avmm, ob_list, bias_ap, wk, tagreads):
        e_sb, sp_sb, a_sb, accf, accb = wk['e'], wk['sp'], wk['a'], wk['accf'], wk['accb']
        ns = len(steps)
        for si, st in enumerate(steps):
            first = si == 0; last = si == ns - 1
            pz = P[si % 2]; py = P[2 + si % 2]
            zmms(si, pz, True, True)
            es_ = e_sb[si % 2]; sp_ = sp_sb[si % 2]; a_ = a_sb[si % 2]
            if bias_ap is not None:
                op('act', lambda e: e.activation(out=es_[:, 0:nq], in_=pz[:, 0:nq], func=AF.Exp, scale=-1.0, bias=bias_ap), [pz, sbb], [es_])
            else:
                op('act', lambda e: e.activation(out=es_[:, 0:nq], in_=pz[:, 0:nq], func=AF.Exp, scale=-1.0), [pz], [es_])
            op('act', lambda e: e.activation(out=sp_[:, 0:nq], in_=es_[:, 0:nq], func=AF.Ln, bias=1.0, scale=1.0), [es_], [sp_])
            if st['mask'] is not None:
                mk = st['mask']
                op('pool', lambda e: e.tensor_tensor(out=sp_[:, 0:nq], in0=sp_[:, 0:nq], in1=mk[:, 0:nq], op=ALU.mult), [sp_, mk], [sp_])
            op('pe', lambda e: e.matmul(py[:, 0:nq], lhsT=tri_b[:], rhs=sp_[:, 0:nq], start=True, stop=False), [tri_b, sp_], [py])
            if not first:
                ab = accb[(si - 1) % 2]
                op('pe', lambda e: e.matmul(py[:, 0:nq], lhsT=ones_b[:], rhs=ab[:, 0:nq], start=False, stop=False), [ones_b, ab], [py])
            zmms(si, py, False, True)
            if bias_ap is not None:
                op('act', lambda e: e.activation(out=a_[:, 0:nq], in_=py[:, 0:nq], func=AF.Exp, scale=-1.0, bias=bias_ap), [py, sbb], [a_])
            else:
                op('act', lambda e: e.activation(out=a_[:, 0:nq], in_=py[:, 0:nq], func=AF.Exp, scale=-1.0), [py], [a_])
            if st['mask'] is not None:
                op('dve', lambda e: e.tensor_tensor(out=a_[:, 0:nq], in0=a_[:, 0:nq], in1=mk[:, 0:nq], op=ALU.mult), [a_, mk], [a_])
            avmm(si, a_, first, last)
            if not last:
                if first:
                    op('pool', lambda e: e.tensor_copy(out=accf[:, 0:nq], in_=sp_[:, 0:nq]), [sp_], [accf])
                else:
                    op('pool', lambda e: e.tensor_tensor(out=accf[:, 0:nq], in0=accf[:, 0:nq], in1=sp_[:, 0:nq], op=ALU.add), [accf, sp_], [accf])
                ab2 = accb[si % 2]
                op('dve', lambda e: e.tensor_copy(out=ab2[:, 0:nq], in_=accf[:, 0:nq]), [accf], [ab2])

    d.push()
    knT = d.sb("knT", [128, NSLOT], BF16)
    vpair = d.sb("vpair", [128, NBLK, 128], BF16)
    qT = d.sb("qT", [128, NQ], BF16)
    wk = dict(e=[d.sb(f"e_sb{i}", [128, 512], F32) for i in range(2)],
              sp=[d.sb(f"sp_sb{i}", [128, 512], BF16) for i in range(2)],
              a=[d.sb(f"a_sb{i}", [128, 512], BF16) for i in range(2)],
              accf=d.sb("accf", [128, 512], F32),
              accb=[d.sb(f"accb{i}", [128, 512], BF16) for i in range(2)])
    opair = d.sb("opair", [128, NQ], F32)
    sqw = [d.sb("sqw0", [128, 512], BF16)]
    rtmp = d.sb("rtmp", [128, 512], F32)
    mixs = d.sb("mixs", [128, NQ], BF16)
    v_dv = v_d.t.rearrange("(b p) c -> p b c", p=128)
    for pc in range(4):
        dma('sp', knT[:], knT_d.t[pc], [knT_d], [knT], knT)
        for q4 in range(4):
            dma('sp', vpair[:, q4 * 16:(q4 + 1) * 16, :], v_dv[:, q4 * 16:(q4 + 1) * 16, pc * 128:(pc + 1) * 128], [v_d], [vpair], vpair)
        dma('sp', qT[:], qT_d.t[pc], [qT_d], [qT], qT)
        for h in range(2):
            hs = slice(64 * h, 64 * h + 64)
            head = pc * 2 + h
            bias_ap = sbb[:, head:head + 1]
            glist = [(0, 128, 47, 1)] + [(128 + 512 * g, 512, 48 + 4 * g, 4) for g in range(4)]
            for (qc0, nq, diag0, ndiag) in glist:
                top = diag0 + ndiag - 1
                kbs = list(range(top, -1, -1))
                steps = [dict(mask=(Mi[kb - diag0] if kb >= diag0 else None)) for kb in kbs]
                po = P[4 + h]
                def zmms(si, pb, start, stop_unused, kbs=kbs, qc0=qc0, nq=nq, hs=hs):
                    kb = kbs[si]
                    op('pe', lambda e: e.matmul(pb[:, 0:nq], lhsT=knT[hs, kb * 128:(kb + 1) * 128], rhs=qT[hs, qc0:qc0 + nq], start=start, stop=True), [knT, qT], [pb])
                def avmm(si, a_, first, last, kbs=kbs, nq=nq, po=po):
                    kb = kbs[si]
                    op('pe', lambda e: e.matmul(po[:, 0:nq], lhsT=vpair[:, kb, :], rhs=a_[:, 0:nq], start=first, stop=last), [vpair, a_], [po])
                attn_chain(steps, nq, zmms, avmm, None, bias_ap, wk, None)
                op('act', lambda e, po=po, qc0=qc0, nq=nq, hs=hs: e.copy(out=opair[hs, qc0:qc0 + nq], in_=po[hs, 0:nq]), [po], [opair])
        for c0 in range(0, NQ, 512):
            n = min(512, NQ - c0)
            colstat_rstd([(opair, opair[:, c0:c0 + n])], n, bones[:], 128, P[6], rtmp, sqw)
            op('dve', lambda e, c0=c0, n=n, pc=pc: e.scalar_tensor_tensor(out=mixs[:, c0:c0 + n], in0=opair[:, c0:c0 + n], scalar=col(C_GNSB + pc), in1=rtmp[:, 0:n], op0=ALU.mult, op1=ALU.mult), [opair, colp, rtmp], [mixs])
        dma('sp', mixT_d.t[pc], mixs[:], [mixs], [mixT_d], mixs)
    d.pop()

    if STOP == 'P2': return bail()
    def conv_post(cvT, n, outfn, wkf, pb1):
        sq = wkf['sq']; tmp = wkf['tmp']; sg = wkf['sg']
        for cc in range(4):
            op('pe', lambda e, cc=cc: e.matmul(pb1[:, 0:n], lhsT=m512[:], rhs=cvT[:, cc, 0:n], start=(cc == 0), stop=(cc == 3)), [m512, cvT], [pb1])
        for cc in range(4):
            op('dve', lambda e, cc=cc: e.tensor_tensor(out=cvT[:, cc, 0:n], in0=cvT[:, cc, 0:n], in1=pb1[:, 0:n], op=ALU.subtract), [cvT, pb1], [cvT])
        colstat_rstd([(cvT, cvT[:, cc, 0:n]) for cc in range(4)], n, m512[:], 128, pb1, tmp, sq)
        for cc in range(4):
            op('dve', lambda e, cc=cc: e.tensor_tensor(out=cvT[:, cc, 0:n], in0=cvT[:, cc, 0:n], in1=tmp[:, 0:n], op=ALU.mult), [cvT, tmp], [cvT])
            op('dve', lambda e, cc=cc: e.tensor_scalar(out=cvT[:, cc, 0:n], in0=cvT[:, cc, 0:n], scalar1=col(C_CLNG + cc), scalar2=col(C_CLNB + cc), op0=ALU.mult, op1=ALU.add), [cvT, colp], [cvT])
            op('act', lambda e, cc=cc: e.activation(out=sg[:, 0:n], in_=cvT[:, cc, 0:n], func=AF.Sigmoid), [cvT], [sg])
            op('dve', lambda e, cc=cc: e.tensor_tensor(out=cvT[:, cc, 0:n], in0=cvT[:, cc, 0:n], in1=sg[:, 0:n], op=ALU.mult), [cvT, sg], [cvT])
        colstat_rstd([(cvT, cvT[:, cc, 0:n]) for cc in range(4)], n, m512[:], 128, pb1, tmp, sq)
        for cc in range(4):
            op('dve', lambda e, cc=cc: e.scalar_tensor_tensor(out=cvT[:, cc, 0:n], in0=cvT[:, cc, 0:n], scalar=col(C_GNCV + cc), in1=tmp[:, 0:n], op0=ALU.mult, op1=ALU.mult), [cvT, colp, tmp], [cvT])
            outfn(cc)

    d.push()
    cvw = [d.sb(f"cvw{i}", [128, 4, 512], F32) for i in range(2)]
    mixc = [d.sb(f"mixc{i}", [128, 4, 512], BF16) for i in range(2)]
    for ci, c0 in enumerate(range(0, NQ, 512)):
        n = min(512, NQ - c0)
        cv = cvw[ci % 2]; mx = mixc[ci % 2]
        for cc in range(4):
            pb = P[cc % 2]
            for k in range(31):
                op('pe', lambda e, k=k, cc=cc, pb=pb: e.matmul(pb[:, 0:n], lhsT=dwd[:, k * 4 + cc, :], rhs=uT_b[:, cc, c0 + 2 + k:c0 + 2 + k + n], start=(k == 0), stop=(k == 30)), [dwd, uT_b], [pb])
            op('act', lambda e, cc=cc, pb=pb: e.activation(out=cv[:, cc, 0:n], in_=pb[:, 0:n], func=AF.Identity, bias=col(C_BDW + cc), scale=1.0), [pb, colp], [cv])
        def outfn(cc, cv=cv, mx=mx, n=n):
            op('pool', lambda e: e.tensor_copy(out=mx[:, cc, 0:n], in_=cv[:, cc, 0:n]), [cv], [mx])
        conv_post(cv, n, outfn, wkf, P[2])
        for cc in range(4):
            dma('sp', mixT_d.t[4 + cc, :, c0:c0 + n], mx[:, cc, 0:n], [mx], [mixT_d], mx)

    d.pop()
    d.pop()
    if STOP == 'P3': return bail()
    d.push()
    A1s = d.sb("A1s", [32, D], F32); B1s = d.sb("B1s", [32, D], F32)
    for s_ in range(4):
        load_mod(B1s, 1 + s_, 0, 8, 8 * s_); load_mod(A1s, 1 + s_, 1024, 8, 8 * s_)
    winb2 = d.sb("winb2", [128, KC, 2560], BF16)
    for k in range(KC):
        dma('pool', winb2[:, k, :], w_in.t[k * 128:(k + 1) * 128, :], [w_in], [winb2], winb2)
    xts = d.sb("xts", [32, D], F32); t1s = d.sb("t1s", [32, D], F32); hbs = d.sb("hbs", [32, D], BF16)
    hTs = d.sb("hTs", [128, KC, 32], BF16)
    dma('sp', xts[:], x_s[:], [x_s], [xts], xts)
    modulate_transpose(xts, 32, A1s, B1s, None, hbs, t1s, hTs, 0)
    knT_n = d.sb("knT_n", [128, 4, 128], BF16)
    qT_n = d.sb("qT_n", [128, 4, 32], BF16)
    vnew = d.sb("vnew", [128, 4, 512], BF16)
    vtok = d.sb("vtok", [32, 512], F32); ktok = d.sb("ktok", [32, 512], F32)
    vtokb = d.sb("vtokb", [32, 512], BF16)
    us_f = d.sb("us_f", [128, 4, 32], F32)
    sgs = d.sb("sgs", [128, 32], F32)
    knT_f = d.sb("knT_f", [128, 4, 32], BF16)
    for pc in range(4):
        pb = P[pc % 2]
        mm_acc(pb, pb[:, 0:32], lambda k, pc=pc: winb2[:, k, 512 + pc * 128:512 + (pc + 1) * 128], lambda k: hTs[:, k, :], KC, [winb2, hTs])
        op('act', lambda e, pb=pb, pc=pc: e.activation(out=knT_f[:, pc, :], in_=pb[:, 0:32], func=AF.Identity, scale=-0.125), [pb], [knT_f])
        pq = P[2 + pc % 2]
        mm_acc(pq, pq[:, 0:32], lambda k, pc=pc: winb2[:, k, pc * 128:(pc + 1) * 128], lambda k: hTs[:, k, :], KC, [winb2, hTs])
        op('dve', lambda e, pq=pq, pc=pc: e.tensor_copy(out=qT_n[:, pc, :], in_=pq[:, 0:32]), [pq], [qT_n])
    pv = P[4]
    mm_acc(pv, pv[0:32, :], lambda k: hTs[:, k, :], lambda k: winb2[:, k, 1024:1536], KC, [winb2, hTs])
    op('dve', lambda e: e.tensor_copy(out=vtok[:], in_=pv[0:32, :]), [pv], [vtok])
    op('act', lambda e: e.copy(out=vtokb[:], in_=pv[0:32, :]), [pv], [vtokb])
    dma('sp', v_s[:], vtok[:], [vtok], [v_s], vtok, is_output=True)
    pk = P[5]
    mm_acc(pk, pk[0:32, :], lambda k: hTs[:, k, :], lambda k: winb2[:, k, 512:1024], KC, [winb2, hTs])
    op('dve', lambda e: e.tensor_copy(out=ktok[:], in_=pk[0:32, :]), [pk], [ktok])
    dma('sp', k_s[:], ktok[:], [ktok], [k_s], ktok, is_output=True)
    op('pool', lambda e: e.memset(vnew[:], 0.0), [], [vnew])
    op('pool', lambda e: e.memset(knT_n[:], 0.0), [], [knT_n])
    for s_ in range(4):
        dma('sp', vnew[0:8, s_, :], vtokb[8 * s_:8 * s_ + 8, :], [vtokb], [vnew], vnew)
    for cc in range(4):
        pa, pg = P[0], P[1]
        mm_acc(pa, pa[:, 0:32], lambda k, cc=cc: winb2[:, k, 1536 + cc * 128:1536 + (cc + 1) * 128], lambda k: hTs[:, k, :], KC, [winb2, hTs])
        mm_acc(pg, pg[:, 0:32], lambda k, cc=cc: winb2[:, k, 2048 + cc * 128:2048 + (cc + 1) * 128], lambda k: hTs[:, k, :], KC, [winb2, hTs])
        op('act', lambda e: e.activation(out=sgs[:], in_=pg[:, 0:32], func=AF.Sigmoid), [pg], [sgs])
        op('dve', lambda e, cc=cc: e.tensor_tensor(out=us_f[:, cc, :], in0=pa[:, 0:32], in1=sgs[:], op=ALU.mult), [pa, sgs], [us_f])
    u_ext = d.sb("u_ext", [128, 4, 4, 38], BF16)
    stc = d.sb("stc", [120, 512], F32)
    dma('sp', stc[:], st_conv.t.rearrange("s t c -> (s t) c"), [st_conv], [stc], stc)
    for cc in range(4):
        pb = P[2 + cc % 2]
        op('pe', lambda e, cc=cc, pb=pb: e.transpose(out=pb[:, 0:120], in_=stc[:, cc * 128:(cc + 1) * 128], identity=ident_f[0:120, 0:120]), [stc, ident_f], [pb])
        op('dve', lambda e, cc=cc, pb=pb: e.tensor_copy(out=u_ext[:, cc, :, 0:30], in_=pb[:, 0:120].rearrange("p (s t) -> p s t", s=4)), [pb], [u_ext])
        op('pool', lambda e, cc=cc: e.tensor_copy(out=u_ext[:, cc, :, 30:38], in_=us_f[:, cc, :].rearrange("p (s t) -> p s t", s=4)), [us_f], [u_ext])
        for s_ in range(4):
            dma('sp', conv_s.t[s_, 22:30, cc * 128:(cc + 1) * 128].rearrange("t p -> p t"), us_f[:, cc, 8 * s_:8 * s_ + 8], [us_f], [conv_s], us_f, is_output=True, allow_slow_non_contiguous=True)
    cps = d.sb("cps", [88, 512], F32)
    for s_ in range(4):
        dma('sp', cps[22 * s_:22 * s_ + 22, :], st_conv.t[s_, 8:30, :], [st_conv], [cps], cps)
    for s_ in range(4):
        dma('sp', conv_s.t[s_, 0:22, :], cps[22 * s_:22 * s_ + 22, :], [cps], [conv_s], cps, is_output=True)
    cvs = d.sb("cvs", [128, 4, 32], F32)
    mixTs = d.sb("mixTs", [128, 8, 32], BF16)
    for cc in range(4):
        pb = P[cc % 2]
        for k in range(31):
            op('pe', lambda e, k=k, cc=cc, pb=pb: e.matmul(pb[:, 0:32], lhsT=dwd[:, k * 4 + cc, :], rhs=u_ext[:, cc, :, k:k + 8], start=(k == 0), stop=(k == 30)), [dwd, u_ext], [pb])
        op('act', lambda e, cc=cc, pb=pb: e.activation(out=cvs[:, cc, :], in_=pb[:, 0:32], func=AF.Identity, bias=col(C_BDW + cc), scale=1.0), [pb, colp], [cvs])
    def outfn_s(cc):
        op('pool', lambda e: e.tensor_copy(out=mixTs[:, 4 + cc, :], in_=cvs[:, cc, :]), [cvs], [mixTs])
    conv_post(cvs, 32, outfn_s, wkf, P[2])

    if STOP == 'S1': return bail()
    Qbd = d.sb("Qbd", [128, 4, 4, 16], BF16)
    op('pool', lambda e: e.memset(Qbd[:], 0.0), [], [Qbd])
    for pc in range(4):
        op('dve', lambda e, pc=pc: e.tensor_copy(out=Qbd[0:64, pc, :, 0:8], in_=qT_n[0:64, pc, :].rearrange("p (s t) -> p s t", s=4)), [qT_n], [Qbd])
        op('dve', lambda e, pc=pc: e.tensor_copy(out=Qbd[64:128, pc, :, 8:16], in_=qT_n[64:128, pc, :].rearrange("p (s t) -> p s t", s=4)), [qT_n], [Qbd])
    negb = d.sb("negb", [1, 256], F32)
    onesrow = d.sb("onesrow", [1, 128], F32)
    dma('sp', negb[:], brow[:], [brow], [negb], negb)
    op('dve', lambda e: e.tensor_scalar(out=negb[:], in0=negb[:], scalar1=-1.0, scalar2=None, op0=ALU.mult), [negb], [negb])
    op('pool', lambda e: e.memset(onesrow[:], 1.0), [], [onesrow])
    knT_ns = d.sb("knT_ns", [128, 4, 4, 128], BF16)
    op('pool', lambda e: e.memset(knT_ns[:], 0.0), [], [knT_ns])
    for s_ in range(4):
        op('dve', lambda e, s_=s_: e.tensor_copy(out=knT_ns[:, s_, :, 0:8], in_=knT_f[:, :, 8 * s_:8 * s_ + 8]), [knT_f], [knT_ns])
    pt_i = d.sb("pt_i", [128, 512], I32)
    ar_f = d.sb("ar_f", [128, 1], F32)
    offs = d.sb("offs", [128, 512], I32)
    dma('sp', pt_i[:], ptab.t.partition_broadcast(128), [ptab], [pt_i], pt_i)
    dma('sp', ar_f[:], arange[:], [arange], [ar_f], ar_f)
    op('dve', lambda e: e.tensor_scalar(out=offs[:], in0=pt_i[:], scalar1=128.0, scalar2=ar_f[:, 0:1], op0=ALU.mult, op1=ALU.add), [pt_i, ar_f], [offs])
    NPG = 3
    kpg = [[d.sb(f"kpg{i}_{s_}", [128, 512], BF16) for s_ in range(4)] for i in range(NPG)]
    vpg = [[d.sb(f"vpg{i}_{s_}", [128, 512], BF16) for s_ in range(4)] for i in range(NPG)]
    kTp = [d.sb(f"kTp{i}", [128, 4, 4, 128], BF16) for i in range(2)]
    wks = dict(e=[d.sb(f"se_sb{i}", [128, 256], F32) for i in range(2)],
               sp=[d.sb(f"ssp_sb{i}", [128, 256], BF16) for i in range(2)],
               a=[d.sb(f"sa_sb{i}", [128, 256], BF16) for i in range(2)],
               accf=d.sb("saccf", [128, 256], F32),
               accb=[d.sb(f"saccb{i}", [128, 256], BF16) for i in range(2)])
    steps = [dict(mask=Ms)] + [dict(mask=None) for _ in range(128)]
    state = {}
    def prep_page(si):
        pg = 128 - si
        i = si % NPG
        for s_ in range(4):
            cidx = s_ * 128 + pg
            dma('pool', kpg[i][s_][:], cache_k.t, [cache_k, offs], [kpg[i][s_]], kpg[i][s_], indirect=bass.IndirectOffsetOnAxis(ap=offs[:, cidx:cidx + 1], axis=0))
            dma('pool', vpg[i][s_][:], cache_v.t, [cache_v, offs], [vpg[i][s_]], vpg[i][s_], indirect=bass.IndirectOffsetOnAxis(ap=offs[:, cidx:cidx + 1], axis=0))
    def prep_kT(si):
        i = si % NPG
        kt = kTp[si % 2]
        for s_ in range(4):
            for pc in range(4):
                op('pe', lambda e, s_=s_, pc=pc: e.transpose(out=PT[:, pc * 128:(pc + 1) * 128], in_=kpg[i][s_][:, pc * 128:(pc + 1) * 128], identity=ident_b[:]), [kpg[i][s_], ident_b], [PT])
            op('dve', lambda e, s_=s_: e.tensor_scalar(out=kt[:, s_, :, :], in0=PT[:, 0:512].rearrange("p (c k) -> p c k", c=4), scalar1=-0.125, scalar2=None, op0=ALU.mult), [PT], [kt])
    def zmms_s(si, pb, start, stop_unused):
        kt = knT_ns if si == 0 else kTp[si % 2]
        op('pe', lambda e: e.matmul(pb[:, 0:256], lhsT=onesrow[:], rhs=negb[:], start=start, stop=False), [onesrow, negb], [pb])
        n = 0
        for s_ in range(4):
            for pc in range(4):
                n += 1
                c0 = s_ * 64 + pc * 16
                op('pe', lambda e, s_=s_, pc=pc, c0=c0, n=n: e.matmul(pb[:, c0:c0 + 16], lhsT=kt[:, s_, pc, :], rhs=Qbd[:, pc, s_, :], start=False, stop=(n == 16)), [kt, Qbd], [pb])
    po_s = P[6]
    def avmm_s(si, a_, first, last):
        for s_ in range(4):
            vsrc = vnew[:, s_, :] if si == 0 else vpg[si % NPG][s_][:]
            vb = vnew if si == 0 else vpg[si % NPG][s_]
            for h in range(8):
                c0 = s_ * 64 + h * 8
                op('pe', lambda e, vsrc=vsrc, h=h, c0=c0, s_=s_: e.matmul(po_s[0:64, c0:c0 + 8], lhsT=vsrc[:, h * 64:(h + 1) * 64], rhs=a_[:, c0:c0 + 8], start=(first and s_ == 0 and h == 0), stop=last, skip_group_check=True), [vb, a_], [po_s])
    orig_zmms = zmms_s
    prep_page(1); prep_page(2); prep_kT(1)
    def zmms_wrapped(si, pb, start, stop_unused):
        orig_zmms(si, pb, start, stop_unused)
    class _StepList(list):
        pass
    def avmm_wrapped(si, a_, first, last):
        avmm_s(si, a_, first, last)
        if si + 3 <= 128: prep_page(si + 3)
        if si + 2 <= 128: prep_kT(si + 2)
    attn_chain(steps, 256, zmms_wrapped, avmm_wrapped, None, None, wks, None)
    osf = d.sb("osf", [64, 256], F32)
    osq = [d.sb("osq", [64, 256], BF16)]
    ortmp = d.sb("ortmp", [64, 256], F32)
    gsb = d.sb("gsb", [64, 8], F32)
    osn = d.sb("osn", [64, 4, 8, 8], BF16)
    dma('sp', gsb[:], gsb_hd[:], [gsb_hd], [gsb], gsb)
    op('act', lambda e: e.copy(out=osf[:], in_=po_s[0:64, 0:256]), [po_s], [osf])
    colstat_rstd([(osf, osf[:, :])], 256, bones[0:64, 0:64], 64, P[0], ortmp, osq)
    op('dve', lambda e: e.tensor_tensor(out=osf[:], in0=osf[:], in1=ortmp[:], op=ALU.mult), [osf, ortmp], [osf])
    osf4 = osf[:, :].rearrange("p (s h t) -> p s h t", s=4, h=8)
    for h_ in range(8):
        op('dve', lambda e, h_=h_: e.tensor_scalar(out=osn[:, :, h_, :], in0=osf4[:, :, h_, :], scalar1=gsb[:, h_:h_ + 1], scalar2=None, op0=ALU.mult), [osf, gsb], [osn])
    for pc in range(4):
        pb = P[1 + pc % 2]
        for s_ in range(4):
            op('pe', lambda e, pc=pc, s_=s_, pb=pb: e.matmul(pb[:, s_ * 8:(s_ + 1) * 8], lhsT=ident_b[0:64, :], rhs=osn[:, s_, 2 * pc, :], start=True, stop=False), [ident_b, osn], [pb])
            op('pe', lambda e, pc=pc, s_=s_, pb=pb: e.matmul(pb[:, s_ * 8:(s_ + 1) * 8], lhsT=shiftI[:], rhs=osn[:, s_, 2 * pc + 1, :], start=False, stop=True), [shiftI, osn], [pb])
        op('dve', lambda e, pc=pc, pb=pb: e.tensor_copy(out=mixTs[:, pc, :], in_=pb[:, 0:32]), [pb], [mixTs])
    mixTs_keep = d.dram("mixTs_d", [128, 8, 32], BF16, OUT if os.environ.get("DBG") else "Internal")
    dma('sp', mixTs_keep[:], mixTs[:], [mixTs], [mixTs_keep], mixTs)
    d.pop()
    d.pop()

    if STOP == 'S2': return bail()
    d.push()
    gts_halo = d.sb("gts_halo", [128, NFC, 8], F32)
    d.push()
    stf = d.sb("stf", [8, DFF], F32)
    dma('sp', stf[:], st_ffn.t.rearrange("s t c -> (s t) c"), [st_ffn], [stf], stf)
    for fc in range(NFC):
        pb = P[fc % 2]
        op('pe', lambda e, fc=fc, pb=pb: e.transpose(out=pb[:, 0:8], in_=stf[:, fc * 128:(fc + 1) * 128], identity=ident_f[0:8, 0:8]), [stf, ident_f], [pb])
        op('dve', lambda e, fc=fc, pb=pb: e.tensor_copy(out=gts_halo[:, fc, :], in_=pb[:, 0:8]), [pb], [gts_halo])
    d.pop()
    woutb = d.sb("woutb", [128, 8, D], BF16)
    for k in range(8):
        dma('pool', woutb[:, k, :], w_out.t[k * 128:(k + 1) * 128, :], [w_out], [woutb], woutb)
    wdnb = d.sb("wdnb", [128, NFC, D], BF16)
    for fc in range(NFC):
        dma('pool', wdnb[:, fc, :], w_down.t[fc * 128:(fc + 1) * 128, :], [w_down], [wdnb], wdnb)
    lnt = [d.sb(f"lnt{i}", [128, D], F32) for i in range(4)]
    for i in range(4):
        dma('sp', lnt[i][:], lnp.t[i:i + 1, :].broadcast_to([128, D]), [lnp], [lnt[i]], lnt[i])
    xt4 = [d.sb(f"xq{i}", [128, D], F32) for i in range(1)]
    tmp4 = d.sb("tmp4", [128, D], F32)
    x1buf = d.sb("x1buf", [128, 4, D], F32)
    hb4 = d.sb("hb4", [128, D], BF16)
    h2T = d.sb("h2T", [128, KC, 512], BF16)
    mix_sb = d.sb("mix_sb", [128, 8, 512], BF16)
    wupg = [d.sb(f"wupg{i}", [128, KC, 128], BF16) for i in range(2)]
    wupv = [d.sb(f"wupv{i}", [128, KC, 128], BF16) for i in range(2)]
    gt_ext = [d.sb(f"gt_ext{i}", [128, 520], F32) for i in range(2)]
    gc = d.sb("gc", [128, 512], F32)
    ge = d.sb("ge", [128, 512], F32)
    fT = d.sb("fT", [128, NFC, 512], BF16)
    ybuf = [d.sb(f"ybuf{i}", [128, D], F32) for i in range(1)]
    stats = d.sb("stats", [128, 12], F32); mv = d.sb("mv", [128, 2], F32); rstd = d.sb("rstd", [128, 1], F32)
    w_up_v = w_up.t.rearrange("(k p) n -> p k n", p=128)
    A2s = d.sb("A2s", [32, D], F32); B2s = d.sb("B2s", [32, D], F32); G1s = d.sb("G1s", [32, D], F32); G2s = d.sb("G2s", [32, D], F32)
    for s_ in range(4):
        load_mod(G1s, 1 + s_, 2048, 8, 8 * s_); load_mod(B2s, 1 + s_, 3072, 8, 8 * s_)
        load_mod(A2s, 1 + s_, 4096, 8, 8 * s_); load_mod(G2s, 1 + s_, 5120, 8, 8 * s_)
    yi = 0

    def token_front(blocks, mixsrc, ncols_total, g1t, a2t, b2t):
        for bi, (ntok, x_ap, xdep, mc0, slot) in enumerate(blocks):
            xq = xt4[0]
            dma('sp', xq[0:ntok, :], x_ap, [xdep], [xq], xq)
            for hf in range(2):
                pb = P[hf]
                mm_acc(pb, pb[0:ntok, :], lambda k: mixsrc[:, k, mc0:mc0 + ntok], lambda k, hf=hf: woutb[:, k, hf * 512:(hf + 1) * 512], 8, [mixsrc, woutb])
                op('dve', lambda e, hf=hf, pb=pb: e.tensor_tensor(out=tmp4[0:ntok, hf * 512:(hf + 1) * 512], in0=pb[0:ntok, :], in1=g1t[0:ntok, hf * 512:(hf + 1) * 512], op=ALU.mult), [pb, g1t], [tmp4])
            op('dve', lambda e: e.scalar_tensor_tensor(out=tmp4[0:ntok, :], in0=xq[0:ntok, :], scalar=ALPHA, in1=tmp4[0:ntok, :], op0=ALU.mult, op1=ALU.add), [xq, tmp4], [tmp4])
            x1 = x1buf
            layernorm_rows_slot(tmp4, ntok, lnt[0], lnt[1], slot)
            modulate_transpose_slot(ntok, slot, a2t, b2t, bi * 128)

    def layernorm_rows_slot(src, ntok, gam, bet, slot):
        for hf in range(2):
            op('dve', lambda e, hf=hf: e.bn_stats(out=stats[0:ntok, hf * 6:(hf + 1) * 6], in_=src[0:ntok, hf * 512:(hf + 1) * 512]), [src], [stats])
        op('dve', lambda e: e.bn_aggr(out=mv[0:ntok, :], in_=stats[0:ntok, :]), [stats], [mv])
        op('dve', lambda e: e.tensor_scalar(out=rstd[0:ntok, :], in0=mv[0:ntok, 1:2], scalar1=EPS, scalar2=None, op0=ALU.add), [mv], [rstd])
        op('pool', lambda e: e.tensor_tensor(out=rstd[0:ntok, :], in0=rstd[0:ntok, :], in1=mhalf[0:ntok, 0:1], op=ALU.pow), [rstd, mhalf], [rstd])
        op('dve', lambda e: e.tensor_scalar(out=x1buf[0:ntok, slot, :], in0=src[0:ntok, :], scalar1=mv[0:ntok, 0:1], scalar2=rstd[0:ntok, 0:1], op0=ALU.subtract, op1=ALU.mult), [src, mv, rstd], [x1buf])
        op('dve', lambda e: e.tensor_tensor(out=x1buf[0:ntok, slot, :], in0=x1buf[0:ntok, slot, :], in1=gam[0:ntok, :], op=ALU.mult), [gam], [x1buf])
        op('dve', lambda e: e.tensor_tensor(out=x1buf[0:ntok, slot, :], in0=x1buf[0:ntok, slot, :], in1=bet[0:ntok, :], op=ALU.add), [bet], [x1buf])

    def modulate_transpose_slot(ntok, slot, a2t, b2t, c0):
        op('dve', lambda e: e.tensor_tensor(out=tmp4[0:ntok, :], in0=x1buf[0:ntok, slot, :], in1=a2t[0:ntok, :], op=ALU.mult), [x1buf, a2t], [tmp4])
        op('dve', lambda e: e.tensor_tensor(out=hb4[0:ntok, :], in0=tmp4[0:ntok, :], in1=b2t[0:ntok, :], op=ALU.add), [tmp4, b2t], [hb4])
        for k in range(KC):
            op('pe', lambda e, k=k: e.transpose(out=PT[:, k * 128:k * 128 + ntok], in_=hb4[0:ntok, k * 128:(k + 1) * 128], identity=ident_b[0:ntok, 0:ntok]), [hb4, ident_b], [PT])
        op('act', lambda e: e.copy(out=h2T[:, :, c0:c0 + ntok], in_=PT[:].rearrange("p (k t) -> p k t", k=KC)[:, :, 0:ntok]), [PT], [h2T])

    def ffn_mid(n, halo_src, nseq, do_val, cs=None):
        tl = n // nseq
        c_lo = 0 if cs is None else cs
        for fc in range(NFC):
            wg = wupg[fc % 2]; wv = wupv[fc % 2]
            dma('pool', wg[:], w_up_v[:, :, DFF + fc * 128:DFF + (fc + 1) * 128], [w_up], [wg], wg)
            if do_val:
                dma('pool', wv[:], w_up_v[:, :, fc * 128:(fc + 1) * 128], [w_up], [wv], wv)
            pg = P[2 + fc % 2]; pvv = P[4 + fc % 2]
            mm_acc(pg, pg[:, 0:n], lambda k: wg[:, k, :], lambda k: h2T[:, k, c_lo:c_lo + n], KC, [wg, h2T])
            if do_val:
                mm_acc(pvv, pvv[:, 0:n], lambda k: wv[:, k, :], lambda k: h2T[:, k, c_lo:c_lo + n], KC, [wv, h2T])
            gx = gt_ext[fc % 2]
            gxv = gx[:, 0:nseq * (tl + 2)].rearrange("p (s t) -> p s t", s=nseq)
            yield ('gate', fc, pg, gx, gxv, pvv)

    def conv3_gelu(fc, gxv, nseq, tl, n, pvv):
        gcv = gc[:, 0:n].rearrange("p (s t) -> p s t", s=nseq)
        op('dve', lambda e: e.tensor_scalar(out=gcv, in0=gxv[:, :, 0:tl], scalar1=col(C_WFDW + 0 * NFC + fc), scalar2=col(C_BFDW + fc), op0=ALU.mult, op1=ALU.add), [gt_ext[fc % 2], colp], [gc])
        for j in (1, 2):
            op('dve', lambda e, j=j: e.scalar_tensor_tensor(out=gcv, in0=gxv[:, :, j:j + tl], scalar=col(C_WFDW + j * NFC + fc), in1=gcv, op0=ALU.mult, op1=ALU.add), [gt_ext[fc % 2], colp, gc], [gc])
        op('act', lambda e: e.activation(out=ge[:, 0:n], in_=gc[:, 0:n], func=AF.Gelu), [gc], [ge])
        op('dve', lambda e: e.tensor_tensor(out=fT[:, fc, 0:n], in0=ge[:, 0:n], in1=pvv[:, 0:n], op=ALU.mult), [ge, pvv], [fT])

    def token_back(blocks, g2t, out_t):
        nonlocal yi
        for bi, (ntok, slot, out_ap) in enumerate(blocks):
            for hf in range(2):
                pb = P[hf]
                mm_acc(pb, pb[0:ntok, :], lambda k: fT[:, k, bi * 128:bi * 128 + ntok], lambda k, hf=hf: wdnb[:, k, hf * 512:(hf + 1) * 512], NFC, [fT, wdnb])
                op('dve', lambda e, hf=hf, pb=pb: e.tensor_tensor(out=tmp4[0:ntok, hf * 512:(hf + 1) * 512], in0=pb[0:ntok, :], in1=g2t[0:ntok, hf * 512:(hf + 1) * 512], op=ALU.mult), [pb, g2t], [tmp4])
            op('dve', lambda e: e.scalar_tensor_tensor(out=tmp4[0:ntok, :], in0=x1buf[0:ntok, slot, :], scalar=ALPHA, in1=tmp4[0:ntok, :], op0=ALU.mult, op1=ALU.add), [x1buf, tmp4], [tmp4])
            yb = ybuf[0]; yi += 1
            layernorm_rows(tmp4, ntok, lnt[2], lnt[3], yb, stats, mv, rstd)
            dma('sp', out_ap, yb[0:ntok, :], [yb], [out_t], yb, is_output=True)

    dma('sp', mix_sb[:, :, 0:128], mixT_d.t[:, :, 0:128].rearrange("c p t -> p c t"), [mixT_d], [mix_sb], mix_sb)
    token_front([(128, xs.t[47 * 128:48 * 128, :], xs, 0, 0)], mix_sb, 128, G1, A2, B2)
    for (kind, fc, pg, gx, gxv, pvv) in ffn_mid(2, None, 1, False, cs=126):
        op('dve', lambda e, fc=fc, pg=pg: e.tensor_scalar(out=gthalo[:, fc, :], in0=pg[:, 0:2], scalar1=hfl[:, 0:1], scalar2=None, op0=ALU.mult), [pg, hfl], [gthalo])
    for sbo in range(4):
        c0 = 128 + sbo * 512
        dma('sp', mix_sb[:], mixT_d.t[:, :, c0:c0 + 512].rearrange("c p t -> p c t"), [mixT_d], [mix_sb], mix_sb)
        blocks = [(128, xs.t[(48 + sbo * 4 + bl) * 128:(49 + sbo * 4 + bl) * 128, :], xs, bl * 128, bl) for bl in range(4)]
        token_front(blocks, mix_sb, 512, G1, A2, B2)
        for (kind, fc, pg, gx, gxv, pvv) in ffn_mid(512, None, 1, True):
            op('pool', lambda e, fc=fc, gx=gx: e.tensor_copy(out=gx[:, 0:2], in_=gthalo[:, fc, :]), [gthalo], [gx])
            op('act', lambda e, pg=pg, gx=gx: e.copy(out=gx[:, 2:514], in_=pg[:, 0:512]), [pg], [gx])
            op('pool', lambda e, fc=fc, gx=gx: e.tensor_copy(out=gthalo[:, fc, :], in_=gx[:, 512:514]), [gx], [gthalo])
            conv3_gelu(fc, gxv, 1, 512, 512, pvv)
        token_back([(128, bl, y_p.t[(sbo * 4 + bl) * 128:(sbo * 4 + bl + 1) * 128, :]) for bl in range(4)], G2, y_p)
    for fc in range(NFC):
        dma('sp', ffn_p.t[:, fc * 128:(fc + 1) * 128].rearrange("t p -> p t"), gthalo[:, fc, :], [gthalo], [ffn_p], gthalo, is_output=True, allow_slow_non_contiguous=True)
    mixTs2 = d.sb("mixTs2", [128, 8, 32], BF16)
    dma('sp', mixTs2[:], mixTs_keep[:], [mixTs_keep], [mixTs2], mixTs2)
    token_front([(32, x_s[:], x_s, 0, 0)], mixTs2, 32, G1s, A2s, B2s)
    gsn = d.sb("gsn", [128, NFC, 8], F32)
    for (kind, fc, pg, gx, gxv, pvv) in ffn_mid(32, None, 4, True):
        op('pool', lambda e, fc=fc, gxv=gxv: e.tensor_copy(out=gxv[:, :, 0:2], in_=gts_halo[:, fc, :].rearrange("p (s t) -> p s t", s=4)), [gts_halo], [gx])
        op('act', lambda e, pg=pg, gxv=gxv: e.copy(out=gxv[:, :, 2:10], in_=pg[:, 0:32].rearrange("p (s t) -> p s t", s=4)), [pg], [gx])
        op('pool', lambda e, fc=fc, gxv=gxv: e.tensor_copy(out=gsn[:, fc, :].rearrange("p (s t) -> p s t", s=4), in_=gxv[:, :, 8:10]), [gx], [gsn])
        conv3_gelu(fc, gxv, 4, 8, 32, pvv)
    token_back([(32, 0, y_s[:])], G2s, y_s)
    for fc in range(NFC):
        for s_ in range(4):
            dma('sp', ffn_s.t[s_, :, fc * 128:(fc + 1) * 128].rearrange("t p -> p t"), gsn[:, fc, 2 * s_:2 * s_ + 2], [gsn], [ffn_s], gsn, is_output=True, allow_slow_non_contiguous=True)
    d.finish()
    d.stacks.pop().close()
    d.stacks.pop().close()
    return nc


_NC = None


def make_in_maps(inp):
    f32 = np.float32
    x_prompt = np.asarray(inp['x_prompt'], f32); x_sample = np.asarray(inp['x_sample'], f32)
    ck = np.ascontiguousarray(np.asarray(inp['cache_k'], f32)[0].reshape(5120 * 128, 512))
    cv = np.ascontiguousarray(np.asarray(inp['cache_v'], f32)[0].reshape(5120 * 128, 512))
    page_table = np.asarray(inp['page_table'], np.int32)
    g = lambda k: np.asarray(inp[k], f32)[0]
    w_dw = g('w_dw'); w_fdw = g('w_fdw')
    def cols(v, n):
        return np.ascontiguousarray(v.reshape(n, 128).T)
    colp = np.concatenate([
        cols(g('gn_sb'), 4), cols(g('gn_conv'), 4), cols(g('cln_g'), 4), cols(g('cln_b'), 4), cols(g('b_dw'), 4),
        np.ascontiguousarray(w_dw.reshape(31, 4, 128).transpose(2, 0, 1).reshape(128, 124)),
        cols(g('b_fdw'), NFC),
        np.ascontiguousarray(w_fdw.reshape(3, NFC, 128).transpose(2, 0, 1).reshape(128, 66)),
    ], axis=1).astype(f32)
    assert colp.shape == (128, NCOLP)
    sbb = g('sb_bias').reshape(1, 8)
    brow = np.ascontiguousarray(np.broadcast_to(sbb.reshape(1, 1, 8, 1), (1, 4, 8, 8)).reshape(1, 256))
    gsb_hd = np.ascontiguousarray(g('gn_sb').reshape(8, 64).T)
    lnp = np.stack([g('ln1_g'), g('ln1_b'), g('ln2_g'), g('ln2_b')]).astype(f32)
    shared = dict(cache_k=ck, cache_v=cv, arange=np.arange(128, dtype=f32).reshape(128, 1),
                  w_ada=g('w_ada'), b_ada=g('b_ada').reshape(1, -1), w_in=g('w_in'), sbbias=sbb, brow=brow,
                  colp=colp, gsb_hd=gsb_hd, w_out=g('w_out'), lnp=lnp, w_up=g('w_up'), w_down=g('w_down'))
    in_maps = []
    for c in range(8):
        b, j = c // 4, c % 4
        nreal = OWN * (j + 1)
        xs = np.zeros((NSLOT, D), f32)
        xs[NSLOT - nreal:] = x_prompt[b, :nreal]
        valid = np.zeros(NSLOT, f32); valid[NSLOT - nreal:] = 1.0
        vmask = np.ascontiguousarray(valid.reshape(NBLK, 128).T)
        hflag = np.full((128, 1), 1.0 if j > 0 else 0.0, f32)
        c5 = np.concatenate([np.asarray(inp['c_prompt'], f32)[b:b + 1], np.asarray(inp['c_sample'], f32)[4 * c:4 * c + 4]], 0)
        c5p = np.zeros((32, D), f32); c5p[0:5] = c5
        c5T = np.ascontiguousarray(c5p.reshape(32, KC, 128).transpose(2, 1, 0))
        m = dict(shared)
        m.update(xs=xs, vmask=vmask, hflag=hflag, c5T=c5T,
                 x_s=np.ascontiguousarray(x_sample[4 * c:4 * c + 4].reshape(32, D)),
                 ptab=np.ascontiguousarray(page_table[4 * c:4 * c + 4].reshape(-1)),
                 st_conv=np.ascontiguousarray(np.asarray(inp['state_conv'], f32)[0, 4 * c:4 * c + 4]),
                 st_ffn=np.ascontiguousarray(np.asarray(inp['state_ffn'], f32)[0, 4 * c:4 * c + 4]))
        in_maps.append(m)
    return in_maps


def kernel(**inp):
    global _NC
    f32 = np.float32
    in_maps = make_in_maps(inp)
    if _NC is None:
        _NC = build_nc()
    res = run_bass_kernel_spmd(_NC, in_maps, core_ids=list(range(8)))
    R = res.results
    y_prompt = np.zeros((2, 8192, D), f32); k_prompt = np.zeros((1, 2, 8192, 8, 64), f32); v_prompt = np.zeros_like(k_prompt)
    conv_prompt = np.zeros((1, 2, 30, 512), f32); ffn_prompt = np.zeros((1, 2, 2, DFF), f32)
    y_sample = np.zeros((32, 8, D), f32); k_sample = np.zeros((1, 32, 8, 8, 64), f32); v_sample = np.zeros_like(k_sample)
    conv_sample = np.zeros((1, 32, 30, 512), f32); ffn_sample = np.zeros((1, 32, 2, DFF), f32)
    for c in range(8):
        b, j = c // 4, c % 4
        r = R[c]
        y_prompt[b, OWN * j:OWN * (j + 1)] = r['y_p']
        k_prompt[0, b, OWN * j:OWN * (j + 1)] = r['k_p'].reshape(OWN, 8, 64)
        v_prompt[0, b, OWN * j:OWN * (j + 1)] = r['v_p'].reshape(OWN, 8, 64)
        if j == 3:
            conv_prompt[0, b] = r['conv_p']; ffn_prompt[0, b] = r['ffn_p']
        y_sample[4 * c:4 * c + 4] = r['y_s'].reshape(4, 8, D)
        k_sample[0, 4 * c:4 * c + 4] = r['k_s'].reshape(4, 8, 8, 64)
        v_sample[0, 4 * c:4 * c + 4] = r['v_s'].reshape(4, 8, 8, 64)
        conv_sample[0, 4 * c:4 * c + 4] = r['conv_s']; ffn_sample[0, 4 * c:4 * c + 4] = r['ffn_s']
    return (y_prompt, y_sample, k_prompt, v_prompt, conv_prompt, ffn_prompt, k_sample, v_sample, conv_sample, ffn_sample)
```

```python
import numpy as np
import os
from contextlib import ExitStack
import concourse.bass as bass
import concourse.mybir as mybir
from concourse.bass_utils import run_bass_kernel_spmd

F32 = mybir.dt.float32; BF16 = mybir.dt.bfloat16; I32 = mybir.dt.int32
ALU = mybir.AluOpType; AF = mybir.ActivationFunctionType

D = 1024; KC = 8; NSLOT = 8192; NBLK = 64; OWN = 2048; DFF = 2816; NFC = 22
ALPHA = 2.0 ** 0.25; EPS = 1e-5
NQ = 128 + OWN
UOFF = 32
NCOLP = 4 * 5 + 124 + 22 + 66


class Buf:
    def __init__(self, t, name):
        self.t = t; self.name = name
        self.w = None; self.r = []
        self.dsem = None; self.dcount = 0
    def __getitem__(self, idx):
        return self.t[idx]


class Dep:
    def __init__(self, nc):
        self.nc = nc
        self.stacks = [ExitStack()]
        self.eng = {'pe': nc.tensor, 'act': nc.scalar, 'dve': nc.vector, 'pool': nc.gpsimd, 'sp': nc.sync}
        self.sem = {k: nc.alloc_semaphore(name=f"sem_{k}") for k in self.eng}
        self.cnt = {k: 0 for k in self.eng}
        self.seen = {k: {} for k in self.eng}
        self.out_tokens = []
        self.dsems = []
        self.dsem_pool = []
        self.scope_bufs = [[]]
        self.retired = {}

    def push(self):
        self.stacks.append(ExitStack())
        self.scope_bufs.append([])
    def pop(self):
        self.barrier()
        for b in self.scope_bufs.pop():
            if b.dsem is not None:
                self.dsem_pool.append((b.dsem, b.dcount))
                self.dsems = [o for o in self.dsems if o is not b]
                self.retired[id(b.dsem)] = (b.dsem, b.dcount)
                b.dsem = None
        self.stacks.pop().close()
    def sb(self, name, shape, dt):
        b = Buf(self.stacks[-1].enter_context(self.nc.sbuf_tensor(name, shape, dt)), name)
        self.scope_bufs[-1].append(b)
        return b
    def ps(self, name, shape, dt):
        b = Buf(self.stacks[-1].enter_context(self.nc.psum_tensor(name, shape, dt)), name)
        b.psum = True
        return b
    def dram(self, name, shape, dt, kind="Internal"):
        return Buf(self.nc.dram_tensor(name, shape, dt, kind=kind).ap(), name)

    def _wait(self, e, s, v):
        k = id(s)
        if self.seen[e].get(k, 0) >= v: return
        self.eng[e].wait_ge(s, v)
        self.seen[e][k] = v

    def _waits(self, e, reads, writes):
        need = {}
        def add(tok):
            if tok is None: return
            s, v = tok
            k = id(s)
            if k not in need or need[k][1] < v: need[k] = (s, v)
        for b in reads: add(b.w)
        for b in writes:
            add(b.w)
            for t in b.r: add(t)
        for k, (s, v) in need.items():
            self._wait(e, s, v)

    def _record(self, tok, reads, writes):
        for b in reads:
            b.r.append(tok)
            if len(b.r) > 24:
                m = {}
                for s, v in b.r:
                    if id(s) not in m or m[id(s)][1] < v: m[id(s)] = (s, v)
                b.r = list(m.values())
        for b in writes:
            b.w = tok; b.r = []

    def op(self, e, fn, reads=(), writes=()):
        pr = [b for b in reads if getattr(b, 'psum', False)]
        if pr:
            writes = list(writes) + [b for b in pr if b not in writes]
            reads = [b for b in reads if not getattr(b, 'psum', False)]
        self._waits(e, reads, writes)
        inst = fn(self.eng[e])
        self.cnt[e] += 1
        inst.then_inc(self.sem[e], 1)
        self._record((self.sem[e], self.cnt[e]), reads, writes)
        return inst

    def dma(self, q, out, in_, reads, writes, owner, is_output=False, indirect=None, **kw):
        self._waits(q, reads, writes)
        eng = self.eng[q]
        if indirect is not None:
            inst = eng.indirect_dma_start(out=out, out_offset=None, in_=in_, in_offset=indirect, **kw)
        else:
            inst = eng.dma_start(out=out, in_=in_, **kw)
        if owner.dsem is None:
            if self.dsem_pool:
                owner.dsem, owner.dcount = self.dsem_pool.pop()
                self.retired.pop(id(owner.dsem), None)
            else:
                owner.dsem = self.nc.alloc_semaphore(name=f"dsem_{owner.name}")
            self.dsems.append(owner)
        owner.dcount += 16
        inst.then_inc(owner.dsem, 16)
        tok = (owner.dsem, owner.dcount)
        self._record(tok, reads, writes)
        if is_output: self.out_tokens.append(tok)
        return inst

    def barrier(self):
        for e in self.eng:
            for e2 in self.eng:
                if self.cnt[e2] > 0: self._wait(e, self.sem[e2], self.cnt[e2])
            for o in self.dsems:
                self._wait(e, o.dsem, o.dcount)
            for (sm, v) in self.retired.values():
                self._wait(e, sm, v)

    def finish(self):
        self.barrier()


def build_nc(NPOOL=5120, STOP=None):
    nc = bass.Bass("TRN2", target_bir_lowering=False)
    d = Dep(nc)
    op = d.op; dma = d.dma
    def bail():
        d.finish()
        while d.stacks: d.stacks.pop().close()
        return nc
    IN = "ExternalInput"; OUT = "ExternalOutput"
    xs = d.dram("xs", [NSLOT, D], F32, IN)
    vmask = d.dram("vmask", [128, NBLK], F32, IN)
    hflag = d.dram("hflag", [128, 1], F32, IN)
    c5T = d.dram("c5T", [128, KC, 32], F32, IN)
    x_s = d.dram("x_s", [32, D], F32, IN)
    cache_k = d.dram("cache_k", [NPOOL * 128, 512], F32, IN)
    cache_v = d.dram("cache_v", [NPOOL * 128, 512], F32, IN)
    ptab = d.dram("ptab", [4 * 128], I32, IN)
    arange = d.dram("arange", [128, 1], F32, IN)
    st_conv = d.dram("st_conv", [4, 30, 512], F32, IN)
    st_ffn = d.dram("st_ffn", [4, 2, DFF], F32, IN)
    w_ada = d.dram("w_ada", [D, 6 * D], F32, IN)
    b_ada = d.dram("b_ada", [1, 6 * D], F32, IN)
    w_in = d.dram("w_in", [D, 2560], F32, IN)
    sbbias = d.dram("sbbias", [1, 8], F32, IN)
    brow = d.dram("brow", [1, 256], F32, IN)
    colp_d = d.dram("colp", [128, NCOLP], F32, IN)
    gsb_hd = d.dram("gsb_hd", [64, 8], F32, IN)
    w_out = d.dram("w_out", [D, D], F32, IN)
    lnp = d.dram("lnp", [4, D], F32, IN)
    w_up = d.dram("w_up", [D, 2 * DFF], F32, IN)
    w_down = d.dram("w_down", [DFF, D], F32, IN)

    y_p = d.dram("y_p", [OWN, D], F32, OUT)
    k_p = d.dram("k_p", [OWN, 512], F32, OUT)
    v_p = d.dram("v_p", [OWN, 512], F32, OUT)
    conv_p = d.dram("conv_p", [30, 512], F32, OUT)
    ffn_p = d.dram("ffn_p", [2, DFF], F32, OUT)
    y_s = d.dram("y_s", [32, D], F32, OUT)
    k_s = d.dram("k_s", [32, 512], F32, OUT)
    v_s = d.dram("v_s", [32, 512], F32, OUT)
    conv_s = d.dram("conv_s", [4, 30, 512], F32, OUT)
    ffn_s = d.dram("ffn_s", [4, 2, DFF], F32, OUT)

    mod_d = d.dram("mod_d", [5, 6 * D], F32)
    knT_d = d.dram("knT_d", [4, 128, NSLOT], BF16)
    v_d = d.dram("v_d", [NSLOT, 512], BF16)
    qT_d = d.dram("qT_d", [4, 128, NQ], BF16)
    mixT_d = d.dram("mixT_d", [8, 128, NQ], BF16)

    P = [d.ps(f"pb{i}", [128, 512], F32) for i in range(7)]
    PT = d.ps("ptr", [128, 1024], BF16)

    ones_f = d.sb("ones_f", [128, 128], F32)
    ident_f = d.sb("ident_f", [128, 128], F32)
    ident_b = d.sb("ident_b", [128, 128], BF16)
    tri_b = d.sb("tri_b", [128, 128], BF16)
    ones_b = d.sb("ones_b", [128, 128], BF16)
    m512 = d.sb("m512", [128, 128], F32)
    bones = d.sb("bones", [128, 128], BF16)
    shiftI = d.sb("shiftI", [64, 128], BF16)
    Mi = [d.sb(f"Mi{i}", [128, 512], BF16) for i in range(4)]
    Ms = d.sb("Ms", [128, 256], BF16)
    ones512 = d.sb("ones512", [128, 512], BF16)
    mhalf = d.sb("mhalf", [128, 512], F32)
    colp = d.sb("colp_sb", [128, NCOLP], F32)
    sbb = d.sb("sbb", [128, 8], F32)
    vm = d.sb("vm", [128, NBLK], F32)
    hfl = d.sb("hfl", [128, 1], F32)
    G1 = d.sb("G1", [128, D], F32); A2 = d.sb("A2", [128, D], F32)
    B2 = d.sb("B2", [128, D], F32); G2 = d.sb("G2", [128, D], F32)
    gthalo = d.sb("gthalo", [128, NFC, 2], F32)

    op('pool', lambda e: e.memset(ones_f[:], 1.0), [], [ones_f])
    op('pool', lambda e: e.memset(ones_b[:], 1.0), [], [ones_b])
    op('pool', lambda e: e.memset(ones512[:], 1.0), [], [ones512])
    op('pool', lambda e: e.memset(m512[:], 1.0 / 512.0), [], [m512])
    op('pool', lambda e: e.memset(mhalf[:], -0.5), [], [mhalf])
    op('pool', lambda e: e.memset(bones[:], 0.0), [], [bones])
    op('pool', lambda e: e.memset(bones[0:64, 0:64], 1.0 / 64.0), [], [bones])
    op('pool', lambda e: e.memset(bones[64:128, 64:128], 1.0 / 64.0), [], [bones])
    op('pool', lambda e: e.affine_select(out=ident_f[:], in_=ones_f[:], pattern=[[-1, 128]], compare_op=ALU.is_equal, fill=0.0, base=0, channel_multiplier=1), [ones_f], [ident_f])
    op('pool', lambda e: e.tensor_copy(out=ident_b[:], in_=ident_f[:]), [ident_f], [ident_b])
    op('pool', lambda e: e.affine_select(out=tri_b[:], in_=ones_b[:], pattern=[[-1, 128]], compare_op=ALU.is_ge, fill=0.0, base=0, channel_multiplier=1), [ones_b], [tri_b])
    op('pool', lambda e: e.affine_select(out=shiftI[:], in_=ones_b[0:64, :], pattern=[[-1, 128]], compare_op=ALU.is_equal, fill=0.0, base=64, channel_multiplier=1), [ones_b], [shiftI])
    for i in range(4):
        op('pool', lambda e, i=i: e.affine_select(out=Mi[i][:], in_=ones512[:], pattern=[[1, 512]], compare_op=ALU.is_gt, fill=0.0, base=-128 * i, channel_multiplier=-1), [ones512], [Mi[i]])
    op('pool', lambda e: e.affine_select(out=Ms[:], in_=ones512[:, 0:256], pattern=[[0, 32], [1, 8]], compare_op=ALU.is_gt, fill=0.0, base=0, channel_multiplier=-1), [ones512], [Ms])
    dma('sp', colp[:], colp_d[:], [colp_d], [colp], colp)
    dma('sp', sbb[:], sbbias.t.broadcast_to([128, 8]), [sbbias], [sbb], sbb)
    dma('sp', vm[:], vmask[:], [vmask], [vm], vm)
    dma('sp', hfl[:], hflag[:], [hflag], [hfl], hfl)
    C_GNSB, C_GNCV, C_CLNG, C_CLNB, C_BDW, C_WDW, C_BFDW, C_WFDW = 0, 4, 8, 12, 16, 20, 144, 166
    def col(c): return colp[:, c:c + 1]

    if STOP == 'C': return bail()
    d.push()
    cT = d.sb("cT", [128, KC * 32], F32)
    sT = d.sb("sT", [128, KC * 32], F32)
    e5 = d.sb("e5", [128, KC * 32], F32)
    modsb = d.sb("modsb", [5, 6 * D], F32)
    bada = d.sb("bada", [5, 6 * D], F32)
    wa = [d.sb(f"wa{i}", [128, KC, 512], F32) for i in range(2)]
    dma('sp', cT[:], c5T.t.rearrange("p k r -> p (k r)"), [c5T], [cT], cT)
    dma('sp', bada[:], b_ada.t.broadcast_to([5, 6 * D]), [b_ada], [bada], bada)
    op('act', lambda e: e.activation(out=e5[:], in_=cT[:], func=AF.Exp, scale=-1.0), [cT], [e5])
    op('dve', lambda e: e.tensor_scalar(out=e5[:], in0=e5[:], scalar1=1.0, scalar2=None, op0=ALU.add), [e5], [e5])
    op('dve', lambda e: e.reciprocal(out=e5[:], in_=e5[:]), [e5], [e5])
    op('dve', lambda e: e.tensor_tensor(out=sT[:], in0=cT[:], in1=e5[:], op=ALU.mult), [cT, e5], [sT])
    w_ada_v = w_ada.t.rearrange("(k p) n -> p k n", p=128)
    for n in range(12):
        w = wa[n % 2]
        dma('sp', w[:], w_ada_v[:, :, n * 512:(n + 1) * 512], [w_ada], [w], w)
        pb = P[n % 2]
        for k in range(KC):
            op('pe', lambda e, k=k, w=w, pb=pb: e.matmul(pb[0:32, :], lhsT=sT[:, k * 32:(k + 1) * 32], rhs=w[:, k, :], start=(k == 0), stop=(k == KC - 1)), [sT, w], [pb])
        op('dve', lambda e, n=n, pb=pb: e.tensor_tensor(out=modsb[:, n * 512:(n + 1) * 512], in0=pb[0:5, :], in1=bada[:, n * 512:(n + 1) * 512], op=ALU.add), [pb, bada], [modsb])
    for a, b_ in ((1024, 3072), (4096, 6144)):
        op('dve', lambda e, a=a, b_=b_: e.tensor_scalar(out=modsb[:, a:b_], in0=modsb[:, a:b_], scalar1=1.0, scalar2=None, op0=ALU.add), [modsb], [modsb])
    dma('sp', mod_d[:], modsb[:], [modsb], [mod_d], modsb)
    d.pop()

    if STOP == 'A': return bail()
    def load_mod(tile, row, lo, npart=128, p0=0):
        dma('sp', tile[p0:p0 + npart, :], mod_d.t[row:row + 1, lo:lo + D].broadcast_to([npart, D]), [mod_d], [tile], tile)
    load_mod(G1, 0, 2048); load_mod(B2, 0, 3072); load_mod(A2, 0, 4096); load_mod(G2, 0, 5120)

    def modulate_transpose(xt, ntok, A, B, vcol, hb, t1, hT, c0):
        op('dve', lambda e: e.tensor_tensor(out=t1[0:ntok, :], in0=xt[0:ntok, :], in1=A[0:ntok, :], op=ALU.mult), [xt, A], [t1])
        if vcol is not None:
            op('dve', lambda e: e.scalar_tensor_tensor(out=hb[0:ntok, :], in0=B[0:ntok, :], scalar=vcol, in1=t1[0:ntok, :], op0=ALU.mult, op1=ALU.add), [B, t1, vm], [hb])
        else:
            op('dve', lambda e: e.tensor_tensor(out=hb[0:ntok, :], in0=t1[0:ntok, :], in1=B[0:ntok, :], op=ALU.add), [B, t1], [hb])
        for k in range(KC):
            op('pe', lambda e, k=k: e.transpose(out=PT[:, k * 128:k * 128 + ntok], in_=hb[0:ntok, k * 128:(k + 1) * 128], identity=ident_b[0:ntok, 0:ntok]), [hb, ident_b], [PT])
        op('act', lambda e: e.copy(out=hT[:, :, c0:c0 + ntok], in_=PT[:].rearrange("p (k t) -> p k t", k=KC)[:, :, 0:ntok]), [PT], [hT])

    def mm_acc(pb, pslice, lhs_fn, rhs_fn, nk, reads):
        for k in range(nk):
            op('pe', lambda e, k=k: e.matmul(pslice, lhsT=lhs_fn(k), rhs=rhs_fn(k), start=(k == 0), stop=(k == nk - 1)), reads, [pb])

    def layernorm_rows(src, ntok, gam, bet, dst, stats, mv, rstd):
        for hf in range(2):
            op('dve', lambda e, hf=hf: e.bn_stats(out=stats[0:ntok, hf * 6:(hf + 1) * 6], in_=src[0:ntok, hf * 512:(hf + 1) * 512]), [src], [stats])
        op('dve', lambda e: e.bn_aggr(out=mv[0:ntok, :], in_=stats[0:ntok, :]), [stats], [mv])
        op('dve', lambda e: e.tensor_scalar(out=rstd[0:ntok, :], in0=mv[0:ntok, 1:2], scalar1=EPS, scalar2=None, op0=ALU.add), [mv], [rstd])
        op('pool', lambda e: e.tensor_tensor(out=rstd[0:ntok, :], in0=rstd[0:ntok, :], in1=mhalf[0:ntok, 0:1], op=ALU.pow), [rstd, mhalf], [rstd])
        op('dve', lambda e: e.tensor_scalar(out=dst[0:ntok, :], in0=src[0:ntok, :], scalar1=mv[0:ntok, 0:1], scalar2=rstd[0:ntok, 0:1], op0=ALU.subtract, op1=ALU.mult), [src, mv, rstd], [dst])
        op('dve', lambda e: e.tensor_tensor(out=dst[0:ntok, :], in0=dst[0:ntok, :], in1=gam[0:ntok, :], op=ALU.mult), [dst, gam], [dst])
        op('dve', lambda e: e.tensor_tensor(out=dst[0:ntok, :], in0=dst[0:ntok, :], in1=bet[0:ntok, :], op=ALU.add), [dst, bet], [dst])

    def colstat_rstd(srcs, n, lhs, npart, pb, tmp, wk):
        for i, (b, ap) in enumerate(srcs):
            op('dve', lambda e, ap=ap, i=i: e.tensor_tensor(out=wk[i][0:npart, 0:n], in0=ap, in1=ap, op=ALU.mult), [b], [wk[i]])
        for i in range(len(srcs)):
            op('pe', lambda e, i=i: e.matmul(pb[0:npart, 0:n], lhsT=lhs, rhs=wk[i][0:npart, 0:n], start=(i == 0), stop=(i == len(srcs) - 1)), [wk[i], m512, bones], [pb])
        op('dve', lambda e: e.tensor_scalar(out=tmp[0:npart, 0:n], in0=pb[0:npart, 0:n], scalar1=EPS, scalar2=None, op0=ALU.add), [pb], [tmp])
        op('pool', lambda e: e.tensor_tensor(out=tmp[0:npart, 0:n], in0=tmp[0:npart, 0:n], in1=mhalf[0:npart, 0:n], op=ALU.pow), [tmp, mhalf], [tmp])

    d.push()
    dwd = d.sb("dwd", [128, 124, 128], BF16)
    for i in range(124):
        op('dve' if i % 2 else 'pool', lambda e, i=i: e.tensor_scalar(out=dwd[:, i, :], in0=ident_f[:], scalar1=col(C_WDW + i), scalar2=None, op0=ALU.mult), [ident_f, colp], [dwd])
    wkf = dict(sq=[d.sb(f"csq{i}", [128, 512], F32) for i in range(4)], tmp=d.sb("ctmp", [128, 512], F32), sg=d.sb("csg", [128, 512], F32))
    d.push()
    uT_b = d.sb("uT_b", [128, 4, UOFF + NQ], BF16)
    op('pool', lambda e: e.memset(uT_b[:], 0.0), [], [uT_b])
    d.push()
    A1 = d.sb("A1", [128, D], F32); B1 = d.sb("B1", [128, D], F32)
    load_mod(B1, 0, 0); load_mod(A1, 0, 1024)
    winb = d.sb("winb", [128, KC, 2560], BF16)
    for k in range(KC):
        dma('pool', winb[:, k, :], w_in.t[k * 128:(k + 1) * 128, :], [w_in], [winb], winb)
    xt = [d.sb(f"xt{i}", [128, D], F32) for i in range(3)]
    t1 = d.sb("t1", [128, D], F32)
    hb = [d.sb(f"hb{i}", [128, D], BF16) for i in range(2)]
    hT = [d.sb(f"hT{i}", [128, KC, 512], BF16) for i in range(2)]
    kn_sb = [d.sb(f"kn_sb{i}", [128, 512], BF16) for i in range(2)]
    q_sb = [d.sb(f"q_sb{i}", [128, 512], BF16) for i in range(2)]
    v_sb = [d.sb(f"v_sb{i}", [128, 512], BF16) for i in range(2)]
    vf_sb = [d.sb(f"vf_sb{i}", [128, 512], F32) for i in range(2)]
    kf_sb = [d.sb(f"kf_sb{i}", [128, 512], F32) for i in range(2)]
    sig = d.sb("sig", [128, 512], F32)
    uf = [d.sb(f"uf{i}", [128, 512], F32) for i in range(4)]

    def proj_group(hTg, ncol, sbi, is_own, is_halo, c_lo):
        cs = slice(c_lo, c_lo + ncol)
        for pc in range(4):
            pb = P[pc % 2]
            mm_acc(pb, pb[:, 0:ncol], lambda k, pc=pc: winb[:, k, 512 + pc * 128:512 + (pc + 1) * 128], lambda k: hTg[:, k, cs], KC, [winb, hTg])
            ks = kn_sb[pc % 2]
            op('act', lambda e, pb=pb, ks=ks: e.activation(out=ks[:, 0:ncol], in_=pb[:, 0:ncol], func=AF.Identity, scale=-0.125), [pb], [ks])
            yield ('kn', pc, ks)
        if is_own or is_halo:
            for pc in range(0 if os.environ.get('SKIP_Q') else 4):
                pb = P[2 + pc % 2]
                mm_acc(pb, pb[:, 0:ncol], lambda k, pc=pc: winb[:, k, pc * 128:(pc + 1) * 128], lambda k: hTg[:, k, cs], KC, [winb, hTg])
                qs = q_sb[pc % 2]
                op('act' if os.environ.get('E1') else 'dve', lambda e, pb=pb, qs=qs: (e.copy if os.environ.get('E1') else e.tensor_copy)(out=qs[:, 0:ncol], in_=pb[:, 0:ncol]), [pb], [qs])
                yield ('q', pc, qs)
            for cc in range(0 if os.environ.get('SKIP_U') else 4):
                pa, pg = P[4], P[5]
                mm_acc(pa, pa[:, 0:ncol], lambda k, cc=cc: winb[:, k, 1536 + cc * 128:1536 + (cc + 1) * 128], lambda k: hTg[:, k, cs], KC, [winb, hTg])
                mm_acc(pg, pg[:, 0:ncol], lambda k, cc=cc: winb[:, k, 2048 + cc * 128:2048 + (cc + 1) * 128], lambda k: hTg[:, k, cs], KC, [winb, hTg])
                op('act', lambda e: e.activation(out=sig[:, 0:ncol], in_=pg[:, 0:ncol], func=AF.Sigmoid), [pg], [sig])
                op('dve', lambda e, cc=cc: e.tensor_tensor(out=uf[cc][:, 0:ncol], in0=pa[:, 0:ncol], in1=sig[:, 0:ncol], op=ALU.mult), [pa, sig], [uf[cc]])
                yield ('u', cc, uf[cc])

    xi = 0
    for sbi in range(16):
        if STOP == 'P1c' and sbi >= 1: break
        if STOP == 'P1d' and sbi not in (0, 11): continue
        if STOP == 'P1e' and sbi not in (0, 12): continue
        if STOP == 'P1f' and sbi not in (0, 15): continue
        is_own = sbi >= 12
        is_halo_sb = sbi == 11
        hTg = hT[sbi % 2]
        for bl in range(4):
            blk = sbi * 4 + bl
            x_t = xt[xi % 3]; xi += 1
            dma('sp', x_t[:], xs.t[blk * 128:(blk + 1) * 128, :], [xs], [x_t], x_t)
            modulate_transpose(x_t, 128, A1, B1, vm[:, blk:blk + 1], hb[blk % 2], t1, hTg, bl * 128)
        if STOP == 'P1a': break
        for bl in range(4):
            blk = sbi * 4 + bl
            pb = P[6]
            mm_acc(pb, pb[:, :], lambda k, bl=bl: hTg[:, k, bl * 128:(bl + 1) * 128], lambda k: winb[:, k, 1024:1536], KC, [winb, hTg])
            vs_ = v_sb[blk % 2]
            op('act', lambda e, vs_=vs_: e.copy(out=vs_[:], in_=pb[:]), [pb], [vs_])
            dma('sp', v_d.t[blk * 128:(blk + 1) * 128, :], vs_[:], [vs_], [v_d], vs_)
            if is_own and not os.environ.get('SKIP_OWNOUT'):
                vf = vf_sb[blk % 2]
                op('dve', lambda e, vf=vf: e.tensor_copy(out=vf[:], in_=pb[:]), [pb], [vf])
                r0 = (blk - 48) * 128
                dma('sp', v_p.t[r0:r0 + 128, :], vf[:], [vf], [v_p], vf, is_output=True)
                pk = P[3]
                mm_acc(pk, pk[:, :], lambda k, bl=bl: hTg[:, k, bl * 128:(bl + 1) * 128], lambda k: winb[:, k, 512:1024], KC, [winb, hTg])
                kf = kf_sb[blk % 2]
                op('dve', lambda e, kf=kf: e.tensor_copy(out=kf[:], in_=pk[:]), [pk], [kf])
                dma('sp', k_p.t[r0:r0 + 128, :], kf[:], [kf], [k_p], kf, is_output=True)
        if STOP == 'P1b': break
        if is_halo_sb:
            groups = [(512, 0, False, False), (128, 384, False, True)]
        else:
            groups = [(512, 0, is_own and not os.environ.get('SKIP_OWNPROJ'), False)]
        for gi, (ncol, c_lo, own_, halo_) in enumerate(groups):
            only_qu = (gi == 1)
            for kind, idx, buf in proj_group(hTg, ncol, sbi, own_, halo_, c_lo):
                if kind == 'kn':
                    if only_qu: continue
                    dma('sp', knT_d.t[idx, :, sbi * 512:sbi * 512 + ncol], buf[:, 0:ncol], [buf], [knT_d], buf)
                elif kind == 'q':
                    qc0 = 0 if halo_ else 128 + (sbi - 12) * 512
                    dma('sp', qT_d.t[idx, :, qc0:qc0 + ncol], buf[:, 0:ncol], [buf], [qT_d], buf)
                else:
                    uc0 = UOFF + (0 if halo_ else 128 + (sbi - 12) * 512)
                    op('dve' if os.environ.get('E2') else 'pool', lambda e, idx=idx, buf=buf, uc0=uc0, ncol=ncol: e.tensor_copy(out=uT_b[:, idx, uc0:uc0 + ncol], in_=buf[:, 0:ncol]), [buf], [uT_b])
                    if sbi == 15:
                        dma('sp', conv_p.t[:, idx * 128:(idx + 1) * 128].rearrange("t p -> p t"), buf[:, 482:512], [buf], [conv_p], buf, is_output=True, allow_slow_non_contiguous=True)
    d.pop()

    if STOP in ('P1', 'P1a', 'P1b', 'P1c', 'P1d', 'P1e', 'P1f'): return bail()
    def attn_chain(steps, nq, zmms, avmm, bias_ap, wk, prepA=None):
        e_sb, sp_sb, e2_sb, a_sb, accf, accb = wk['e'], wk['sp'], wk['e2'], wk['a'], wk['accf'], wk['accb']
        ns = len(steps)
        def stageA(si):
            if prepA is not None: prepA(si)
            pz = P[si % 2]
            zmms(si, pz, True, True)
            es_ = e_sb[si % 3]; sp_ = sp_sb[si % 3]
            if bias_ap is not None:
                op('act', lambda e: e.activation(out=es_[:, 0:nq], in_=pz[:, 0:nq], func=AF.Exp, scale=-1.0, bias=bias_ap), [pz, sbb], [es_])
            else:
                op('act', lambda e: e.activation(out=es_[:, 0:nq], in_=pz[:, 0:nq], func=AF.Exp, scale=-1.0), [pz], [es_])
            op('act', lambda e: e.activation(out=sp_[:, 0:nq], in_=es_[:, 0:nq], func=AF.Ln, bias=1.0, scale=1.0), [es_], [sp_])
            mk = steps[si]['mask']
            if mk is not None:
                op('pool', lambda e: e.tensor_tensor(out=sp_[:, 0:nq], in0=sp_[:, 0:nq], in1=mk[:, 0:nq], op=ALU.mult), [sp_, mk], [sp_])
            if si < ns - 1:
                if si == 0:
                    op('pool', lambda e: e.tensor_copy(out=accf[:, 0:nq], in_=sp_[:, 0:nq]), [sp_], [accf])
                else:
                    op('pool', lambda e: e.tensor_tensor(out=accf[:, 0:nq], in0=accf[:, 0:nq], in1=sp_[:, 0:nq], op=ALU.add), [accf, sp_], [accf])
                ab2 = accb[si % 3]
                op('dve', lambda e: e.tensor_copy(out=ab2[:, 0:nq], in_=accf[:, 0:nq]), [accf], [ab2])
        def stageB1(si):
            first = si == 0
            py = P[2 + si % 2]
            es_ = e_sb[si % 3]; sp_ = sp_sb[si % 3]; e2_ = e2_sb[si % 2]; a_ = a_sb[si % 2]
            op('pe', lambda e: e.matmul(py[:, 0:nq], lhsT=tri_b[:], rhs=sp_[:, 0:nq], start=True, stop=first), [tri_b, sp_], [py])
            if not first:
                ab = accb[(si - 1) % 3]
                op('pe', lambda e: e.matmul(py[:, 0:nq], lhsT=ones_b[:], rhs=ab[:, 0:nq], start=False, stop=True), [ones_b, ab], [py])
            op('act', lambda e: e.activation(out=e2_[:, 0:nq], in_=py[:, 0:nq], func=AF.Exp, scale=-1.0), [py], [e2_])
            op('dve', lambda e: e.tensor_tensor(out=a_[:, 0:nq], in0=es_[:, 0:nq], in1=e2_[:, 0:nq], op=ALU.mult), [es_, e2_], [a_])
            mk = steps[si]['mask']
            if mk is not None:
                op('dve', lambda e: e.tensor_tensor(out=a_[:, 0:nq], in0=a_[:, 0:nq], in1=mk[:, 0:nq], op=ALU.mult), [a_, mk], [a_])
        def stageB2(si):
            avmm(si, a_sb[si % 2], si == 0, si == ns - 1)
        stageA(0)
        if ns > 1: stageA(1)
        for si in range(ns):
            stageB1(si)
            if si + 2 < ns: stageA(si + 2)
            if si >= 1: stageB2(si - 1)
        stageB2(ns - 1)

    d.push()
    knT = d.sb("knT", [128, NSLOT], BF16)
    vpair = d.sb("vpair", [128, NBLK, 128], BF16)
    qT = d.sb("qT", [128, NQ], BF16)
    wk = dict(e=[d.sb(f"e_sb{i}", [128, 512], F32) for i in range(3)],
              sp=[d.sb(f"sp_sb{i}", [128, 512], BF16) for i in range(3)],
              e2=[d.sb(f"e2_sb{i}", [128, 512], F32) for i in range(2)],
              a=[d.sb(f"a_sb{i}", [128, 512], BF16) for i in range(2)],
              accf=d.sb("accf", [128, 512], F32),
              accb=[d.sb(f"accb{i}", [128, 512], BF16) for i in range(3)])
    opair = d.sb("opair", [128, NQ], F32)
    sqw = [d.sb("sqw0", [128, 512], BF16)]
    rtmp = d.sb("rtmp", [128, 512], F32)
    mixs = d.sb("mixs", [128, NQ], BF16)
    v_dv = v_d.t.rearrange("(b p) c -> p b c", p=128)
    for pc in range(4):
        dma('sp', knT[:], knT_d.t[pc], [knT_d], [knT], knT)
        for q4 in range(4):
            dma('sp', vpair[:, q4 * 16:(q4 + 1) * 16, :], v_dv[:, q4 * 16:(q4 + 1) * 16, pc * 128:(pc + 1) * 128], [v_d], [vpair], vpair)
        dma('sp', qT[:], qT_d.t[pc], [qT_d], [qT], qT)
        for h in range(2):
            hs = slice(64 * h, 64 * h + 64)
            head = pc * 2 + h
            bias_ap = sbb[:, head:head + 1]
            glist = [(0, 128, 47, 1)] + [(128 + 512 * g, 512, 48 + 4 * g, 4) for g in range(4)]
            for (qc0, nq, diag0, ndiag) in glist:
                top = diag0 + ndiag - 1
                kbs = list(range(top, -1, -1))
                steps = [dict(mask=(Mi[kb - diag0] if kb >= diag0 else None)) for kb in kbs]
                po = P[4 + h]
                def zmms(si, pb, start, stop_unused, kbs=kbs, qc0=qc0, nq=nq, hs=hs):
                    kb = kbs[si]
                    op('pe', lambda e: e.matmul(pb[:, 0:nq], lhsT=knT[hs, kb * 128:(kb + 1) * 128], rhs=qT[hs, qc0:qc0 + nq], start=start, stop=True), [knT, qT], [pb])
                def avmm(si, a_, first, last, kbs=kbs, nq=nq, po=po):
                    kb = kbs[si]
                    op('pe', lambda e: e.matmul(po[:, 0:nq], lhsT=vpair[:, kb, :], rhs=a_[:, 0:nq], start=first, stop=last), [vpair, a_], [po])
                attn_chain(steps, nq, zmms, avmm, bias_ap, wk)
                op('act', lambda e, po=po, qc0=qc0, nq=nq, hs=hs: e.copy(out=opair[hs, qc0:qc0 + nq], in_=po[hs, 0:nq]), [po], [opair])
        for c0 in range(0, NQ, 512):
            n = min(512, NQ - c0)
            colstat_rstd([(opair, opair[:, c0:c0 + n])], n, bones[:], 128, P[6], rtmp, sqw)
            op('dve', lambda e, c0=c0, n=n, pc=pc: e.scalar_tensor_tensor(out=mixs[:, c0:c0 + n], in0=opair[:, c0:c0 + n], scalar=col(C_GNSB + pc), in1=rtmp[:, 0:n], op0=ALU.mult, op1=ALU.mult), [opair, colp, rtmp], [mixs])
        dma('sp', mixT_d.t[pc], mixs[:], [mixs], [mixT_d], mixs)
    d.pop()

    if STOP == 'P2': return bail()
    def conv_post(cvT, n, outfn, wkf, pb1):
        sq = wkf['sq']; tmp = wkf['tmp']; sg = wkf['sg']
        for cc in range(4):
            op('pe', lambda e, cc=cc: e.matmul(pb1[:, 0:n], lhsT=m512[:], rhs=cvT[:, cc, 0:n], start=(cc == 0), stop=(cc == 3)), [m512, cvT], [pb1])
        for cc in range(4):
            op('dve', lambda e, cc=cc: e.tensor_tensor(out=cvT[:, cc, 0:n], in0=cvT[:, cc, 0:n], in1=pb1[:, 0:n], op=ALU.subtract), [cvT, pb1], [cvT])
        colstat_rstd([(cvT, cvT[:, cc, 0:n]) for cc in range(4)], n, m512[:], 128, pb1, tmp, sq)
        for cc in range(4):
            op('dve', lambda e, cc=cc: e.tensor_tensor(out=cvT[:, cc, 0:n], in0=cvT[:, cc, 0:n], in1=tmp[:, 0:n], op=ALU.mult), [cvT, tmp], [cvT])
            op('dve', lambda e, cc=cc: e.tensor_scalar(out=cvT[:, cc, 0:n], in0=cvT[:, cc, 0:n], scalar1=col(C_CLNG + cc), scalar2=col(C_CLNB + cc), op0=ALU.mult, op1=ALU.add), [cvT, colp], [cvT])
            op('act', lambda e, cc=cc: e.activation(out=sg[:, 0:n], in_=cvT[:, cc, 0:n], func=AF.Sigmoid), [cvT], [sg])
            op('dve', lambda e, cc=cc: e.tensor_tensor(out=cvT[:, cc, 0:n], in0=cvT[:, cc, 0:n], in1=sg[:, 0:n], op=ALU.mult), [cvT, sg], [cvT])
        colstat_rstd([(cvT, cvT[:, cc, 0:n]) for cc in range(4)], n, m512[:], 128, pb1, tmp, sq)
        for cc in range(4):
            op('dve', lambda e, cc=cc: e.scalar_tensor_tensor(out=cvT[:, cc, 0:n], in0=cvT[:, cc, 0:n], scalar=col(C_GNCV + cc), in1=tmp[:, 0:n], op0=ALU.mult, op1=ALU.mult), [cvT, colp, tmp], [cvT])
            outfn(cc)

    d.push()
    cvw = [d.sb(f"cvw{i}", [128, 4, 512], F32) for i in range(2)]
    mixc = [d.sb(f"mixc{i}", [128, 4, 512], BF16) for i in range(2)]
    for ci, c0 in enumerate(range(0, NQ, 512)):
        n = min(512, NQ - c0)
        cv = cvw[ci % 2]; mx = mixc[ci % 2]
        for cc in range(4):
            pb = P[cc % 2]
            for k in range(31):
                op('pe', lambda e, k=k, cc=cc, pb=pb: e.matmul(pb[:, 0:n], lhsT=dwd[:, k * 4 + cc, :], rhs=uT_b[:, cc, c0 + 2 + k:c0 + 2 + k + n], start=(k == 0), stop=(k == 30)), [dwd, uT_b], [pb])
            op('act', lambda e, cc=cc, pb=pb: e.activation(out=cv[:, cc, 0:n], in_=pb[:, 0:n], func=AF.Identity, bias=col(C_BDW + cc), scale=1.0), [pb, colp], [cv])
        def outfn(cc, cv=cv, mx=mx, n=n):
            op('pool', lambda e: e.tensor_copy(out=mx[:, cc, 0:n], in_=cv[:, cc, 0:n]), [cv], [mx])
        conv_post(cv, n, outfn, wkf, P[2])
        for cc in range(4):
            dma('sp', mixT_d.t[4 + cc, :, c0:c0 + n], mx[:, cc, 0:n], [mx], [mixT_d], mx)

    d.pop()
    d.pop()
    if STOP == 'P3': return bail()
    d.push()
    A1s = d.sb("A1s", [32, D], F32); B1s = d.sb("B1s", [32, D], F32)
    for s_ in range(4):
        load_mod(B1s, 1 + s_, 0, 8, 8 * s_); load_mod(A1s, 1 + s_, 1024, 8, 8 * s_)
    winb2 = d.sb("winb2", [128, KC, 2560], BF16)
    for k in range(KC):
        dma('pool', winb2[:, k, :], w_in.t[k * 128:(k + 1) * 128, :], [w_in], [winb2], winb2)
    xts = d.sb("xts", [32, D], F32); t1s = d.sb("t1s", [32, D], F32); hbs = d.sb("hbs", [32, D], BF16)
    hTs = d.sb("hTs", [128, KC, 32], BF16)
    dma('sp', xts[:], x_s[:], [x_s], [xts], xts)
    modulate_transpose(xts, 32, A1s, B1s, None, hbs, t1s, hTs, 0)
    qT_n = d.sb("qT_n", [128, 4, 32], BF16)
    vnew = d.sb("vnew", [128, 4, 512], BF16)
    vtok = d.sb("vtok", [32, 512], F32); ktok = d.sb("ktok", [32, 512], F32)
    vtokb = d.sb("vtokb", [32, 512], BF16)
    us_f = d.sb("us_f", [128, 4, 32], F32)
    sgs = d.sb("sgs", [128, 32], F32)
    knT_f = d.sb("knT_f", [128, 4, 32], BF16)
    for pc in range(4):
        pb = P[pc % 2]
        mm_acc(pb, pb[:, 0:32], lambda k, pc=pc: winb2[:, k, 512 + pc * 128:512 + (pc + 1) * 128], lambda k: hTs[:, k, :], KC, [winb2, hTs])
        op('act', lambda e, pb=pb, pc=pc: e.activation(out=knT_f[:, pc, :], in_=pb[:, 0:32], func=AF.Identity, scale=-0.125), [pb], [knT_f])
        pq = P[2 + pc % 2]
        mm_acc(pq, pq[:, 0:32], lambda k, pc=pc: winb2[:, k, pc * 128:(pc + 1) * 128], lambda k: hTs[:, k, :], KC, [winb2, hTs])
        op('dve', lambda e, pq=pq, pc=pc: e.tensor_copy(out=qT_n[:, pc, :], in_=pq[:, 0:32]), [pq], [qT_n])
    pv = P[4]
    mm_acc(pv, pv[0:32, :], lambda k: hTs[:, k, :], lambda k: winb2[:, k, 1024:1536], KC, [winb2, hTs])
    op('dve', lambda e: e.tensor_copy(out=vtok[:], in_=pv[0:32, :]), [pv], [vtok])
    op('act', lambda e: e.copy(out=vtokb[:], in_=pv[0:32, :]), [pv], [vtokb])
    dma('sp', v_s[:], vtok[:], [vtok], [v_s], vtok, is_output=True)
    pk = P[5]
    mm_acc(pk, pk[0:32, :], lambda k: hTs[:, k, :], lambda k: winb2[:, k, 512:1024], KC, [winb2, hTs])
    op('dve', lambda e: e.tensor_copy(out=ktok[:], in_=pk[0:32, :]), [pk], [ktok])
    dma('sp', k_s[:], ktok[:], [ktok], [k_s], ktok, is_output=True)
    op('pool', lambda e: e.memset(vnew[:], 0.0), [], [vnew])
    for s_ in range(4):
        dma('sp', vnew[0:8, s_, :], vtokb[8 * s_:8 * s_ + 8, :], [vtokb], [vnew], vnew)
    for cc in range(4):
        pa, pg = P[0], P[1]
        mm_acc(pa, pa[:, 0:32], lambda k, cc=cc: winb2[:, k, 1536 + cc * 128:1536 + (cc + 1) * 128], lambda k: hTs[:, k, :], KC, [winb2, hTs])
        mm_acc(pg, pg[:, 0:32], lambda k, cc=cc: winb2[:, k, 2048 + cc * 128:2048 + (cc + 1) * 128], lambda k: hTs[:, k, :], KC, [winb2, hTs])
        op('act', lambda e: e.activation(out=sgs[:], in_=pg[:, 0:32], func=AF.Sigmoid), [pg], [sgs])
        op('dve', lambda e, cc=cc: e.tensor_tensor(out=us_f[:, cc, :], in0=pa[:, 0:32], in1=sgs[:], op=ALU.mult), [pa, sgs], [us_f])
    u_ext = d.sb("u_ext", [128, 4, 4, 38], BF16)
    stc = d.sb("stc", [120, 512], F32)
    dma('sp', stc[:], st_conv.t.rearrange("s t c -> (s t) c"), [st_conv], [stc], stc)
    for cc in range(4):
        pb = P[2 + cc % 2]
        op('pe', lambda e, cc=cc, pb=pb: e.transpose(out=pb[:, 0:120], in_=stc[:, cc * 128:(cc + 1) * 128], identity=ident_f[0:120, 0:120]), [stc, ident_f], [pb])
        op('dve', lambda e, cc=cc, pb=pb: e.tensor_copy(out=u_ext[:, cc, :, 0:30], in_=pb[:, 0:120].rearrange("p (s t) -> p s t", s=4)), [pb], [u_ext])
        op('pool', lambda e, cc=cc: e.tensor_copy(out=u_ext[:, cc, :, 30:38], in_=us_f[:, cc, :].rearrange("p (s t) -> p s t", s=4)), [us_f], [u_ext])
        for s_ in range(4):
            dma('sp', conv_s.t[s_, 22:30, cc * 128:(cc + 1) * 128].rearrange("t p -> p t"), us_f[:, cc, 8 * s_:8 * s_ + 8], [us_f], [conv_s], us_f, is_output=True, allow_slow_non_contiguous=True)
    cps = d.sb("cps", [88, 512], F32)
    for s_ in range(4):
        dma('sp', cps[22 * s_:22 * s_ + 22, :], st_conv.t[s_, 8:30, :], [st_conv], [cps], cps)
    for s_ in range(4):
        dma('sp', conv_s.t[s_, 0:22, :], cps[22 * s_:22 * s_ + 22, :], [cps], [conv_s], cps, is_output=True)
    cvs = d.sb("cvs", [128, 4, 32], F32)
    mixTs = d.sb("mixTs", [128, 8, 32], BF16)
    for cc in range(4):
        pb = P[cc % 2]
        for k in range(31):
            op('pe', lambda e, k=k, cc=cc, pb=pb: e.matmul(pb[:, 0:32], lhsT=dwd[:, k * 4 + cc, :], rhs=u_ext[:, cc, :, k:k + 8], start=(k == 0), stop=(k == 30)), [dwd, u_ext], [pb])
        op('act', lambda e, cc=cc, pb=pb: e.activation(out=cvs[:, cc, :], in_=pb[:, 0:32], func=AF.Identity, bias=col(C_BDW + cc), scale=1.0), [pb, colp], [cvs])
    def outfn_s(cc):
        op('pool', lambda e: e.tensor_copy(out=mixTs[:, 4 + cc, :], in_=cvs[:, cc, :]), [cvs], [mixTs])
    conv_post(cvs, 32, outfn_s, wkf, P[2])

    if STOP == 'S1': return bail()
    Qbd = d.sb("Qbd", [128, 4, 4, 16], BF16)
    op('pool', lambda e: e.memset(Qbd[:], 0.0), [], [Qbd])
    for pc in range(4):
        op('dve', lambda e, pc=pc: e.tensor_copy(out=Qbd[0:64, pc, :, 0:8], in_=qT_n[0:64, pc, :].rearrange("p (s t) -> p s t", s=4)), [qT_n], [Qbd])
        op('dve', lambda e, pc=pc: e.tensor_copy(out=Qbd[64:128, pc, :, 8:16], in_=qT_n[64:128, pc, :].rearrange("p (s t) -> p s t", s=4)), [qT_n], [Qbd])
    negb = d.sb("negb", [1, 256], F32)
    onesrow = d.sb("onesrow", [1, 128], F32)
    dma('sp', negb[:], brow[:], [brow], [negb], negb)
    op('dve', lambda e: e.tensor_scalar(out=negb[:], in0=negb[:], scalar1=-1.0, scalar2=None, op0=ALU.mult), [negb], [negb])
    op('pool', lambda e: e.memset(onesrow[:], 1.0), [], [onesrow])
    knT_ns = d.sb("knT_ns", [128, 4, 4, 128], BF16)
    op('pool', lambda e: e.memset(knT_ns[:], 0.0), [], [knT_ns])
    for s_ in range(4):
        op('dve', lambda e, s_=s_: e.tensor_copy(out=knT_ns[:, s_, :, 0:8], in_=knT_f[:, :, 8 * s_:8 * s_ + 8]), [knT_f], [knT_ns])
    pt_i = d.sb("pt_i", [128, 512], I32)
    ar_f = d.sb("ar_f", [128, 1], F32)
    offs = d.sb("offs", [128, 512], I32)
    dma('sp', pt_i[:], ptab.t.partition_broadcast(128), [ptab], [pt_i], pt_i)
    dma('sp', ar_f[:], arange[:], [arange], [ar_f], ar_f)
    op('dve', lambda e: e.tensor_scalar(out=offs[:], in0=pt_i[:], scalar1=128.0, scalar2=ar_f[:, 0:1], op0=ALU.mult, op1=ALU.add), [pt_i, ar_f], [offs])
    NKP = 3; NVP = 4
    kpg = [[d.sb(f"kpg{i}_{s_}", [128, 512], BF16) for s_ in range(4)] for i in range(NKP)]
    vpg = [[d.sb(f"vpg{i}_{s_}", [128, 512], BF16) for s_ in range(4)] for i in range(NVP)]
    kTp = [d.sb(f"kTp{i}", [128, 4, 4, 128], BF16) for i in range(2)]
    wks = dict(e=[d.sb(f"se_sb{i}", [128, 256], F32) for i in range(3)],
               sp=[d.sb(f"ssp_sb{i}", [128, 256], BF16) for i in range(3)],
               e2=[d.sb(f"se2_sb{i}", [128, 256], F32) for i in range(2)],
               a=[d.sb(f"sa_sb{i}", [128, 256], BF16) for i in range(2)],
               accf=d.sb("saccf", [128, 256], F32),
               accb=[d.sb(f"saccb{i}", [128, 256], BF16) for i in range(3)])
    steps = [dict(mask=Ms)] + [dict(mask=None) for _ in range(128)]
    state = {}
    def prep_pageK(si):
        pg = 128 - si
        i = si % NKP
        for s_ in range(4):
            cidx = s_ * 128 + pg
            dma('pool', kpg[i][s_][:], cache_k.t, [cache_k, offs], [kpg[i][s_]], kpg[i][s_], indirect=bass.IndirectOffsetOnAxis(ap=offs[:, cidx:cidx + 1], axis=0))
    def prep_pageV(si):
        pg = 128 - si
        iv = si % NVP
        for s_ in range(4):
            cidx = s_ * 128 + pg
            dma('pool', vpg[iv][s_][:], cache_v.t, [cache_v, offs], [vpg[iv][s_]], vpg[iv][s_], indirect=bass.IndirectOffsetOnAxis(ap=offs[:, cidx:cidx + 1], axis=0))
    def prep_kT(si):
        i = si % NKP
        kt = kTp[si % 2]
        for s_ in range(4):
            for pc in range(4):
                op('pe', lambda e, s_=s_, pc=pc: e.transpose(out=PT[:, pc * 128:(pc + 1) * 128], in_=kpg[i][s_][:, pc * 128:(pc + 1) * 128], identity=ident_b[:]), [kpg[i][s_], ident_b], [PT])
            op('dve', lambda e, s_=s_: e.tensor_scalar(out=kt[:, s_, :, :], in0=PT[:, 0:512].rearrange("p (c k) -> p c k", c=4), scalar1=-0.125, scalar2=None, op0=ALU.mult), [PT], [kt])
    def zmms_s(si, pb, start, stop_unused):
        kt = knT_ns if si == 0 else kTp[si % 2]
        op('pe', lambda e: e.matmul(pb[:, 0:256], lhsT=onesrow[:], rhs=negb[:], start=start, stop=False), [onesrow, negb], [pb])
        n = 0
        for s_ in range(4):
            for pc in range(4):
                n += 1
                c0 = s_ * 64 + pc * 16
                op('pe', lambda e, s_=s_, pc=pc, c0=c0, n=n: e.matmul(pb[:, c0:c0 + 16], lhsT=kt[:, s_, pc, :], rhs=Qbd[:, pc, s_, :], start=False, stop=(n == 16)), [kt, Qbd], [pb])
    po_s = P[6]
    def avmm_s(si, a_, first, last):
        for s_ in range(4):
            vsrc = vnew[:, s_, :] if si == 0 else vpg[si % NVP][s_][:]
            vb = vnew if si == 0 else vpg[si % NVP][s_]
            for h in range(8):
                c0 = s_ * 64 + h * 8
                op('pe', lambda e, vsrc=vsrc, h=h, c0=c0, s_=s_: e.matmul(po_s[0:64, c0:c0 + 8], lhsT=vsrc[:, h * 64:(h + 1) * 64], rhs=a_[:, c0:c0 + 8], start=(first and s_ == 0 and h == 0), stop=last, skip_group_check=True), [vb, a_], [po_s])
    prep_pageK(1); prep_pageK(2)
    def prepA_s(si):
        if si == 0: return
        if si + 2 <= 128: prep_pageK(si + 2)
        prep_pageV(si)
        prep_kT(si)
    attn_chain(steps, 256, zmms_s, avmm_s, None, wks, prepA=prepA_s)
    osf = d.sb("osf", [64, 256], F32)
    osq = [d.sb("osq", [64, 256], BF16)]
    ortmp = d.sb("ortmp", [64, 256], F32)
    gsb = d.sb("gsb", [64, 8], F32)
    osn = d.sb("osn", [64, 4, 8, 8], BF16)
    dma('sp', gsb[:], gsb_hd[:], [gsb_hd], [gsb], gsb)
    op('act', lambda e: e.copy(out=osf[:], in_=po_s[0:64, 0:256]), [po_s], [osf])
    colstat_rstd([(osf, osf[:, :])], 256, bones[0:64, 0:64], 64, P[0], ortmp, osq)
    op('dve', lambda e: e.tensor_tensor(out=osf[:], in0=osf[:], in1=ortmp[:], op=ALU.mult), [osf, ortmp], [osf])
    osf4 = osf[:, :].rearrange("p (s h t) -> p s h t", s=4, h=8)
    for h_ in range(8):
        op('dve', lambda e, h_=h_: e.tensor_scalar(out=osn[:, :, h_, :], in0=osf4[:, :, h_, :], scalar1=gsb[:, h_:h_ + 1], scalar2=None, op0=ALU.mult), [osf, gsb], [osn])
    for pc in range(4):
        pb = P[1 + pc % 2]
        for s_ in range(4):
            op('pe', lambda e, pc=pc, s_=s_, pb=pb: e.matmul(pb[:, s_ * 8:(s_ + 1) * 8], lhsT=ident_b[0:64, :], rhs=osn[:, s_, 2 * pc, :], start=True, stop=False), [ident_b, osn], [pb])
            op('pe', lambda e, pc=pc, s_=s_, pb=pb: e.matmul(pb[:, s_ * 8:(s_ + 1) * 8], lhsT=shiftI[:], rhs=osn[:, s_, 2 * pc + 1, :], start=False, stop=True), [shiftI, osn], [pb])
        op('dve', lambda e, pc=pc, pb=pb: e.tensor_copy(out=mixTs[:, pc, :], in_=pb[:, 0:32]), [pb], [mixTs])
    mixTs_keep = d.dram("mixTs_d", [128, 8, 32], BF16, OUT if os.environ.get("DBG") else "Internal")
    dma('sp', mixTs_keep[:], mixTs[:], [mixTs], [mixTs_keep], mixTs)
    d.pop()
    d.pop()

    if STOP == 'S2': return bail()
    d.push()
    gts_halo = d.sb("gts_halo", [128, NFC, 8], F32)
    d.push()
    stf = d.sb("stf", [8, DFF], F32)
    dma('sp', stf[:], st_ffn.t.rearrange("s t c -> (s t) c"), [st_ffn], [stf], stf)
    for fc in range(NFC):
        pb = P[fc % 2]
        op('pe', lambda e, fc=fc, pb=pb: e.transpose(out=pb[:, 0:8], in_=stf[:, fc * 128:(fc + 1) * 128], identity=ident_f[0:8, 0:8]), [stf, ident_f], [pb])
        op('dve', lambda e, fc=fc, pb=pb: e.tensor_copy(out=gts_halo[:, fc, :], in_=pb[:, 0:8]), [pb], [gts_halo])
    d.pop()
    woutb = d.sb("woutb", [128, 8, D], BF16)
    for k in range(8):
        dma('pool', woutb[:, k, :], w_out.t[k * 128:(k + 1) * 128, :], [w_out], [woutb], woutb)
    wdnb = d.sb("wdnb", [128, NFC, D], BF16)
    for fc in range(NFC):
        dma('pool', wdnb[:, fc, :], w_down.t[fc * 128:(fc + 1) * 128, :], [w_down], [wdnb], wdnb)
    lnt = [d.sb(f"lnt{i}", [128, D], F32) for i in range(4)]
    for i in range(4):
        dma('sp', lnt[i][:], lnp.t[i:i + 1, :].broadcast_to([128, D]), [lnp], [lnt[i]], lnt[i])
    xt4 = [d.sb(f"xq{i}", [128, D], F32) for i in range(1)]
    tmp4 = d.sb("tmp4", [128, D], F32)
    x1buf = d.sb("x1buf", [128, 4, D], F32)
    hb4 = d.sb("hb4", [128, D], BF16)
    h2T = d.sb("h2T", [128, KC, 512], BF16)
    mix_sb = d.sb("mix_sb", [128, 8, 512], BF16)
    wupg = [d.sb(f"wupg{i}", [128, KC, 128], BF16) for i in range(2)]
    wupv = [d.sb(f"wupv{i}", [128, KC, 128], BF16) for i in range(2)]
    gt_ext = [d.sb(f"gt_ext{i}", [128, 520], F32) for i in range(2)]
    gc = d.sb("gc", [128, 512], F32)
    ge = d.sb("ge", [128, 512], F32)
    fT = d.sb("fT", [128, NFC, 512], BF16)
    ybuf = [d.sb(f"ybuf{i}", [128, D], F32) for i in range(1)]
    stats = d.sb("stats", [128, 12], F32); mv = d.sb("mv", [128, 2], F32); rstd = d.sb("rstd", [128, 1], F32)
    w_up_v = w_up.t.rearrange("(k p) n -> p k n", p=128)
    A2s = d.sb("A2s", [32, D], F32); B2s = d.sb("B2s", [32, D], F32); G1s = d.sb("G1s", [32, D], F32); G2s = d.sb("G2s", [32, D], F32)
    for s_ in range(4):
        load_mod(G1s, 1 + s_, 2048, 8, 8 * s_); load_mod(B2s, 1 + s_, 3072, 8, 8 * s_)
        load_mod(A2s, 1 + s_, 4096, 8, 8 * s_); load_mod(G2s, 1 + s_, 5120, 8, 8 * s_)
    yi = 0

    def token_front(blocks, mixsrc, ncols_total, g1t, a2t, b2t):
        for bi, (ntok, x_ap, xdep, mc0, slot) in enumerate(blocks):
            xq = xt4[0]
            dma('sp', xq[0:ntok, :], x_ap, [xdep], [xq], xq)
            for hf in range(2):
                pb = P[hf]
                mm_acc(pb, pb[0:ntok, :], lambda k: mixsrc[:, k, mc0:mc0 + ntok], lambda k, hf=hf: woutb[:, k, hf * 512:(hf + 1) * 512], 8, [mixsrc, woutb])
                op('dve', lambda e, hf=hf, pb=pb: e.tensor_tensor(out=tmp4[0:ntok, hf * 512:(hf + 1) * 512], in0=pb[0:ntok, :], in1=g1t[0:ntok, hf * 512:(hf + 1) * 512], op=ALU.mult), [pb, g1t], [tmp4])
            op('dve', lambda e: e.scalar_tensor_tensor(out=tmp4[0:ntok, :], in0=xq[0:ntok, :], scalar=ALPHA, in1=tmp4[0:ntok, :], op0=ALU.mult, op1=ALU.add), [xq, tmp4], [tmp4])
            x1 = x1buf
            layernorm_rows_slot(tmp4, ntok, lnt[0], lnt[1], slot)
            modulate_transpose_slot(ntok, slot, a2t, b2t, bi * 128)

    def layernorm_rows_slot(src, ntok, gam, bet, slot):
        for hf in range(2):
            op('dve', lambda e, hf=hf: e.bn_stats(out=stats[0:ntok, hf * 6:(hf + 1) * 6], in_=src[0:ntok, hf * 512:(hf + 1) * 512]), [src], [stats])
        op('dve', lambda e: e.bn_aggr(out=mv[0:ntok, :], in_=stats[0:ntok, :]), [stats], [mv])
        op('dve', lambda e: e.tensor_scalar(out=rstd[0:ntok, :], in0=mv[0:ntok, 1:2], scalar1=EPS, scalar2=None, op0=ALU.add), [mv], [rstd])
        op('pool', lambda e: e.tensor_tensor(out=rstd[0:ntok, :], in0=rstd[0:ntok, :], in1=mhalf[0:ntok, 0:1], op=ALU.pow), [rstd, mhalf], [rstd])
        op('dve', lambda e: e.tensor_scalar(out=x1buf[0:ntok, slot, :], in0=src[0:ntok, :], scalar1=mv[0:ntok, 0:1], scalar2=rstd[0:ntok, 0:1], op0=ALU.subtract, op1=ALU.mult), [src, mv, rstd], [x1buf])
        op('dve', lambda e: e.tensor_tensor(out=x1buf[0:ntok, slot, :], in0=x1buf[0:ntok, slot, :], in1=gam[0:ntok, :], op=ALU.mult), [gam], [x1buf])
        op('dve', lambda e: e.tensor_tensor(out=x1buf[0:ntok, slot, :], in0=x1buf[0:ntok, slot, :], in1=bet[0:ntok, :], op=ALU.add), [bet], [x1buf])

    def modulate_transpose_slot(ntok, slot, a2t, b2t, c0):
        op('dve', lambda e: e.tensor_tensor(out=tmp4[0:ntok, :], in0=x1buf[0:ntok, slot, :], in1=a2t[0:ntok, :], op=ALU.mult), [x1buf, a2t], [tmp4])
        op('dve', lambda e: e.tensor_tensor(out=hb4[0:ntok, :], in0=tmp4[0:ntok, :], in1=b2t[0:ntok, :], op=ALU.add), [tmp4, b2t], [hb4])
        for k in range(KC):
            op('pe', lambda e, k=k: e.transpose(out=PT[:, k * 128:k * 128 + ntok], in_=hb4[0:ntok, k * 128:(k + 1) * 128], identity=ident_b[0:ntok, 0:ntok]), [hb4, ident_b], [PT])
        op('act', lambda e: e.copy(out=h2T[:, :, c0:c0 + ntok], in_=PT[:].rearrange("p (k t) -> p k t", k=KC)[:, :, 0:ntok]), [PT], [h2T])

    def ffn_mid(n, halo_src, nseq, do_val, cs=None):
        tl = n // nseq
        c_lo = 0 if cs is None else cs
        for fc in range(NFC):
            wg = wupg[fc % 2]; wv = wupv[fc % 2]
            dma('pool', wg[:], w_up_v[:, :, DFF + fc * 128:DFF + (fc + 1) * 128], [w_up], [wg], wg)
            if do_val:
                dma('pool', wv[:], w_up_v[:, :, fc * 128:(fc + 1) * 128], [w_up], [wv], wv)
            pg = P[2 + fc % 2]; pvv = P[4 + fc % 2]
            mm_acc(pg, pg[:, 0:n], lambda k: wg[:, k, :], lambda k: h2T[:, k, c_lo:c_lo + n], KC, [wg, h2T])
            if do_val:
                mm_acc(pvv, pvv[:, 0:n], lambda k: wv[:, k, :], lambda k: h2T[:, k, c_lo:c_lo + n], KC, [wv, h2T])
            gx = gt_ext[fc % 2]
            gxv = gx[:, 0:nseq * (tl + 2)].rearrange("p (s t) -> p s t", s=nseq)
            yield ('gate', fc, pg, gx, gxv, pvv)

    def conv3_gelu(fc, gxv, nseq, tl, n, pvv):
        gcv = gc[:, 0:n].rearrange("p (s t) -> p s t", s=nseq)
        op('dve', lambda e: e.tensor_scalar(out=gcv, in0=gxv[:, :, 0:tl], scalar1=col(C_WFDW + 0 * NFC + fc), scalar2=col(C_BFDW + fc), op0=ALU.mult, op1=ALU.add), [gt_ext[fc % 2], colp], [gc])
        for j in (1, 2):
            op('dve', lambda e, j=j: e.scalar_tensor_tensor(out=gcv, in0=gxv[:, :, j:j + tl], scalar=col(C_WFDW + j * NFC + fc), in1=gcv, op0=ALU.mult, op1=ALU.add), [gt_ext[fc % 2], colp, gc], [gc])
        op('act', lambda e: e.activation(out=ge[:, 0:n], in_=gc[:, 0:n], func=AF.Gelu), [gc], [ge])
        op('dve', lambda e: e.tensor_tensor(out=fT[:, fc, 0:n], in0=ge[:, 0:n], in1=pvv[:, 0:n], op=ALU.mult), [ge, pvv], [fT])

    def token_back(blocks, g2t, out_t):
        nonlocal yi
        for bi, (ntok, slot, out_ap) in enumerate(blocks):
            for hf in range(2):
                pb = P[hf]
                mm_acc(pb, pb[0:ntok, :], lambda k: fT[:, k, bi * 128:bi * 128 + ntok], lambda k, hf=hf: wdnb[:, k, hf * 512:(hf + 1) * 512], NFC, [fT, wdnb])
                op('dve', lambda e, hf=hf, pb=pb: e.tensor_tensor(out=tmp4[0:ntok, hf * 512:(hf + 1) * 512], in0=pb[0:ntok, :], in1=g2t[0:ntok, hf * 512:(hf + 1) * 512], op=ALU.mult), [pb, g2t], [tmp4])
            op('dve', lambda e: e.scalar_tensor_tensor(out=tmp4[0:ntok, :], in0=x1buf[0:ntok, slot, :], scalar=ALPHA, in1=tmp4[0:ntok, :], op0=ALU.mult, op1=ALU.add), [x1buf, tmp4], [tmp4])
            yb = ybuf[0]; yi += 1
            layernorm_rows(tmp4, ntok, lnt[2], lnt[3], yb, stats, mv, rstd)
            dma('sp', out_ap, yb[0:ntok, :], [yb], [out_t], yb, is_output=True)

    dma('sp', mix_sb[:, :, 0:128], mixT_d.t[:, :, 0:128].rearrange("c p t -> p c t"), [mixT_d], [mix_sb], mix_sb)
    token_front([(128, xs.t[47 * 128:48 * 128, :], xs, 0, 0)], mix_sb, 128, G1, A2, B2)
    for (kind, fc, pg, gx, gxv, pvv) in ffn_mid(2, None, 1, False, cs=126):
        op('dve', lambda e, fc=fc, pg=pg: e.tensor_scalar(out=gthalo[:, fc, :], in0=pg[:, 0:2], scalar1=hfl[:, 0:1], scalar2=None, op0=ALU.mult), [pg, hfl], [gthalo])
    for sbo in range(4):
        c0 = 128 + sbo * 512
        dma('sp', mix_sb[:], mixT_d.t[:, :, c0:c0 + 512].rearrange("c p t -> p c t"), [mixT_d], [mix_sb], mix_sb)
        blocks = [(128, xs.t[(48 + sbo * 4 + bl) * 128:(49 + sbo * 4 + bl) * 128, :], xs, bl * 128, bl) for bl in range(4)]
        token_front(blocks, mix_sb, 512, G1, A2, B2)
        for (kind, fc, pg, gx, gxv, pvv) in ffn_mid(512, None, 1, True):
            op('pool', lambda e, fc=fc, gx=gx: e.tensor_copy(out=gx[:, 0:2], in_=gthalo[:, fc, :]), [gthalo], [gx])
            op('act', lambda e, pg=pg, gx=gx: e.copy(out=gx[:, 2:514], in_=pg[:, 0:512]), [pg], [gx])
            op('pool', lambda e, fc=fc, gx=gx: e.tensor_copy(out=gthalo[:, fc, :], in_=gx[:, 512:514]), [gx], [gthalo])
            conv3_gelu(fc, gxv, 1, 512, 512, pvv)
        token_back([(128, bl, y_p.t[(sbo * 4 + bl) * 128:(sbo * 4 + bl + 1) * 128, :]) for bl in range(4)], G2, y_p)
    for fc in range(NFC):
        dma('sp', ffn_p.t[:, fc * 128:(fc + 1) * 128].rearrange("t p -> p t"), gthalo[:, fc, :], [gthalo], [ffn_p], gthalo, is_output=True, allow_slow_non_contiguous=True)
    mixTs2 = d.sb("mixTs2", [128, 8, 32], BF16)
    dma('sp', mixTs2[:], mixTs_keep[:], [mixTs_keep], [mixTs2], mixTs2)
    token_front([(32, x_s[:], x_s, 0, 0)], mixTs2, 32, G1s, A2s, B2s)
    gsn = d.sb("gsn", [128, NFC, 8], F32)
    for (kind, fc, pg, gx, gxv, pvv) in ffn_mid(32, None, 4, True):
        op('pool', lambda e, fc=fc, gxv=gxv: e.tensor_copy(out=gxv[:, :, 0:2], in_=gts_halo[:, fc, :].rearrange("p (s t) -> p s t", s=4)), [gts_halo], [gx])
        op('act', lambda e, pg=pg, gxv=gxv: e.copy(out=gxv[:, :, 2:10], in_=pg[:, 0:32].rearrange("p (s t) -> p s t", s=4)), [pg], [gx])
        op('pool', lambda e, fc=fc, gxv=gxv: e.tensor_copy(out=gsn[:, fc, :].rearrange("p (s t) -> p s t", s=4), in_=gxv[:, :, 8:10]), [gx], [gsn])
        conv3_gelu(fc, gxv, 4, 8, 32, pvv)
    token_back([(32, 0, y_s[:])], G2s, y_s)
    for fc in range(NFC):
        for s_ in range(4):
            dma('sp', ffn_s.t[s_, :, fc * 128:(fc + 1) * 128].rearrange("t p -> p t"), gsn[:, fc, 2 * s_:2 * s_ + 2], [gsn], [ffn_s], gsn, is_output=True, allow_slow_non_contiguous=True)
    d.finish()
    d.stacks.pop().close()
    d.stacks.pop().close()
    return nc


_NC = None


def make_in_maps(inp):
    f32 = np.float32
    x_prompt = np.asarray(inp['x_prompt'], f32); x_sample = np.asarray(inp['x_sample'], f32)
    ck = np.ascontiguousarray(np.asarray(inp['cache_k'], f32)[0].reshape(5120 * 128, 512))
    cv = np.ascontiguousarray(np.asarray(inp['cache_v'], f32)[0].reshape(5120 * 128, 512))
    page_table = np.asarray(inp['page_table'], np.int32)
    g = lambda k: np.asarray(inp[k], f32)[0]
    w_dw = g('w_dw'); w_fdw = g('w_fdw')
    def cols(v, n):
        return np.ascontiguousarray(v.reshape(n, 128).T)
    colp = np.concatenate([
        cols(g('gn_sb'), 4), cols(g('gn_conv'), 4), cols(g('cln_g'), 4), cols(g('cln_b'), 4), cols(g('b_dw'), 4),
        np.ascontiguousarray(w_dw.reshape(31, 4, 128).transpose(2, 0, 1).reshape(128, 124)),
        cols(g('b_fdw'), NFC),
        np.ascontiguousarray(w_fdw.reshape(3, NFC, 128).transpose(2, 0, 1).reshape(128, 66)),
    ], axis=1).astype(f32)
    assert colp.shape == (128, NCOLP)
    sbb = g('sb_bias').reshape(1, 8)
    brow = np.ascontiguousarray(np.broadcast_to(sbb.reshape(1, 1, 8, 1), (1, 4, 8, 8)).reshape(1, 256))
    gsb_hd = np.ascontiguousarray(g('gn_sb').reshape(8, 64).T)
    lnp = np.stack([g('ln1_g'), g('ln1_b'), g('ln2_g'), g('ln2_b')]).astype(f32)
    shared = dict(cache_k=ck, cache_v=cv, arange=np.arange(128, dtype=f32).reshape(128, 1),
                  w_ada=g('w_ada'), b_ada=g('b_ada').reshape(1, -1), w_in=g('w_in'), sbbias=sbb, brow=brow,
                  colp=colp, gsb_hd=gsb_hd, w_out=g('w_out'), lnp=lnp, w_up=g('w_up'), w_down=g('w_down'))
    in_maps = []
    for c in range(8):
        b, j = c // 4, c % 4
        nreal = OWN * (j + 1)
        xs = np.zeros((NSLOT, D), f32)
        xs[NSLOT - nreal:] = x_prompt[b, :nreal]
        valid = np.zeros(NSLOT, f32); valid[NSLOT - nreal:] = 1.0
        vmask = np.ascontiguousarray(valid.reshape(NBLK, 128).T)
        hflag = np.full((128, 1), 1.0 if j > 0 else 0.0, f32)
        c5 = np.concatenate([np.asarray(inp['c_prompt'], f32)[b:b + 1], np.asarray(inp['c_sample'], f32)[4 * c:4 * c + 4]], 0)
        c5p = np.zeros((32, D), f32); c5p[0:5] = c5
        c5T = np.ascontiguousarray(c5p.reshape(32, KC, 128).transpose(2, 1, 0))
        m = dict(shared)
        m.update(xs=xs, vmask=vmask, hflag=hflag, c5T=c5T,
                 x_s=np.ascontiguousarray(x_sample[4 * c:4 * c + 4].reshape(32, D)),
                 ptab=np.ascontiguousarray(page_table[4 * c:4 * c + 4].reshape(-1)),
                 st_conv=np.ascontiguousarray(np.asarray(inp['state_conv'], f32)[0, 4 * c:4 * c + 4]),
                 st_ffn=np.ascontiguousarray(np.asarray(inp['state_ffn'], f32)[0, 4 * c:4 * c + 4]))
        in_maps.append(m)
    return in_maps


def kernel(**inp):
    global _NC
    f32 = np.float32
    in_maps = make_in_maps(inp)
    if _NC is None:
        _NC = build_nc()
    res = run_bass_kernel_spmd(_NC, in_maps, core_ids=list(range(8)))
    R = res.results
    y_prompt = np.zeros((2, 8192, D), f32); k_prompt = np.zeros((1, 2, 8192, 8, 64), f32); v_prompt = np.zeros_like(k_prompt)
    conv_prompt = np.zeros((1, 2, 30, 512), f32); ffn_prompt = np.zeros((1, 2, 2, DFF), f32)
    y_sample = np.zeros((32, 8, D), f32); k_sample = np.zeros((1, 32, 8, 8, 64), f32); v_sample = np.zeros_like(k_sample)
    conv_sample = np.zeros((1, 32, 30, 512), f32); ffn_sample = np.zeros((1, 32, 2, DFF), f32)
    for c in range(8):
        b, j = c // 4, c % 4
        r = R[c]
        y_prompt[b, OWN * j:OWN * (j + 1)] = r['y_p']
        k_prompt[0, b, OWN * j:OWN * (j + 1)] = r['k_p'].reshape(OWN, 8, 64)
        v_prompt[0, b, OWN * j:OWN * (j + 1)] = r['v_p'].reshape(OWN, 8, 64)
        if j == 3:
            conv_prompt[0, b] = r['conv_p']; ffn_prompt[0, b] = r['ffn_p']
        y_sample[4 * c:4 * c + 4] = r['y_s'].reshape(4, 8, D)
        k_sample[0, 4 * c:4 * c + 4] = r['k_s'].reshape(4, 8, 8, 64)
        v_sample[0, 4 * c:4 * c + 4] = r['v_s'].reshape(4, 8, 8, 64)
        conv_sample[0, 4 * c:4 * c + 4] = r['conv_s']; ffn_sample[0, 4 * c:4 * c + 4] = r['ffn_s']
    return (y_prompt, y_sample, k_prompt, v_prompt, conv_prompt, ffn_prompt, k_sample, v_sample, conv_sample, ffn_sample)
```

```python
import numpy as np
import os
from contextlib import ExitStack
import concourse.bass as bass
import concourse.mybir as mybir
from concourse.bass_utils import run_bass_kernel_spmd

F32 = mybir.dt.float32; BF16 = mybir.dt.bfloat16; I32 = mybir.dt.int32
ALU = mybir.AluOpType; AF = mybir.ActivationFunctionType

D = 1024; KC = 8; NSLOT = 8192; NBLK = 64; OWN = 2048; DFF = 2816; NFC = 22
ALPHA = 2.0 ** 0.25; EPS = 1e-5
NQ = 128 + OWN
UOFF = 32
NCOLP = 4 * 5 + 124 + 22 + 66


class Buf:
    def __init__(self, t, name):
        self.t = t; self.name = name
        self.w = None; self.r = []
        self.dsem = None; self.dcount = 0
    def __getitem__(self, idx):
        return self.t[idx]


class Dep:
    def __init__(self, nc):
        self.nc = nc
        self.stacks = [ExitStack()]
        self.eng = {'pe': nc.tensor, 'act': nc.scalar, 'dve': nc.vector, 'pool': nc.gpsimd, 'sp': nc.sync}
        self.sem = {k: nc.alloc_semaphore(name=f"sem_{k}") for k in self.eng}
        self.cnt = {k: 0 for k in self.eng}
        self.seen = {k: {} for k in self.eng}
        self.out_tokens = []
        self.dsems = []
        self.dsem_pool = []
        self.scope_bufs = [[]]
        self.retired = {}

    def push(self):
        self.stacks.append(ExitStack())
        self.scope_bufs.append([])
    def pop(self):
        self.barrier()
        for b in self.scope_bufs.pop():
            if b.dsem is not None:
                self.dsem_pool.append((b.dsem, b.dcount))
                self.dsems = [o for o in self.dsems if o is not b]
                self.retired[id(b.dsem)] = (b.dsem, b.dcount)
                b.dsem = None
        self.stacks.pop().close()
    def sb(self, name, shape, dt):
        b = Buf(self.stacks[-1].enter_context(self.nc.sbuf_tensor(name, shape, dt)), name)
        self.scope_bufs[-1].append(b)
        return b
    def ps(self, name, shape, dt):
        b = Buf(self.stacks[-1].enter_context(self.nc.psum_tensor(name, shape, dt)), name)
        b.psum = True
        return b
    def dram(self, name, shape, dt, kind="Internal"):
        return Buf(self.nc.dram_tensor(name, shape, dt, kind=kind).ap(), name)

    def _wait(self, e, s, v):
        k = id(s)
        if self.seen[e].get(k, 0) >= v: return
        self.eng[e].wait_ge(s, v)
        self.seen[e][k] = v

    def _waits(self, e, reads, writes):
        need = {}
        def add(tok):
            if tok is None: return
            s, v = tok
            k = id(s)
            if k not in need or need[k][1] < v: need[k] = (s, v)
        for b in reads: add(b.w)
        for b in writes:
            add(b.w)
            for t in b.r: add(t)
        for k, (s, v) in need.items():
            self._wait(e, s, v)

    def _record(self, tok, reads, writes):
        for b in reads:
            b.r.append(tok)
            if len(b.r) > 24:
                m = {}
                for s, v in b.r:
                    if id(s) not in m or m[id(s)][1] < v: m[id(s)] = (s, v)
                b.r = list(m.values())
        for b in writes:
            b.w = tok; b.r = []

    def op(self, e, fn, reads=(), writes=()):
        pr = [b for b in reads if getattr(b, 'psum', False)]
        if pr:
            writes = list(writes) + [b for b in pr if b not in writes]
            reads = [b for b in reads if not getattr(b, 'psum', False)]
        self._waits(e, reads, writes)
        inst = fn(self.eng[e])
        self.cnt[e] += 1
        inst.then_inc(self.sem[e], 1)
        self._record((self.sem[e], self.cnt[e]), reads, writes)
        return inst

    def dma(self, q, out, in_, reads, writes, owner, is_output=False, indirect=None, **kw):
        self._waits(q, reads, writes)
        eng = self.eng[q]
        if indirect is not None:
            inst = eng.indirect_dma_start(out=out, out_offset=None, in_=in_, in_offset=indirect, **kw)
        else:
            inst = eng.dma_start(out=out, in_=in_, **kw)
        if owner.dsem is None:
            if self.dsem_pool:
                owner.dsem, owner.dcount = self.dsem_pool.pop()
                self.retired.pop(id(owner.dsem), None)
            else:
                owner.dsem = self.nc.alloc_semaphore(name=f"dsem_{owner.name}")
            self.dsems.append(owner)
        owner.dcount += 16
        inst.then_inc(owner.dsem, 16)
        tok = (owner.dsem, owner.dcount)
        self._record(tok, reads, writes)
        if is_output: self.out_tokens.append(tok)
        return inst

    def barrier(self):
        for e in self.eng:
            for e2 in self.eng:
                if self.cnt[e2] > 0: self._wait(e, self.sem[e2], self.cnt[e2])
            for o in self.dsems:
                self._wait(e, o.dsem, o.dcount)
            for (sm, v) in self.retired.values():
                self._wait(e, sm, v)

    def finish(self):
        self.barrier()


def build_nc(NPOOL=5120, STOP=None):
    nc = bass.Bass("TRN2", target_bir_lowering=False)
    d = Dep(nc)
    op = d.op; dma = d.dma
    def bail():
        d.finish()
        while d.stacks: d.stacks.pop().close()
        return nc
    IN = "ExternalInput"; OUT = "ExternalOutput"
    xs = d.dram("xs", [NSLOT, D], F32, IN)
    vmask = d.dram("vmask", [128, NBLK], F32, IN)
    hflag = d.dram("hflag", [128, 1], F32, IN)
    c5T = d.dram("c5T", [128, KC, 32], F32, IN)
    x_s = d.dram("x_s", [32, D], F32, IN)
    cache_k = d.dram("cache_k", [NPOOL * 128, 512], F32, IN)
    cache_v = d.dram("cache_v", [NPOOL * 128, 512], F32, IN)
    ptab = d.dram("ptab", [4 * 128], I32, IN)
    arange = d.dram("arange", [128, 1], F32, IN)
    st_conv = d.dram("st_conv", [4, 30, 512], F32, IN)
    st_ffn = d.dram("st_ffn", [4, 2, DFF], F32, IN)
    w_ada = d.dram("w_ada", [D, 6 * D], F32, IN)
    b_ada = d.dram("b_ada", [1, 6 * D], F32, IN)
    w_in = d.dram("w_in", [D, 2560], F32, IN)
    sbbias = d.dram("sbbias", [1, 8], F32, IN)
    brow = d.dram("brow", [1, 256], F32, IN)
    colp_d = d.dram("colp", [128, NCOLP], F32, IN)
    gsb_hd = d.dram("gsb_hd", [64, 8], F32, IN)
    w_out = d.dram("w_out", [D, D], F32, IN)
    lnp = d.dram("lnp", [4, D], F32, IN)
    w_up = d.dram("w_up", [D, 2 * DFF], F32, IN)
    w_down = d.dram("w_down", [DFF, D], F32, IN)

    y_p = d.dram("y_p", [OWN, D], F32, OUT)
    k_p = d.dram("k_p", [OWN, 512], F32, OUT)
    v_p = d.dram("v_p", [OWN, 512], F32, OUT)
    conv_p = d.dram("conv_p", [30, 512], F32, OUT)
    ffn_p = d.dram("ffn_p", [2, DFF], F32, OUT)
    y_s = d.dram("y_s", [32, D], F32, OUT)
    k_s = d.dram("k_s", [32, 512], F32, OUT)
    v_s = d.dram("v_s", [32, 512], F32, OUT)
    conv_s = d.dram("conv_s", [4, 30, 512], F32, OUT)
    ffn_s = d.dram("ffn_s", [4, 2, DFF], F32, OUT)

    mod_d = d.dram("mod_d", [5, 6 * D], F32)
    knT_d = d.dram("knT_d", [4, 128, NSLOT], BF16)
    v_d = d.dram("v_d", [NSLOT, 512], BF16)
    qT_d = d.dram("qT_d", [4, 128, NQ], BF16)
    mixT_d = d.dram("mixT_d", [8, 128, NQ], BF16)

    P = [d.ps(f"pb{i}", [128, 512], F32) for i in range(7)]
    PT = d.ps("ptr", [128, 1024], BF16)

    ones_f = d.sb("ones_f", [128, 128], F32)
    ident_f = d.sb("ident_f", [128, 128], F32)
    ident_b = d.sb("ident_b", [128, 128], BF16)
    tri_b = d.sb("tri_b", [128, 128], BF16)
    ones_b = d.sb("ones_b", [128, 128], BF16)
    m512 = d.sb("m512", [128, 128], F32)
    bones = d.sb("bones", [128, 128], BF16)
    shiftI = d.sb("shiftI", [64, 128], BF16)
    Mi = [d.sb(f"Mi{i}", [128, 512], BF16) for i in range(4)]
    Ms = d.sb("Ms", [128, 256], BF16)
    ones512 = d.sb("ones512", [128, 512], BF16)
    mhalf = d.sb("mhalf", [128, 512], F32)
    colp = d.sb("colp_sb", [128, NCOLP], F32)
    sbb = d.sb("sbb", [128, 8], F32)
    vm = d.sb("vm", [128, NBLK], F32)
    hfl = d.sb("hfl", [128, 1], F32)
    G1 = d.sb("G1", [128, D], F32); A2 = d.sb("A2", [128, D], F32)
    B2 = d.sb("B2", [128, D], F32); G2 = d.sb("G2", [128, D], F32)
    gthalo = d.sb("gthalo", [128, NFC, 2], F32)

    op('pool', lambda e: e.memset(ones_f[:], 1.0), [], [ones_f])
    op('pool', lambda e: e.memset(ones_b[:], 1.0), [], [ones_b])
    op('pool', lambda e: e.memset(ones512[:], 1.0), [], [ones512])
    op('pool', lambda e: e.memset(m512[:], 1.0 / 512.0), [], [m512])
    op('pool', lambda e: e.memset(mhalf[:], -0.5), [], [mhalf])
    op('pool', lambda e: e.memset(bones[:], 0.0), [], [bones])
    op('pool', lambda e: e.memset(bones[0:64, 0:64], 1.0 / 64.0), [], [bones])
    op('pool', lambda e: e.memset(bones[64:128, 64:128], 1.0 / 64.0), [], [bones])
    op('pool', lambda e: e.affine_select(out=ident_f[:], in_=ones_f[:], pattern=[[-1, 128]], compare_op=ALU.is_equal, fill=0.0, base=0, channel_multiplier=1), [ones_f], [ident_f])
    op('pool', lambda e: e.tensor_copy(out=ident_b[:], in_=ident_f[:]), [ident_f], [ident_b])
    op('pool', lambda e: e.affine_select(out=tri_b[:], in_=ones_b[:], pattern=[[-1, 128]], compare_op=ALU.is_ge, fill=0.0, base=0, channel_multiplier=1), [ones_b], [tri_b])
    op('pool', lambda e: e.affine_select(out=shiftI[:], in_=ones_b[0:64, :], pattern=[[-1, 128]], compare_op=ALU.is_equal, fill=0.0, base=64, channel_multiplier=1), [ones_b], [shiftI])
    for i in range(4):
        op('pool', lambda e, i=i: e.affine_select(out=Mi[i][:], in_=ones512[:], pattern=[[1, 512]], compare_op=ALU.is_gt, fill=0.0, base=-128 * i, channel_multiplier=-1), [ones512], [Mi[i]])
    op('pool', lambda e: e.affine_select(out=Ms[:], in_=ones512[:, 0:256], pattern=[[0, 32], [1, 8]], compare_op=ALU.is_gt, fill=0.0, base=0, channel_multiplier=-1), [ones512], [Ms])
    dma('sp', colp[:], colp_d[:], [colp_d], [colp], colp)
    dma('sp', sbb[:], sbbias.t.broadcast_to([128, 8]), [sbbias], [sbb], sbb)
    dma('sp', vm[:], vmask[:], [vmask], [vm], vm)
    dma('sp', hfl[:], hflag[:], [hflag], [hfl], hfl)
    C_GNSB, C_GNCV, C_CLNG, C_CLNB, C_BDW, C_WDW, C_BFDW, C_WFDW = 0, 4, 8, 12, 16, 20, 144, 166
    def col(c): return colp[:, c:c + 1]

    if STOP == 'C': return bail()
    d.push()
    cT = d.sb("cT", [128, KC * 32], F32)
    sT = d.sb("sT", [128, KC * 32], F32)
    e5 = d.sb("e5", [128, KC * 32], F32)
    modsb = d.sb("modsb", [5, 6 * D], F32)
    bada = d.sb("bada", [5, 6 * D], F32)
    wa = [d.sb(f"wa{i}", [128, KC, 512], F32) for i in range(2)]
    dma('sp', cT[:], c5T.t.rearrange("p k r -> p (k r)"), [c5T], [cT], cT)
    dma('sp', bada[:], b_ada.t.broadcast_to([5, 6 * D]), [b_ada], [bada], bada)
    op('act', lambda e: e.activation(out=e5[:], in_=cT[:], func=AF.Exp, scale=-1.0), [cT], [e5])
    op('dve', lambda e: e.tensor_scalar(out=e5[:], in0=e5[:], scalar1=1.0, scalar2=None, op0=ALU.add), [e5], [e5])
    op('dve', lambda e: e.reciprocal(out=e5[:], in_=e5[:]), [e5], [e5])
    op('dve', lambda e: e.tensor_tensor(out=sT[:], in0=cT[:], in1=e5[:], op=ALU.mult), [cT, e5], [sT])
    w_ada_v = w_ada.t.rearrange("(k p) n -> p k n", p=128)
    for n in range(12):
        w = wa[n % 2]
        dma('sp', w[:], w_ada_v[:, :, n * 512:(n + 1) * 512], [w_ada], [w], w)
        pb = P[n % 2]
        for k in range(KC):
            op('pe', lambda e, k=k, w=w, pb=pb: e.matmul(pb[0:32, :], lhsT=sT[:, k * 32:(k + 1) * 32], rhs=w[:, k, :], start=(k == 0), stop=(k == KC - 1)), [sT, w], [pb])
        op('dve', lambda e, n=n, pb=pb: e.tensor_tensor(out=modsb[:, n * 512:(n + 1) * 512], in0=pb[0:5, :], in1=bada[:, n * 512:(n + 1) * 512], op=ALU.add), [pb, bada], [modsb])
    for a, b_ in ((1024, 3072), (4096, 6144)):
        op('dve', lambda e, a=a, b_=b_: e.tensor_scalar(out=modsb[:, a:b_], in0=modsb[:, a:b_], scalar1=1.0, scalar2=None, op0=ALU.add), [modsb], [modsb])
    dma('sp', mod_d[:], modsb[:], [modsb], [mod_d], modsb)
    d.pop()

    if STOP == 'A': return bail()
    def load_mod(tile, row, lo, npart=128, p0=0):
        dma('sp', tile[p0:p0 + npart, :], mod_d.t[row:row + 1, lo:lo + D].broadcast_to([npart, D]), [mod_d], [tile], tile)
    load_mod(G1, 0, 2048); load_mod(B2, 0, 3072); load_mod(A2, 0, 4096); load_mod(G2, 0, 5120)

    def modulate_transpose(xt, ntok, A, B, vcol, hb, t1, hT, c0):
        op('dve', lambda e: e.tensor_tensor(out=t1[0:ntok, :], in0=xt[0:ntok, :], in1=A[0:ntok, :], op=ALU.mult), [xt, A], [t1])
        if vcol is not None:
            op('dve', lambda e: e.scalar_tensor_tensor(out=hb[0:ntok, :], in0=B[0:ntok, :], scalar=vcol, in1=t1[0:ntok, :], op0=ALU.mult, op1=ALU.add), [B, t1, vm], [hb])
        else:
            op('dve', lambda e: e.tensor_tensor(out=hb[0:ntok, :], in0=t1[0:ntok, :], in1=B[0:ntok, :], op=ALU.add), [B, t1], [hb])
        for k in range(KC):
            op('pe', lambda e, k=k: e.transpose(out=PT[:, k * 128:k * 128 + ntok], in_=hb[0:ntok, k * 128:(k + 1) * 128], identity=ident_b[0:ntok, 0:ntok]), [hb, ident_b], [PT])
        op('act', lambda e: e.copy(out=hT[:, :, c0:c0 + ntok], in_=PT[:].rearrange("p (k t) -> p k t", k=KC)[:, :, 0:ntok]), [PT], [hT])

    def mm_acc(pb, pslice, lhs_fn, rhs_fn, nk, reads):
        for k in range(nk):
            op('pe', lambda e, k=k: e.matmul(pslice, lhsT=lhs_fn(k), rhs=rhs_fn(k), start=(k == 0), stop=(k == nk - 1)), reads, [pb])

    def layernorm_rows(src, ntok, gam, bet, dst, stats, mv, rstd):
        for hf in range(2):
            op('dve', lambda e, hf=hf: e.bn_stats(out=stats[0:ntok, hf * 6:(hf + 1) * 6], in_=src[0:ntok, hf * 512:(hf + 1) * 512]), [src], [stats])
        op('dve', lambda e: e.bn_aggr(out=mv[0:ntok, :], in_=stats[0:ntok, :]), [stats], [mv])
        op('dve', lambda e: e.tensor_scalar(out=rstd[0:ntok, :], in0=mv[0:ntok, 1:2], scalar1=EPS, scalar2=None, op0=ALU.add), [mv], [rstd])
        op('pool', lambda e: e.tensor_tensor(out=rstd[0:ntok, :], in0=rstd[0:ntok, :], in1=mhalf[0:ntok, 0:1], op=ALU.pow), [rstd, mhalf], [rstd])
        op('dve', lambda e: e.tensor_scalar(out=dst[0:ntok, :], in0=src[0:ntok, :], scalar1=mv[0:ntok, 0:1], scalar2=rstd[0:ntok, 0:1], op0=ALU.subtract, op1=ALU.mult), [src, mv, rstd], [dst])
        op('dve', lambda e: e.tensor_tensor(out=dst[0:ntok, :], in0=dst[0:ntok, :], in1=gam[0:ntok, :], op=ALU.mult), [dst, gam], [dst])
        op('dve', lambda e: e.tensor_tensor(out=dst[0:ntok, :], in0=dst[0:ntok, :], in1=bet[0:ntok, :], op=ALU.add), [dst, bet], [dst])

    def colstat_rstd(srcs, n, lhs, npart, pb, tmp, wk):
        for i, (b, ap) in enumerate(srcs):
            op('dve', lambda e, ap=ap, i=i: e.tensor_tensor(out=wk[i][0:npart, 0:n], in0=ap, in1=ap, op=ALU.mult), [b], [wk[i]])
        for i in range(len(srcs)):
            op('pe', lambda e, i=i: e.matmul(pb[0:npart, 0:n], lhsT=lhs, rhs=wk[i][0:npart, 0:n], start=(i == 0), stop=(i == len(srcs) - 1)), [wk[i], m512, bones], [pb])
        op('dve', lambda e: e.tensor_scalar(out=tmp[0:npart, 0:n], in0=pb[0:npart, 0:n], scalar1=EPS, scalar2=None, op0=ALU.add), [pb], [tmp])
        op('act', lambda e: e.activation(out=tmp[0:npart, 0:n], in_=tmp[0:npart, 0:n], func=AF.Ln), [tmp], [tmp])
        op('act', lambda e: e.activation(out=tmp[0:npart, 0:n], in_=tmp[0:npart, 0:n], func=AF.Exp, scale=-0.5), [tmp], [tmp])

    d.push()
    dwd = d.sb("dwd", [128, 124, 128], BF16)
    for i in range(124):
        op('dve' if i % 2 else 'pool', lambda e, i=i: e.tensor_scalar(out=dwd[:, i, :], in0=ident_f[:], scalar1=col(C_WDW + i), scalar2=None, op0=ALU.mult), [ident_f, colp], [dwd])
    wkf = dict(sq=[d.sb(f"csq{i}", [128, 512], F32) for i in range(4)], tmp=d.sb("ctmp", [128, 512], F32), sg=d.sb("csg", [128, 512], F32))
    d.push()
    uT_b = d.sb("uT_b", [128, 4, UOFF + NQ], BF16)
    op('pool', lambda e: e.memset(uT_b[:], 0.0), [], [uT_b])
    d.push()
    A1 = d.sb("A1", [128, D], F32); B1 = d.sb("B1", [128, D], F32)
    load_mod(B1, 0, 0); load_mod(A1, 0, 1024)
    winb = d.sb("winb", [128, KC, 2560], BF16)
    for k in range(KC):
        dma('pool', winb[:, k, :], w_in.t[k * 128:(k + 1) * 128, :], [w_in], [winb], winb)
    xt = [d.sb(f"xt{i}", [128, D], F32) for i in range(3)]
    t1 = d.sb("t1", [128, D], F32)
    hb = [d.sb(f"hb{i}", [128, D], BF16) for i in range(2)]
    hT = [d.sb(f"hT{i}", [128, KC, 512], BF16) for i in range(2)]
    kn_sb = [d.sb(f"kn_sb{i}", [128, 512], BF16) for i in range(2)]
    q_sb = [d.sb(f"q_sb{i}", [128, 512], BF16) for i in range(2)]
    v_sb = [d.sb(f"v_sb{i}", [128, 512], BF16) for i in range(2)]
    vf_sb = [d.sb(f"vf_sb{i}", [128, 512], F32) for i in range(2)]
    kf_sb = [d.sb(f"kf_sb{i}", [128, 512], F32) for i in range(2)]
    sig = d.sb("sig", [128, 512], F32)
    uf = [d.sb(f"uf{i}", [128, 512], F32) for i in range(4)]

    def proj_group(hTg, ncol, sbi, is_own, is_halo, c_lo):
        cs = slice(c_lo, c_lo + ncol)
        for pc in range(4):
            pb = P[pc % 2]
            mm_acc(pb, pb[:, 0:ncol], lambda k, pc=pc: winb[:, k, 512 + pc * 128:512 + (pc + 1) * 128], lambda k: hTg[:, k, cs], KC, [winb, hTg])
            ks = kn_sb[pc % 2]
            op('act', lambda e, pb=pb, ks=ks: e.activation(out=ks[:, 0:ncol], in_=pb[:, 0:ncol], func=AF.Identity, scale=-0.125), [pb], [ks])
            yield ('kn', pc, ks)
        if is_own or is_halo:
            for pc in range(0 if os.environ.get('SKIP_Q') else 4):
                pb = P[2 + pc % 2]
                mm_acc(pb, pb[:, 0:ncol], lambda k, pc=pc: winb[:, k, pc * 128:(pc + 1) * 128], lambda k: hTg[:, k, cs], KC, [winb, hTg])
                qs = q_sb[pc % 2]
                op('act' if os.environ.get('E1') else 'dve', lambda e, pb=pb, qs=qs: (e.copy if os.environ.get('E1') else e.tensor_copy)(out=qs[:, 0:ncol], in_=pb[:, 0:ncol]), [pb], [qs])
                yield ('q', pc, qs)
            for cc in range(0 if os.environ.get('SKIP_U') else 4):
                pa, pg = P[4], P[5]
                mm_acc(pa, pa[:, 0:ncol], lambda k, cc=cc: winb[:, k, 1536 + cc * 128:1536 + (cc + 1) * 128], lambda k: hTg[:, k, cs], KC, [winb, hTg])
                mm_acc(pg, pg[:, 0:ncol], lambda k, cc=cc: winb[:, k, 2048 + cc * 128:2048 + (cc + 1) * 128], lambda k: hTg[:, k, cs], KC, [winb, hTg])
                op('act', lambda e: e.activation(out=sig[:, 0:ncol], in_=pg[:, 0:ncol], func=AF.Sigmoid), [pg], [sig])
                op('dve', lambda e, cc=cc: e.tensor_tensor(out=uf[cc][:, 0:ncol], in0=pa[:, 0:ncol], in1=sig[:, 0:ncol], op=ALU.mult), [pa, sig], [uf[cc]])
                yield ('u', cc, uf[cc])

    xi = 0
    for sbi in range(16):
        if STOP == 'P1c' and sbi >= 1: break
        if STOP == 'P1d' and sbi not in (0, 11): continue
        if STOP == 'P1e' and sbi not in (0, 12): continue
        if STOP == 'P1f' and sbi not in (0, 15): continue
        is_own = sbi >= 12
        is_halo_sb = sbi == 11
        hTg = hT[sbi % 2]
        for bl in range(4):
            blk = sbi * 4 + bl
            x_t = xt[xi % 3]; xi += 1
            dma('sp', x_t[:], xs.t[blk * 128:(blk + 1) * 128, :], [xs], [x_t], x_t)
            modulate_transpose(x_t, 128, A1, B1, vm[:, blk:blk + 1], hb[blk % 2], t1, hTg, bl * 128)
        if STOP == 'P1a': break
        for bl in range(4):
            blk = sbi * 4 + bl
            pb = P[6]
            mm_acc(pb, pb[:, :], lambda k, bl=bl: hTg[:, k, bl * 128:(bl + 1) * 128], lambda k: winb[:, k, 1024:1536], KC, [winb, hTg])
            vs_ = v_sb[blk % 2]
            op('act', lambda e, vs_=vs_: e.copy(out=vs_[:], in_=pb[:]), [pb], [vs_])
            dma('sp', v_d.t[blk * 128:(blk + 1) * 128, :], vs_[:], [vs_], [v_d], vs_)
            if is_own and not os.environ.get('SKIP_OWNOUT'):
                vf = vf_sb[blk % 2]
                op('dve', lambda e, vf=vf: e.tensor_copy(out=vf[:], in_=pb[:]), [pb], [vf])
                r0 = (blk - 48) * 128
                dma('sp', v_p.t[r0:r0 + 128, :], vf[:], [vf], [v_p], vf, is_output=True)
                pk = P[3]
                mm_acc(pk, pk[:, :], lambda k, bl=bl: hTg[:, k, bl * 128:(bl + 1) * 128], lambda k: winb[:, k, 512:1024], KC, [winb, hTg])
                kf = kf_sb[blk % 2]
                op('dve', lambda e, kf=kf: e.tensor_copy(out=kf[:], in_=pk[:]), [pk], [kf])
                dma('sp', k_p.t[r0:r0 + 128, :], kf[:], [kf], [k_p], kf, is_output=True)
        if STOP == 'P1b': break
        if is_halo_sb:
            groups = [(512, 0, False, False), (128, 384, False, True)]
        else:
            groups = [(512, 0, is_own and not os.environ.get('SKIP_OWNPROJ'), False)]
        for gi, (ncol, c_lo, own_, halo_) in enumerate(groups):
            only_qu = (gi == 1)
            for kind, idx, buf in proj_group(hTg, ncol, sbi, own_, halo_, c_lo):
                if kind == 'kn':
                    if only_qu: continue
                    dma('sp', knT_d.t[idx, :, sbi * 512:sbi * 512 + ncol], buf[:, 0:ncol], [buf], [knT_d], buf)
                elif kind == 'q':
                    qc0 = 0 if halo_ else 128 + (sbi - 12) * 512
                    dma('sp', qT_d.t[idx, :, qc0:qc0 + ncol], buf[:, 0:ncol], [buf], [qT_d], buf)
                else:
                    uc0 = UOFF + (0 if halo_ else 128 + (sbi - 12) * 512)
                    op('dve' if os.environ.get('E2') else 'pool', lambda e, idx=idx, buf=buf, uc0=uc0, ncol=ncol: e.tensor_copy(out=uT_b[:, idx, uc0:uc0 + ncol], in_=buf[:, 0:ncol]), [buf], [uT_b])
                    if sbi == 15:
                        dma('sp', conv_p.t[:, idx * 128:(idx + 1) * 128].rearrange("t p -> p t"), buf[:, 482:512], [buf], [conv_p], buf, is_output=True, allow_slow_non_contiguous=True)
    d.pop()

    if STOP in ('P1', 'P1a', 'P1b', 'P1c', 'P1d', 'P1e', 'P1f'): return bail()
    def attn_chain(steps, nq, zmms, avmm, bias_ap, wk, prepA=None):
        e_sb, sp_sb, e2_sb, a_sb, accf, accb = wk['e'], wk['sp'], wk['e2'], wk['a'], wk['accf'], wk['accb']
        ns = len(steps)
        def stageA(si):
            if prepA is not None: prepA(si)
            pz = P[si % 2]
            zmms(si, pz, True, True)
            es_ = e_sb[si % 3]; sp_ = sp_sb[si % 3]
            if bias_ap is not None:
                op('act', lambda e: e.activation(out=es_[:, 0:nq], in_=pz[:, 0:nq], func=AF.Exp, scale=-1.0, bias=bias_ap), [pz, sbb], [es_])
            else:
                op('act', lambda e: e.activation(out=es_[:, 0:nq], in_=pz[:, 0:nq], func=AF.Exp, scale=-1.0), [pz], [es_])
            op('act', lambda e: e.activation(out=sp_[:, 0:nq], in_=es_[:, 0:nq], func=AF.Ln, bias=1.0, scale=1.0), [es_], [sp_])
            mk = steps[si]['mask']
            if mk is not None:
                op('pool', lambda e: e.tensor_tensor(out=sp_[:, 0:nq], in0=sp_[:, 0:nq], in1=mk[:, 0:nq], op=ALU.mult), [sp_, mk], [sp_])
            if si < ns - 1:
                if si == 0:
                    op('pool', lambda e: e.tensor_copy(out=accf[:, 0:nq], in_=sp_[:, 0:nq]), [sp_], [accf])
                else:
                    op('pool', lambda e: e.tensor_tensor(out=accf[:, 0:nq], in0=accf[:, 0:nq], in1=sp_[:, 0:nq], op=ALU.add), [accf, sp_], [accf])
                ab2 = accb[si % 3]
                op('dve', lambda e: e.tensor_copy(out=ab2[:, 0:nq], in_=accf[:, 0:nq]), [accf], [ab2])
        def stageB1(si):
            first = si == 0
            py = P[2 + si % 2]
            es_ = e_sb[si % 3]; sp_ = sp_sb[si % 3]; e2_ = e2_sb[si % 2]; a_ = a_sb[si % 2]
            op('pe', lambda e: e.matmul(py[:, 0:nq], lhsT=tri_b[:], rhs=sp_[:, 0:nq], start=True, stop=first), [tri_b, sp_], [py])
            if not first:
                ab = accb[(si - 1) % 3]
                op('pe', lambda e: e.matmul(py[:, 0:nq], lhsT=ones_b[:], rhs=ab[:, 0:nq], start=False, stop=True), [ones_b, ab], [py])
            op('act', lambda e: e.activation(out=e2_[:, 0:nq], in_=py[:, 0:nq], func=AF.Exp, scale=-1.0), [py], [e2_])
            op('dve', lambda e: e.tensor_tensor(out=a_[:, 0:nq], in0=es_[:, 0:nq], in1=e2_[:, 0:nq], op=ALU.mult), [es_, e2_], [a_])
            mk = steps[si]['mask']
            if mk is not None:
                op('dve', lambda e: e.tensor_tensor(out=a_[:, 0:nq], in0=a_[:, 0:nq], in1=mk[:, 0:nq], op=ALU.mult), [a_, mk], [a_])
        def stageB2(si):
            avmm(si, a_sb[si % 2], si == 0, si == ns - 1)
        stageA(0)
        if ns > 1: stageA(1)
        for si in range(ns):
            stageB1(si)
            if si + 2 < ns: stageA(si + 2)
            if si >= 1: stageB2(si - 1)
        stageB2(ns - 1)

    d.push()
    knT2 = [d.sb(f"knT{i}", [128, NSLOT], BF16) for i in range(2)]
    vpair2 = [d.sb(f"vpair{i}", [128, NBLK, 128], BF16) for i in range(2)]
    qT2 = [d.sb(f"qT{i}", [128, NQ], BF16) for i in range(2)]
    wk = dict(e=[d.sb(f"e_sb{i}", [128, 512], F32) for i in range(3)],
              sp=[d.sb(f"sp_sb{i}", [128, 512], BF16) for i in range(3)],
              e2=[d.sb(f"e2_sb{i}", [128, 512], F32) for i in range(2)],
              a=[d.sb(f"a_sb{i}", [128, 512], BF16) for i in range(2)],
              accf=d.sb("accf", [128, 512], F32),
              accb=[d.sb(f"accb{i}", [128, 512], BF16) for i in range(3)])
    opair = d.sb("opair", [128, NQ], F32)
    sqw = [d.sb("sqw0", [128, 512], BF16)]
    rtmp = d.sb("rtmp", [128, 512], F32)
    mixs = d.sb("mixs", [128, NQ], BF16)
    v_dv = v_d.t.rearrange("(b p) c -> p b c", p=128)
    def load_pair(pc):
        knT = knT2[pc % 2]; vpair = vpair2[pc % 2]; qT = qT2[pc % 2]
        dma('sp', knT[:], knT_d.t[pc], [knT_d], [knT], knT)
        for q4 in range(4):
            dma('sp', vpair[:, q4 * 16:(q4 + 1) * 16, :], v_dv[:, q4 * 16:(q4 + 1) * 16, pc * 128:(pc + 1) * 128], [v_d], [vpair], vpair)
        dma('sp', qT[:], qT_d.t[pc], [qT_d], [qT], qT)
    load_pair(0)
    for pc in range(4):
        if pc + 1 < 4: load_pair(pc + 1)
        knT = knT2[pc % 2]; vpair = vpair2[pc % 2]; qT = qT2[pc % 2]
        for h in range(2):
            hs = slice(64 * h, 64 * h + 64)
            head = pc * 2 + h
            bias_ap = sbb[:, head:head + 1]
            glist = [(0, 128, 47, 1)] + [(128 + 512 * g, 512, 48 + 4 * g, 4) for g in range(4)]
            for (qc0, nq, diag0, ndiag) in glist:
                top = diag0 + ndiag - 1
                kbs = list(range(top, -1, -1))
                steps = [dict(mask=(Mi[kb - diag0] if kb >= diag0 else None)) for kb in kbs]
                po = P[4 + h]
                def zmms(si, pb, start, stop_unused, kbs=kbs, qc0=qc0, nq=nq, hs=hs, knT=knT, qT=qT):
                    kb = kbs[si]
                    op('pe', lambda e: e.matmul(pb[:, 0:nq], lhsT=knT[hs, kb * 128:(kb + 1) * 128], rhs=qT[hs, qc0:qc0 + nq], start=start, stop=True), [knT, qT], [pb])
                def avmm(si, a_, first, last, kbs=kbs, nq=nq, po=po, vpair=vpair):
                    kb = kbs[si]
                    op('pe', lambda e: e.matmul(po[:, 0:nq], lhsT=vpair[:, kb, :], rhs=a_[:, 0:nq], start=first, stop=last), [vpair, a_], [po])
                attn_chain(steps, nq, zmms, avmm, bias_ap, wk)
                op('act', lambda e, po=po, qc0=qc0, nq=nq, hs=hs: e.copy(out=opair[hs, qc0:qc0 + nq], in_=po[hs, 0:nq]), [po], [opair])
        for c0 in range(0, NQ, 512):
            n = min(512, NQ - c0)
            colstat_rstd([(opair, opair[:, c0:c0 + n])], n, bones[:], 128, P[6], rtmp, sqw)
            op('dve', lambda e, c0=c0, n=n, pc=pc: e.scalar_tensor_tensor(out=mixs[:, c0:c0 + n], in0=opair[:, c0:c0 + n], scalar=col(C_GNSB + pc), in1=rtmp[:, 0:n], op0=ALU.mult, op1=ALU.mult), [opair, colp, rtmp], [mixs])
        dma('sp', mixT_d.t[pc], mixs[:], [mixs], [mixT_d], mixs)
    d.pop()

    if STOP == 'P2': return bail()
    def conv_post(cvT, n, outfn, wkf, pb1):
        sq = wkf['sq']; tmp = wkf['tmp']; sg = wkf['sg']
        for cc in range(4):
            op('pe', lambda e, cc=cc: e.matmul(pb1[:, 0:n], lhsT=m512[:], rhs=cvT[:, cc, 0:n], start=(cc == 0), stop=(cc == 3)), [m512, cvT], [pb1])
        for cc in range(4):
            op('dve', lambda e, cc=cc: e.tensor_tensor(out=cvT[:, cc, 0:n], in0=cvT[:, cc, 0:n], in1=pb1[:, 0:n], op=ALU.subtract), [cvT, pb1], [cvT])
        colstat_rstd([(cvT, cvT[:, cc, 0:n]) for cc in range(4)], n, m512[:], 128, pb1, tmp, sq)
        for cc in range(4):
            op('dve', lambda e, cc=cc: e.tensor_tensor(out=cvT[:, cc, 0:n], in0=cvT[:, cc, 0:n], in1=tmp[:, 0:n], op=ALU.mult), [cvT, tmp], [cvT])
            op('dve', lambda e, cc=cc: e.tensor_scalar(out=cvT[:, cc, 0:n], in0=cvT[:, cc, 0:n], scalar1=col(C_CLNG + cc), scalar2=col(C_CLNB + cc), op0=ALU.mult, op1=ALU.add), [cvT, colp], [cvT])
            op('act', lambda e, cc=cc: e.activation(out=sg[:, 0:n], in_=cvT[:, cc, 0:n], func=AF.Sigmoid), [cvT], [sg])
            op('dve', lambda e, cc=cc: e.tensor_tensor(out=cvT[:, cc, 0:n], in0=cvT[:, cc, 0:n], in1=sg[:, 0:n], op=ALU.mult), [cvT, sg], [cvT])
        colstat_rstd([(cvT, cvT[:, cc, 0:n]) for cc in range(4)], n, m512[:], 128, pb1, tmp, sq)
        for cc in range(4):
            op('dve', lambda e, cc=cc: e.scalar_tensor_tensor(out=cvT[:, cc, 0:n], in0=cvT[:, cc, 0:n], scalar=col(C_GNCV + cc), in1=tmp[:, 0:n], op0=ALU.mult, op1=ALU.mult), [cvT, colp, tmp], [cvT])
            outfn(cc)

    d.push()
    cvw = [d.sb(f"cvw{i}", [128, 4, 512], F32) for i in range(2)]
    mixc = [d.sb(f"mixc{i}", [128, 4, 512], BF16) for i in range(2)]
    for ci, c0 in enumerate(range(0, NQ, 512)):
        n = min(512, NQ - c0)
        cv = cvw[ci % 2]; mx = mixc[ci % 2]
        for cc in range(4):
            pb = P[cc % 2]
            for k in range(31):
                op('pe', lambda e, k=k, cc=cc, pb=pb: e.matmul(pb[:, 0:n], lhsT=dwd[:, k * 4 + cc, :], rhs=uT_b[:, cc, c0 + 2 + k:c0 + 2 + k + n], start=(k == 0), stop=(k == 30)), [dwd, uT_b], [pb])
            op('act', lambda e, cc=cc, pb=pb: e.activation(out=cv[:, cc, 0:n], in_=pb[:, 0:n], func=AF.Identity, bias=col(C_BDW + cc), scale=1.0), [pb, colp], [cv])
        def outfn(cc, cv=cv, mx=mx, n=n):
            op('pool', lambda e: e.tensor_copy(out=mx[:, cc, 0:n], in_=cv[:, cc, 0:n]), [cv], [mx])
        conv_post(cv, n, outfn, wkf, P[2])
        for cc in range(4):
            dma('sp', mixT_d.t[4 + cc, :, c0:c0 + n], mx[:, cc, 0:n], [mx], [mixT_d], mx)

    d.pop()
    d.pop()
    if STOP == 'P3': return bail()
    d.push()
    A1s = d.sb("A1s", [32, D], F32); B1s = d.sb("B1s", [32, D], F32)
    for s_ in range(4):
        load_mod(B1s, 1 + s_, 0, 8, 8 * s_); load_mod(A1s, 1 + s_, 1024, 8, 8 * s_)
    winb2 = d.sb("winb2", [128, KC, 2560], BF16)
    for k in range(KC):
        dma('pool', winb2[:, k, :], w_in.t[k * 128:(k + 1) * 128, :], [w_in], [winb2], winb2)
    xts = d.sb("xts", [32, D], F32); t1s = d.sb("t1s", [32, D], F32); hbs = d.sb("hbs", [32, D], BF16)
    hTs = d.sb("hTs", [128, KC, 32], BF16)
    dma('sp', xts[:], x_s[:], [x_s], [xts], xts)
    modulate_transpose(xts, 32, A1s, B1s, None, hbs, t1s, hTs, 0)
    qT_n = d.sb("qT_n", [128, 4, 32], BF16)
    vnew = d.sb("vnew", [128, 4, 512], BF16)
    vtok = d.sb("vtok", [32, 512], F32); ktok = d.sb("ktok", [32, 512], F32)
    vtokb = d.sb("vtokb", [32, 512], BF16)
    us_f = d.sb("us_f", [128, 4, 32], F32)
    sgs = d.sb("sgs", [128, 32], F32)
    knT_f = d.sb("knT_f", [128, 4, 32], BF16)
    for pc in range(4):
        pb = P[pc % 2]
        mm_acc(pb, pb[:, 0:32], lambda k, pc=pc: winb2[:, k, 512 + pc * 128:512 + (pc + 1) * 128], lambda k: hTs[:, k, :], KC, [winb2, hTs])
        op('act', lambda e, pb=pb, pc=pc: e.activation(out=knT_f[:, pc, :], in_=pb[:, 0:32], func=AF.Identity, scale=-0.125), [pb], [knT_f])
        pq = P[2 + pc % 2]
        mm_acc(pq, pq[:, 0:32], lambda k, pc=pc: winb2[:, k, pc * 128:(pc + 1) * 128], lambda k: hTs[:, k, :], KC, [winb2, hTs])
        op('dve', lambda e, pq=pq, pc=pc: e.tensor_copy(out=qT_n[:, pc, :], in_=pq[:, 0:32]), [pq], [qT_n])
    pv = P[4]
    mm_acc(pv, pv[0:32, :], lambda k: hTs[:, k, :], lambda k: winb2[:, k, 1024:1536], KC, [winb2, hTs])
    op('dve', lambda e: e.tensor_copy(out=vtok[:], in_=pv[0:32, :]), [pv], [vtok])
    op('act', lambda e: e.copy(out=vtokb[:], in_=pv[0:32, :]), [pv], [vtokb])
    dma('sp', v_s[:], vtok[:], [vtok], [v_s], vtok, is_output=True)
    pk = P[5]
    mm_acc(pk, pk[0:32, :], lambda k: hTs[:, k, :], lambda k: winb2[:, k, 512:1024], KC, [winb2, hTs])
    op('dve', lambda e: e.tensor_copy(out=ktok[:], in_=pk[0:32, :]), [pk], [ktok])
    dma('sp', k_s[:], ktok[:], [ktok], [k_s], ktok, is_output=True)
    op('pool', lambda e: e.memset(vnew[:], 0.0), [], [vnew])
    for s_ in range(4):
        dma('sp', vnew[0:8, s_, :], vtokb[8 * s_:8 * s_ + 8, :], [vtokb], [vnew], vnew)
    for cc in range(4):
        pa, pg = P[0], P[1]
        mm_acc(pa, pa[:, 0:32], lambda k, cc=cc: winb2[:, k, 1536 + cc * 128:1536 + (cc + 1) * 128], lambda k: hTs[:, k, :], KC, [winb2, hTs])
        mm_acc(pg, pg[:, 0:32], lambda k, cc=cc: winb2[:, k, 2048 + cc * 128:2048 + (cc + 1) * 128], lambda k: hTs[:, k, :], KC, [winb2, hTs])
        op('act', lambda e: e.activation(out=sgs[:], in_=pg[:, 0:32], func=AF.Sigmoid), [pg], [sgs])
        op('dve', lambda e, cc=cc: e.tensor_tensor(out=us_f[:, cc, :], in0=pa[:, 0:32], in1=sgs[:], op=ALU.mult), [pa, sgs], [us_f])
    u_ext = d.sb("u_ext", [128, 4, 4, 38], BF16)
    stc = d.sb("stc", [120, 512], F32)
    dma('sp', stc[:], st_conv.t.rearrange("s t c -> (s t) c"), [st_conv], [stc], stc)
    for cc in range(4):
        pb = P[2 + cc % 2]
        op('pe', lambda e, cc=cc, pb=pb: e.transpose(out=pb[:, 0:120], in_=stc[:, cc * 128:(cc + 1) * 128], identity=ident_f[0:120, 0:120]), [stc, ident_f], [pb])
        op('dve', lambda e, cc=cc, pb=pb: e.tensor_copy(out=u_ext[:, cc, :, 0:30], in_=pb[:, 0:120].rearrange("p (s t) -> p s t", s=4)), [pb], [u_ext])
        op('pool', lambda e, cc=cc: e.tensor_copy(out=u_ext[:, cc, :, 30:38], in_=us_f[:, cc, :].rearrange("p (s t) -> p s t", s=4)), [us_f], [u_ext])
        for s_ in range(4):
            dma('sp', conv_s.t[s_, 22:30, cc * 128:(cc + 1) * 128].rearrange("t p -> p t"), us_f[:, cc, 8 * s_:8 * s_ + 8], [us_f], [conv_s], us_f, is_output=True, allow_slow_non_contiguous=True)
    cps = d.sb("cps", [88, 512], F32)
    for s_ in range(4):
        dma('sp', cps[22 * s_:22 * s_ + 22, :], st_conv.t[s_, 8:30, :], [st_conv], [cps], cps)
    for s_ in range(4):
        dma('sp', conv_s.t[s_, 0:22, :], cps[22 * s_:22 * s_ + 22, :], [cps], [conv_s], cps, is_output=True)
    cvs = d.sb("cvs", [128, 4, 32], F32)
    mixTs = d.sb("mixTs", [128, 8, 32], BF16)
    for cc in range(4):
        pb = P[cc % 2]
        for k in range(31):
            op('pe', lambda e, k=k, cc=cc, pb=pb: e.matmul(pb[:, 0:32], lhsT=dwd[:, k * 4 + cc, :], rhs=u_ext[:, cc, :, k:k + 8], start=(k == 0), stop=(k == 30)), [dwd, u_ext], [pb])
        op('act', lambda e, cc=cc, pb=pb: e.activation(out=cvs[:, cc, :], in_=pb[:, 0:32], func=AF.Identity, bias=col(C_BDW + cc), scale=1.0), [pb, colp], [cvs])
    def outfn_s(cc):
        op('pool', lambda e: e.tensor_copy(out=mixTs[:, 4 + cc, :], in_=cvs[:, cc, :]), [cvs], [mixTs])
    conv_post(cvs, 32, outfn_s, wkf, P[2])

    if STOP == 'S1': return bail()
    Qbd = d.sb("Qbd", [128, 4, 4, 16], BF16)
    op('pool', lambda e: e.memset(Qbd[:], 0.0), [], [Qbd])
    for pc in range(4):
        op('dve', lambda e, pc=pc: e.tensor_copy(out=Qbd[0:64, pc, :, 0:8], in_=qT_n[0:64, pc, :].rearrange("p (s t) -> p s t", s=4)), [qT_n], [Qbd])
        op('dve', lambda e, pc=pc: e.tensor_copy(out=Qbd[64:128, pc, :, 8:16], in_=qT_n[64:128, pc, :].rearrange("p (s t) -> p s t", s=4)), [qT_n], [Qbd])
    negb = d.sb("negb", [1, 256], F32)
    onesrow = d.sb("onesrow", [1, 128], F32)
    dma('sp', negb[:], brow[:], [brow], [negb], negb)
    op('dve', lambda e: e.tensor_scalar(out=negb[:], in0=negb[:], scalar1=-1.0, scalar2=None, op0=ALU.mult), [negb], [negb])
    op('pool', lambda e: e.memset(onesrow[:], 1.0), [], [onesrow])
    knT_ns = d.sb("knT_ns", [128, 4, 4, 128], BF16)
    op('pool', lambda e: e.memset(knT_ns[:], 0.0), [], [knT_ns])
    for s_ in range(4):
        op('dve', lambda e, s_=s_: e.tensor_copy(out=knT_ns[:, s_, :, 0:8], in_=knT_f[:, :, 8 * s_:8 * s_ + 8]), [knT_f], [knT_ns])
    pt_i = d.sb("pt_i", [128, 512], I32)
    ar_f = d.sb("ar_f", [128, 1], F32)
    offs = d.sb("offs", [128, 512], I32)
    dma('sp', pt_i[:], ptab.t.partition_broadcast(128), [ptab], [pt_i], pt_i)
    dma('sp', ar_f[:], arange[:], [arange], [ar_f], ar_f)
    op('dve', lambda e: e.tensor_scalar(out=offs[:], in0=pt_i[:], scalar1=128.0, scalar2=ar_f[:, 0:1], op0=ALU.mult, op1=ALU.add), [pt_i, ar_f], [offs])
    NKP = 3; NVP = 4
    kpg = [[d.sb(f"kpg{i}_{s_}", [128, 512], BF16) for s_ in range(4)] for i in range(NKP)]
    vpg = [[d.sb(f"vpg{i}_{s_}", [128, 512], BF16) for s_ in range(4)] for i in range(NVP)]
    kTp = [d.sb(f"kTp{i}", [128, 4, 4, 128], BF16) for i in range(2)]
    wks = dict(e=[d.sb(f"se_sb{i}", [128, 256], F32) for i in range(3)],
               sp=[d.sb(f"ssp_sb{i}", [128, 256], BF16) for i in range(3)],
               e2=[d.sb(f"se2_sb{i}", [128, 256], F32) for i in range(2)],
               a=[d.sb(f"sa_sb{i}", [128, 256], BF16) for i in range(2)],
               accf=d.sb("saccf", [128, 256], F32),
               accb=[d.sb(f"saccb{i}", [128, 256], BF16) for i in range(3)])
    steps = [dict(mask=Ms)] + [dict(mask=None) for _ in range(128)]
    state = {}
    def prep_pageK(si):
        pg = 128 - si
        i = si % NKP
        for s_ in range(4):
            cidx = s_ * 128 + pg
            dma('pool', kpg[i][s_][:], cache_k.t, [cache_k, offs], [kpg[i][s_]], kpg[i][s_], indirect=bass.IndirectOffsetOnAxis(ap=offs[:, cidx:cidx + 1], axis=0))
    def prep_pageV(si):
        pg = 128 - si
        iv = si % NVP
        for s_ in range(4):
            cidx = s_ * 128 + pg
            dma('pool', vpg[iv][s_][:], cache_v.t, [cache_v, offs], [vpg[iv][s_]], vpg[iv][s_], indirect=bass.IndirectOffsetOnAxis(ap=offs[:, cidx:cidx + 1], axis=0))
    def prep_kT(si):
        i = si % NKP
        kt = kTp[si % 2]
        for s_ in range(4):
            for pc in range(4):
                op('pe', lambda e, s_=s_, pc=pc: e.transpose(out=PT[:, pc * 128:(pc + 1) * 128], in_=kpg[i][s_][:, pc * 128:(pc + 1) * 128], identity=ident_b[:]), [kpg[i][s_], ident_b], [PT])
            op('dve', lambda e, s_=s_: e.tensor_scalar(out=kt[:, s_, :, :], in0=PT[:, 0:512].rearrange("p (c k) -> p c k", c=4), scalar1=-0.125, scalar2=None, op0=ALU.mult), [PT], [kt])
    def zmms_s(si, pb, start, stop_unused):
        kt = knT_ns if si == 0 else kTp[si % 2]
        op('pe', lambda e: e.matmul(pb[:, 0:256], lhsT=onesrow[:], rhs=negb[:], start=start, stop=False), [onesrow, negb], [pb])
        n = 0
        for s_ in range(4):
            for pc in range(4):
                n += 1
                c0 = s_ * 64 + pc * 16
                op('pe', lambda e, s_=s_, pc=pc, c0=c0, n=n: e.matmul(pb[:, c0:c0 + 16], lhsT=kt[:, s_, pc, :], rhs=Qbd[:, pc, s_, :], start=False, stop=(n == 16)), [kt, Qbd], [pb])
    po_s = P[6]
    def avmm_s(si, a_, first, last):
        for s_ in range(4):
            vsrc = vnew[:, s_, :] if si == 0 else vpg[si % NVP][s_][:]
            vb = vnew if si == 0 else vpg[si % NVP][s_]
            for h in range(8):
                c0 = s_ * 64 + h * 8
                op('pe', lambda e, vsrc=vsrc, h=h, c0=c0, s_=s_: e.matmul(po_s[0:64, c0:c0 + 8], lhsT=vsrc[:, h * 64:(h + 1) * 64], rhs=a_[:, c0:c0 + 8], start=(first and s_ == 0 and h == 0), stop=last, skip_group_check=True), [vb, a_], [po_s])
    prep_pageK(1); prep_pageK(2)
    def prepA_s(si):
        if si == 0: return
        if si + 2 <= 128: prep_pageK(si + 2)
        prep_pageV(si)
        prep_kT(si)
    attn_chain(steps, 256, zmms_s, avmm_s, None, wks, prepA=prepA_s)
    osf = d.sb("osf", [64, 256], F32)
    osq = [d.sb("osq", [64, 256], BF16)]
    ortmp = d.sb("ortmp", [64, 256], F32)
    gsb = d.sb("gsb", [64, 8], F32)
    osn = d.sb("osn", [64, 4, 8, 8], BF16)
    dma('sp', gsb[:], gsb_hd[:], [gsb_hd], [gsb], gsb)
    op('act', lambda e: e.copy(out=osf[:], in_=po_s[0:64, 0:256]), [po_s], [osf])
    colstat_rstd([(osf, osf[:, :])], 256, bones[0:64, 0:64], 64, P[0], ortmp, osq)
    op('dve', lambda e: e.tensor_tensor(out=osf[:], in0=osf[:], in1=ortmp[:], op=ALU.mult), [osf, ortmp], [osf])
    osf4 = osf[:, :].rearrange("p (s h t) -> p s h t", s=4, h=8)
    for h_ in range(8):
        op('dve', lambda e, h_=h_: e.tensor_scalar(out=osn[:, :, h_, :], in0=osf4[:, :, h_, :], scalar1=gsb[:, h_:h_ + 1], scalar2=None, op0=ALU.mult), [osf, gsb], [osn])
    for pc in range(4):
        pb = P[1 + pc % 2]
        for s_ in range(4):
            op('pe', lambda e, pc=pc, s_=s_, pb=pb: e.matmul(pb[:, s_ * 8:(s_ + 1) * 8], lhsT=ident_b[0:64, :], rhs=osn[:, s_, 2 * pc, :], start=True, stop=False), [ident_b, osn], [pb])
            op('pe', lambda e, pc=pc, s_=s_, pb=pb: e.matmul(pb[:, s_ * 8:(s_ + 1) * 8], lhsT=shiftI[:], rhs=osn[:, s_, 2 * pc + 1, :], start=False, stop=True), [shiftI, osn], [pb])
        op('dve', lambda e, pc=pc, pb=pb: e.tensor_copy(out=mixTs[:, pc, :], in_=pb[:, 0:32]), [pb], [mixTs])
    mixTs_keep = d.dram("mixTs_d", [128, 8, 32], BF16, OUT if os.environ.get("DBG") else "Internal")
    dma('sp', mixTs_keep[:], mixTs[:], [mixTs], [mixTs_keep], mixTs)
    d.pop()
    d.pop()

    if STOP == 'S2': return bail()
    d.push()
    gts_halo = d.sb("gts_halo", [128, NFC, 8], F32)
    d.push()
    stf = d.sb("stf", [8, DFF], F32)
    dma('sp', stf[:], st_ffn.t.rearrange("s t c -> (s t) c"), [st_ffn], [stf], stf)
    for fc in range(NFC):
        pb = P[fc % 2]
        op('pe', lambda e, fc=fc, pb=pb: e.transpose(out=pb[:, 0:8], in_=stf[:, fc * 128:(fc + 1) * 128], identity=ident_f[0:8, 0:8]), [stf, ident_f], [pb])
        op('dve', lambda e, fc=fc, pb=pb: e.tensor_copy(out=gts_halo[:, fc, :], in_=pb[:, 0:8]), [pb], [gts_halo])
    d.pop()
    woutb = d.sb("woutb", [128, 8, D], BF16)
    for k in range(8):
        dma('pool', woutb[:, k, :], w_out.t[k * 128:(k + 1) * 128, :], [w_out], [woutb], woutb)
    wdnb = d.sb("wdnb", [128, NFC, D], BF16)
    for fc in range(NFC):
        dma('pool', wdnb[:, fc, :], w_down.t[fc * 128:(fc + 1) * 128, :], [w_down], [wdnb], wdnb)
    lnt = [d.sb(f"lnt{i}", [128, D], F32) for i in range(4)]
    for i in range(4):
        dma('sp', lnt[i][:], lnp.t[i:i + 1, :].broadcast_to([128, D]), [lnp], [lnt[i]], lnt[i])
    xt4 = [d.sb(f"xq{i}", [128, D], F32) for i in range(1)]
    tmp4 = d.sb("tmp4", [128, D], F32)
    x1buf = d.sb("x1buf", [128, 4, D], F32)
    hb4 = d.sb("hb4", [128, D], BF16)
    h2T = d.sb("h2T", [128, KC, 512], BF16)
    mix_sb = d.sb("mix_sb", [128, 8, 512], BF16)
    wupg = [d.sb(f"wupg{i}", [128, KC, 128], BF16) for i in range(2)]
    wupv = [d.sb(f"wupv{i}", [128, KC, 128], BF16) for i in range(2)]
    gt_ext = [d.sb(f"gt_ext{i}", [128, 520], F32) for i in range(2)]
    gc = d.sb("gc", [128, 512], F32)
    ge = d.sb("ge", [128, 512], F32)
    fT = d.sb("fT", [128, NFC, 512], BF16)
    ybuf = [d.sb(f"ybuf{i}", [128, D], F32) for i in range(1)]
    stats = d.sb("stats", [128, 12], F32); mv = d.sb("mv", [128, 2], F32); rstd = d.sb("rstd", [128, 1], F32)
    w_up_v = w_up.t.rearrange("(k p) n -> p k n", p=128)
    A2s = d.sb("A2s", [32, D], F32); B2s = d.sb("B2s", [32, D], F32); G1s = d.sb("G1s", [32, D], F32); G2s = d.sb("G2s", [32, D], F32)
    for s_ in range(4):
        load_mod(G1s, 1 + s_, 2048, 8, 8 * s_); load_mod(B2s, 1 + s_, 3072, 8, 8 * s_)
        load_mod(A2s, 1 + s_, 4096, 8, 8 * s_); load_mod(G2s, 1 + s_, 5120, 8, 8 * s_)
    yi = 0

    def token_front(blocks, mixsrc, ncols_total, g1t, a2t, b2t):
        for bi, (ntok, x_ap, xdep, mc0, slot) in enumerate(blocks):
            xq = xt4[0]
            dma('sp', xq[0:ntok, :], x_ap, [xdep], [xq], xq)
            for hf in range(2):
                pb = P[hf]
                mm_acc(pb, pb[0:ntok, :], lambda k: mixsrc[:, k, mc0:mc0 + ntok], lambda k, hf=hf: woutb[:, k, hf * 512:(hf + 1) * 512], 8, [mixsrc, woutb])
                op('dve', lambda e, hf=hf, pb=pb: e.tensor_tensor(out=tmp4[0:ntok, hf * 512:(hf + 1) * 512], in0=pb[0:ntok, :], in1=g1t[0:ntok, hf * 512:(hf + 1) * 512], op=ALU.mult), [pb, g1t], [tmp4])
            op('dve', lambda e: e.scalar_tensor_tensor(out=tmp4[0:ntok, :], in0=xq[0:ntok, :], scalar=ALPHA, in1=tmp4[0:ntok, :], op0=ALU.mult, op1=ALU.add), [xq, tmp4], [tmp4])
            x1 = x1buf
            layernorm_rows_slot(tmp4, ntok, lnt[0], lnt[1], slot)
            modulate_transpose_slot(ntok, slot, a2t, b2t, bi * 128)

    def layernorm_rows_slot(src, ntok, gam, bet, slot):
        for hf in range(2):
            op('dve', lambda e, hf=hf: e.bn_stats(out=stats[0:ntok, hf * 6:(hf + 1) * 6], in_=src[0:ntok, hf * 512:(hf + 1) * 512]), [src], [stats])
        op('dve', lambda e: e.bn_aggr(out=mv[0:ntok, :], in_=stats[0:ntok, :]), [stats], [mv])
        op('dve', lambda e: e.tensor_scalar(out=rstd[0:ntok, :], in0=mv[0:ntok, 1:2], scalar1=EPS, scalar2=None, op0=ALU.add), [mv], [rstd])
        op('pool', lambda e: e.tensor_tensor(out=rstd[0:ntok, :], in0=rstd[0:ntok, :], in1=mhalf[0:ntok, 0:1], op=ALU.pow), [rstd, mhalf], [rstd])
        op('dve', lambda e: e.tensor_scalar(out=x1buf[0:ntok, slot, :], in0=src[0:ntok, :], scalar1=mv[0:ntok, 0:1], scalar2=rstd[0:ntok, 0:1], op0=ALU.subtract, op1=ALU.mult), [src, mv, rstd], [x1buf])
        op('dve', lambda e: e.tensor_tensor(out=x1buf[0:ntok, slot, :], in0=x1buf[0:ntok, slot, :], in1=gam[0:ntok, :], op=ALU.mult), [gam], [x1buf])
        op('dve', lambda e: e.tensor_tensor(out=x1buf[0:ntok, slot, :], in0=x1buf[0:ntok, slot, :], in1=bet[0:ntok, :], op=ALU.add), [bet], [x1buf])

    def modulate_transpose_slot(ntok, slot, a2t, b2t, c0):
        op('dve', lambda e: e.tensor_tensor(out=tmp4[0:ntok, :], in0=x1buf[0:ntok, slot, :], in1=a2t[0:ntok, :], op=ALU.mult), [x1buf, a2t], [tmp4])
        op('dve', lambda e: e.tensor_tensor(out=hb4[0:ntok, :], in0=tmp4[0:ntok, :], in1=b2t[0:ntok, :], op=ALU.add), [tmp4, b2t], [hb4])
        for k in range(KC):
            op('pe', lambda e, k=k: e.transpose(out=PT[:, k * 128:k * 128 + ntok], in_=hb4[0:ntok, k * 128:(k + 1) * 128], identity=ident_b[0:ntok, 0:ntok]), [hb4, ident_b], [PT])
        op('act', lambda e: e.copy(out=h2T[:, :, c0:c0 + ntok], in_=PT[:].rearrange("p (k t) -> p k t", k=KC)[:, :, 0:ntok]), [PT], [h2T])

    def ffn_mid(n, halo_src, nseq, do_val, cs=None):
        tl = n // nseq
        c_lo = 0 if cs is None else cs
        for fc in range(NFC):
            wg = wupg[fc % 2]; wv = wupv[fc % 2]
            dma('pool', wg[:], w_up_v[:, :, DFF + fc * 128:DFF + (fc + 1) * 128], [w_up], [wg], wg)
            if do_val:
                dma('pool', wv[:], w_up_v[:, :, fc * 128:(fc + 1) * 128], [w_up], [wv], wv)
            pg = P[2 + fc % 2]; pvv = P[4 + fc % 2]
            mm_acc(pg, pg[:, 0:n], lambda k: wg[:, k, :], lambda k: h2T[:, k, c_lo:c_lo + n], KC, [wg, h2T])
            if do_val:
                mm_acc(pvv, pvv[:, 0:n], lambda k: wv[:, k, :], lambda k: h2T[:, k, c_lo:c_lo + n], KC, [wv, h2T])
            gx = gt_ext[fc % 2]
            gxv = gx[:, 0:nseq * (tl + 2)].rearrange("p (s t) -> p s t", s=nseq)
            yield ('gate', fc, pg, gx, gxv, pvv)

    def conv3_gelu(fc, gxv, nseq, tl, n, pvv):
        gcv = gc[:, 0:n].rearrange("p (s t) -> p s t", s=nseq)
        op('dve', lambda e: e.tensor_scalar(out=gcv, in0=gxv[:, :, 0:tl], scalar1=col(C_WFDW + 0 * NFC + fc), scalar2=col(C_BFDW + fc), op0=ALU.mult, op1=ALU.add), [gt_ext[fc % 2], colp], [gc])
        for j in (1, 2):
            op('dve', lambda e, j=j: e.scalar_tensor_tensor(out=gcv, in0=gxv[:, :, j:j + tl], scalar=col(C_WFDW + j * NFC + fc), in1=gcv, op0=ALU.mult, op1=ALU.add), [gt_ext[fc % 2], colp, gc], [gc])
        op('act', lambda e: e.activation(out=ge[:, 0:n], in_=gc[:, 0:n], func=AF.Gelu), [gc], [ge])
        op('dve', lambda e: e.tensor_tensor(out=fT[:, fc, 0:n], in0=ge[:, 0:n], in1=pvv[:, 0:n], op=ALU.mult), [ge, pvv], [fT])

    def token_back(blocks, g2t, out_t):
        nonlocal yi
        for bi, (ntok, slot, out_ap) in enumerate(blocks):
            for hf in range(2):
                pb = P[hf]
                mm_acc(pb, pb[0:ntok, :], lambda k: fT[:, k, bi * 128:bi * 128 + ntok], lambda k, hf=hf: wdnb[:, k, hf * 512:(hf + 1) * 512], NFC, [fT, wdnb])
                op('dve', lambda e, hf=hf, pb=pb: e.tensor_tensor(out=tmp4[0:ntok, hf * 512:(hf + 1) * 512], in0=pb[0:ntok, :], in1=g2t[0:ntok, hf * 512:(hf + 1) * 512], op=ALU.mult), [pb, g2t], [tmp4])
            op('dve', lambda e: e.scalar_tensor_tensor(out=tmp4[0:ntok, :], in0=x1buf[0:ntok, slot, :], scalar=ALPHA, in1=tmp4[0:ntok, :], op0=ALU.mult, op1=ALU.add), [x1buf, tmp4], [tmp4])
            yb = ybuf[0]; yi += 1
            layernorm_rows(tmp4, ntok, lnt[2], lnt[3], yb, stats, mv, rstd)
            dma('sp', out_ap, yb[0:ntok, :], [yb], [out_t], yb, is_output=True)

    dma('sp', mix_sb[:, :, 0:128], mixT_d.t[:, :, 0:128].rearrange("c p t -> p c t"), [mixT_d], [mix_sb], mix_sb)
    token_front([(128, xs.t[47 * 128:48 * 128, :], xs, 0, 0)], mix_sb, 128, G1, A2, B2)
    for (kind, fc, pg, gx, gxv, pvv) in ffn_mid(2, None, 1, False, cs=126):
        op('dve', lambda e, fc=fc, pg=pg: e.tensor_scalar(out=gthalo[:, fc, :], in0=pg[:, 0:2], scalar1=hfl[:, 0:1], scalar2=None, op0=ALU.mult), [pg, hfl], [gthalo])
    for sbo in range(4):
        c0 = 128 + sbo * 512
        dma('sp', mix_sb[:], mixT_d.t[:, :, c0:c0 + 512].rearrange("c p t -> p c t"), [mixT_d], [mix_sb], mix_sb)
        blocks = [(128, xs.t[(48 + sbo * 4 + bl) * 128:(49 + sbo * 4 + bl) * 128, :], xs, bl * 128, bl) for bl in range(4)]
        token_front(blocks, mix_sb, 512, G1, A2, B2)
        for (kind, fc, pg, gx, gxv, pvv) in ffn_mid(512, None, 1, True):
            op('pool', lambda e, fc=fc, gx=gx: e.tensor_copy(out=gx[:, 0:2], in_=gthalo[:, fc, :]), [gthalo], [gx])
            op('act', lambda e, pg=pg, gx=gx: e.copy(out=gx[:, 2:514], in_=pg[:, 0:512]), [pg], [gx])
            op('pool', lambda e, fc=fc, gx=gx: e.tensor_copy(out=gthalo[:, fc, :], in_=gx[:, 512:514]), [gx], [gthalo])
            conv3_gelu(fc, gxv, 1, 512, 512, pvv)
        token_back([(128, bl, y_p.t[(sbo * 4 + bl) * 128:(sbo * 4 + bl + 1) * 128, :]) for bl in range(4)], G2, y_p)
    for fc in range(NFC):
        dma('sp', ffn_p.t[:, fc * 128:(fc + 1) * 128].rearrange("t p -> p t"), gthalo[:, fc, :], [gthalo], [ffn_p], gthalo, is_output=True, allow_slow_non_contiguous=True)
    mixTs2 = d.sb("mixTs2", [128, 8, 32], BF16)
    dma('sp', mixTs2[:], mixTs_keep[:], [mixTs_keep], [mixTs2], mixTs2)
    token_front([(32, x_s[:], x_s, 0, 0)], mixTs2, 32, G1s, A2s, B2s)
    gsn = d.sb("gsn", [128, NFC, 8], F32)
    for (kind, fc, pg, gx, gxv, pvv) in ffn_mid(32, None, 4, True):
        op('pool', lambda e, fc=fc, gxv=gxv: e.tensor_copy(out=gxv[:, :, 0:2], in_=gts_halo[:, fc, :].rearrange("p (s t) -> p s t", s=4)), [gts_halo], [gx])
        op('act', lambda e, pg=pg, gxv=gxv: e.copy(out=gxv[:, :, 2:10], in_=pg[:, 0:32].rearrange("p (s t) -> p s t", s=4)), [pg], [gx])
        op('pool', lambda e, fc=fc, gxv=gxv: e.tensor_copy(out=gsn[:, fc, :].rearrange("p (s t) -> p s t", s=4), in_=gxv[:, :, 8:10]), [gx], [gsn])
        conv3_gelu(fc, gxv, 4, 8, 32, pvv)
    token_back([(32, 0, y_s[:])], G2s, y_s)
    for fc in range(NFC):
        for s_ in range(4):
            dma('sp', ffn_s.t[s_, :, fc * 128:(fc + 1) * 128].rearrange("t p -> p t"), gsn[:, fc, 2 * s_:2 * s_ + 2], [gsn], [ffn_s], gsn, is_output=True, allow_slow_non_contiguous=True)
    d.finish()
    d.stacks.pop().close()
    d.stacks.pop().close()
    return nc


_NC = None


def make_in_maps(inp):
    f32 = np.float32
    x_prompt = np.asarray(inp['x_prompt'], f32); x_sample = np.asarray(inp['x_sample'], f32)
    ck = np.ascontiguousarray(np.asarray(inp['cache_k'], f32)[0].reshape(5120 * 128, 512))
    cv = np.ascontiguousarray(np.asarray(inp['cache_v'], f32)[0].reshape(5120 * 128, 512))
    page_table = np.asarray(inp['page_table'], np.int32)
    g = lambda k: np.asarray(inp[k], f32)[0]
    w_dw = g('w_dw'); w_fdw = g('w_fdw')
    def cols(v, n):
        return np.ascontiguousarray(v.reshape(n, 128).T)
    colp = np.concatenate([
        cols(g('gn_sb'), 4), cols(g('gn_conv'), 4), cols(g('cln_g'), 4), cols(g('cln_b'), 4), cols(g('b_dw'), 4),
        np.ascontiguousarray(w_dw.reshape(31, 4, 128).transpose(2, 0, 1).reshape(128, 124)),
        cols(g('b_fdw'), NFC),
        np.ascontiguousarray(w_fdw.reshape(3, NFC, 128).transpose(2, 0, 1).reshape(128, 66)),
    ], axis=1).astype(f32)
    assert colp.shape == (128, NCOLP)
    sbb = g('sb_bias').reshape(1, 8)
    brow = np.ascontiguousarray(np.broadcast_to(sbb.reshape(1, 1, 8, 1), (1, 4, 8, 8)).reshape(1, 256))
    gsb_hd = np.ascontiguousarray(g('gn_sb').reshape(8, 64).T)
    lnp = np.stack([g('ln1_g'), g('ln1_b'), g('ln2_g'), g('ln2_b')]).astype(f32)
    shared = dict(cache_k=ck, cache_v=cv, arange=np.arange(128, dtype=f32).reshape(128, 1),
                  w_ada=g('w_ada'), b_ada=g('b_ada').reshape(1, -1), w_in=g('w_in'), sbbias=sbb, brow=brow,
                  colp=colp, gsb_hd=gsb_hd, w_out=g('w_out'), lnp=lnp, w_up=g('w_up'), w_down=g('w_down'))
    in_maps = []
    for c in range(8):
        b, j = c // 4, c % 4
        nreal = OWN * (j + 1)
        xs = np.zeros((NSLOT, D), f32)
        xs[NSLOT - nreal:] = x_prompt[b, :nreal]
        valid = np.zeros(NSLOT, f32); valid[NSLOT - nreal:] = 1.0
        vmask = np.ascontiguousarray(valid.reshape(NBLK, 128).T)
        hflag = np.full((128, 1), 1.0 if j > 0 else 0.0, f32)
        c5 = np.concatenate([np.asarray(inp['c_prompt'], f32)[b:b + 1], np.asarray(inp['c_sample'], f32)[4 * c:4 * c + 4]], 0)
        c5p = np.zeros((32, D), f32); c5p[0:5] = c5
        c5T = np.ascontiguousarray(c5p.reshape(32, KC, 128).transpose(2, 1, 0))
        m = dict(shared)
        m.update(xs=xs, vmask=vmask, hflag=hflag, c5T=c5T,
                 x_s=np.ascontiguousarray(x_sample[4 * c:4 * c + 4].reshape(32, D)),
                 ptab=np.ascontiguousarray(page_table[4 * c:4 * c + 4].reshape(-1)),
                 st_conv=np.ascontiguousarray(np.asarray(inp['state_conv'], f32)[0, 4 * c:4 * c + 4]),
                 st_ffn=np.ascontiguousarray(np.asarray(inp['state_ffn'], f32)[0, 4 * c:4 * c + 4]))
        in_maps.append(m)
    return in_maps


def kernel(**inp):
    global _NC
    f32 = np.float32
    in_maps = make_in_maps(inp)
    if _NC is None:
        _NC = build_nc()
    res = run_bass_kernel_spmd(_NC, in_maps, core_ids=list(range(8)))
    R = res.results
    y_prompt = np.zeros((2, 8192, D), f32); k_prompt = np.zeros((1, 2, 8192, 8, 64), f32); v_prompt = np.zeros_like(k_prompt)
    conv_prompt = np.zeros((1, 2, 30, 512), f32); ffn_prompt = np.zeros((1, 2, 2, DFF), f32)
    y_sample = np.zeros((32, 8, D), f32); k_sample = np.zeros((1, 32, 8, 8, 64), f32); v_sample = np.zeros_like(k_sample)
    conv_sample = np.zeros((1, 32, 30, 512), f32); ffn_sample = np.zeros((1, 32, 2, DFF), f32)
    for c in range(8):
        b, j = c // 4, c % 4
        r = R[c]
        y_prompt[b, OWN * j:OWN * (j + 1)] = r['y_p']
        k_prompt[0, b, OWN * j:OWN * (j + 1)] = r['k_p'].reshape(OWN, 8, 64)
        v_prompt[0, b, OWN * j:OWN * (j + 1)] = r['v_p'].reshape(OWN, 8, 64)
        if j == 3:
            conv_prompt[0, b] = r['conv_p']; ffn_prompt[0, b] = r['ffn_p']
        y_sample[4 * c:4 * c + 4] = r['y_s'].reshape(4, 8, D)
        k_sample[0, 4 * c:4 * c + 4] = r['k_s'].reshape(4, 8, 8, 64)
        v_sample[0, 4 * c:4 * c + 4] = r['v_s'].reshape(4, 8, 8, 64)
        conv_sample[0, 4 * c:4 * c + 4] = r['conv_s']; ffn_sample[0, 4 * c:4 * c + 4] = r['ffn_s']
    return (y_prompt, y_sample, k_prompt, v_prompt, conv_prompt, ffn_prompt, k_sample, v_sample, conv_sample, ffn_sample)
```

```python
import numpy as np
import os
from contextlib import ExitStack
import concourse.bass as bass
import concourse.mybir as mybir
from concourse.bass_utils import run_bass_kernel_spmd

F32 = mybir.dt.float32; BF16 = mybir.dt.bfloat16; I32 = mybir.dt.int32
ALU = mybir.AluOpType; AF = mybir.ActivationFunctionType

D = 1024; KC = 8; NSLOT = 8192; NBLK = 64; OWN = 2048; DFF = 2816; NFC = 22
ALPHA = 2.0 ** 0.25; EPS = 1e-5
NQ = 128 + OWN
UOFF = 32
NCOLP = 4 * 5 + 124 + 22 + 66


class Buf:
    def __init__(self, t, name):
        self.t = t; self.name = name
        self.w = None; self.r = []
        self.dsem = None; self.dcount = 0
    def __getitem__(self, idx):
        return self.t[idx]


class Dep:
    def __init__(self, nc):
        self.nc = nc
        self.stacks = [ExitStack()]
        self.eng = {'pe': nc.tensor, 'act': nc.scalar, 'dve': nc.vector, 'pool': nc.gpsimd, 'sp': nc.sync}
        self.sem = {k: nc.alloc_semaphore(name=f"sem_{k}") for k in self.eng}
        self.cnt = {k: 0 for k in self.eng}
        self.seen = {k: {} for k in self.eng}
        self.out_tokens = []
        self.dsems = []
        self.dsem_pool = []
        self.scope_bufs = [[]]
        self.retired = {}

    def push(self):
        self.stacks.append(ExitStack())
        self.scope_bufs.append([])
    def pop(self):
        self.barrier()
        for b in self.scope_bufs.pop():
            if b.dsem is not None:
                self.dsem_pool.append((b.dsem, b.dcount))
                self.dsems = [o for o in self.dsems if o is not b]
                self.retired[id(b.dsem)] = (b.dsem, b.dcount)
                b.dsem = None
        self.stacks.pop().close()
    def sb(self, name, shape, dt):
        b = Buf(self.stacks[-1].enter_context(self.nc.sbuf_tensor(name, shape, dt)), name)
        self.scope_bufs[-1].append(b)
        return b
    def ps(self, name, shape, dt):
        b = Buf(self.stacks[-1].enter_context(self.nc.psum_tensor(name, shape, dt)), name)
        b.psum = True
        return b
    def dram(self, name, shape, dt, kind="Internal"):
        return Buf(self.nc.dram_tensor(name, shape, dt, kind=kind).ap(), name)

    def _wait(self, e, s, v):
        k = id(s)
        if self.seen[e].get(k, 0) >= v: return
        self.eng[e].wait_ge(s, v)
        self.seen[e][k] = v

    def _waits(self, e, reads, writes):
        need = {}
        def add(tok):
            if tok is None: return
            s, v = tok
            k = id(s)
            if k not in need or need[k][1] < v: need[k] = (s, v)
        for b in reads: add(b.w)
        for b in writes:
            add(b.w)
            for t in b.r: add(t)
        for k, (s, v) in need.items():
            self._wait(e, s, v)

    def _record(self, tok, reads, writes):
        for b in reads:
            b.r.append(tok)
            if len(b.r) > 24:
                m = {}
                for s, v in b.r:
                    if id(s) not in m or m[id(s)][1] < v: m[id(s)] = (s, v)
                b.r = list(m.values())
        for b in writes:
            b.w = tok; b.r = []

    def op(self, e, fn, reads=(), writes=()):
        pr = [b for b in reads if getattr(b, 'psum', False)]
        if pr:
            writes = list(writes) + [b for b in pr if b not in writes]
            reads = [b for b in reads if not getattr(b, 'psum', False)]
        self._waits(e, reads, writes)
        inst = fn(self.eng[e])
        self.cnt[e] += 1
        inst.then_inc(self.sem[e], 1)
        self._record((self.sem[e], self.cnt[e]), reads, writes)
        return inst

    def dma(self, q, out, in_, reads, writes, owner, is_output=False, indirect=None, **kw):
        self._waits(q, reads, writes)
        eng = self.eng[q]
        if indirect is not None:
            inst = eng.indirect_dma_start(out=out, out_offset=None, in_=in_, in_offset=indirect, **kw)
        else:
            inst = eng.dma_start(out=out, in_=in_, **kw)
        if owner.dsem is None:
            if self.dsem_pool:
                owner.dsem, owner.dcount = self.dsem_pool.pop()
                self.retired.pop(id(owner.dsem), None)
            else:
                owner.dsem = self.nc.alloc_semaphore(name=f"dsem_{owner.name}")
            self.dsems.append(owner)
        owner.dcount += 16
        inst.then_inc(owner.dsem, 16)
        tok = (owner.dsem, owner.dcount)
        self._record(tok, reads, writes)
        if is_output: self.out_tokens.append(tok)
        return inst

    def barrier(self):
        for e in self.eng:
            for e2 in self.eng:
                if self.cnt[e2] > 0: self._wait(e, self.sem[e2], self.cnt[e2])
            for o in self.dsems:
                self._wait(e, o.dsem, o.dcount)
            for (sm, v) in self.retired.values():
                self._wait(e, sm, v)

    def finish(self):
        self.barrier()


def build_nc(NPOOL=5120, STOP=None):
    nc = bass.Bass("TRN2", target_bir_lowering=False)
    d = Dep(nc)
    op = d.op; dma = d.dma
    def bail():
        d.finish()
        while d.stacks: d.stacks.pop().close()
        return nc
    IN = "ExternalInput"; OUT = "ExternalOutput"
    xs = d.dram("xs", [NSLOT, D], F32, IN)
    vmask = d.dram("vmask", [128, NBLK], F32, IN)
    hflag = d.dram("hflag", [128, 1], F32, IN)
    c5T = d.dram("c5T", [128, KC, 32], F32, IN)
    x_s = d.dram("x_s", [32, D], F32, IN)
    cache_k = d.dram("cache_k", [NPOOL * 128, 512], F32, IN)
    cache_v = d.dram("cache_v", [NPOOL * 128, 512], F32, IN)
    ptab = d.dram("ptab", [4 * 128], I32, IN)
    arange = d.dram("arange", [128, 1], F32, IN)
    st_conv = d.dram("st_conv", [4, 30, 512], F32, IN)
    st_ffn = d.dram("st_ffn", [4, 2, DFF], F32, IN)
    w_ada = d.dram("w_ada", [D, 6 * D], F32, IN)
    b_ada = d.dram("b_ada", [1, 6 * D], F32, IN)
    w_in = d.dram("w_in", [D, 2560], F32, IN)
    sbbias = d.dram("sbbias", [1, 8], F32, IN)
    brow = d.dram("brow", [1, 256], F32, IN)
    colp_d = d.dram("colp", [128, NCOLP], F32, IN)
    gsb_hd = d.dram("gsb_hd", [64, 8], F32, IN)
    w_out = d.dram("w_out", [D, D], F32, IN)
    lnp = d.dram("lnp", [4, D], F32, IN)
    w_up = d.dram("w_up", [D, 2 * DFF], F32, IN)
    w_down = d.dram("w_down", [DFF, D], F32, IN)

    y_p = d.dram("y_p", [OWN, D], F32, OUT)
    k_p = d.dram("k_p", [OWN, 512], F32, OUT)
    v_p = d.dram("v_p", [OWN, 512], F32, OUT)
    conv_p = d.dram("conv_p", [30, 512], F32, OUT)
    ffn_p = d.dram("ffn_p", [2, DFF], F32, OUT)
    y_s = d.dram("y_s", [32, D], F32, OUT)
    k_s = d.dram("k_s", [32, 512], F32, OUT)
    v_s = d.dram("v_s", [32, 512], F32, OUT)
    conv_s = d.dram("conv_s", [4, 30, 512], F32, OUT)
    ffn_s = d.dram("ffn_s", [4, 2, DFF], F32, OUT)

    mod_d = d.dram("mod_d", [5, 6 * D], F32)
    knT_d = d.dram("knT_d", [4, 128, NSLOT], BF16)
    v_d = d.dram("v_d", [NSLOT, 512], BF16)
    qT_d = d.dram("qT_d", [4, 128, NQ], BF16)
    mixT_d = d.dram("mixT_d", [8, 128, NQ], BF16)

    P = [d.ps(f"pb{i}", [128, 512], F32) for i in range(7)]
    PT = d.ps("ptr", [128, 1024], BF16)

    ones_f = d.sb("ones_f", [128, 128], F32)
    ident_f = d.sb("ident_f", [128, 128], F32)
    ident_b = d.sb("ident_b", [128, 128], BF16)
    tri_b = d.sb("tri_b", [128, 128], BF16)
    ones_b = d.sb("ones_b", [128, 128], BF16)
    m512 = d.sb("m512", [128, 128], F32)
    bones = d.sb("bones", [128, 128], BF16)
    shiftI = d.sb("shiftI", [64, 128], BF16)
    Mi = [d.sb(f"Mi{i}", [128, 512], BF16) for i in range(4)]
    Ms = d.sb("Ms", [128, 256], BF16)
    ones512 = d.sb("ones512", [128, 512], BF16)
    mhalf = d.sb("mhalf", [128, 512], F32)
    colp = d.sb("colp_sb", [128, NCOLP], F32)
    sbb = d.sb("sbb", [128, 8], F32)
    vm = d.sb("vm", [128, NBLK], F32)
    hfl = d.sb("hfl", [128, 1], F32)
    G1 = d.sb("G1", [128, D], F32); A2 = d.sb("A2", [128, D], F32)
    B2 = d.sb("B2", [128, D], F32); G2 = d.sb("G2", [128, D], F32)
    gthalo = d.sb("gthalo", [128, NFC, 2], F32)

    op('pool', lambda e: e.memset(ones_f[:], 1.0), [], [ones_f])
    op('pool', lambda e: e.memset(ones_b[:], 1.0), [], [ones_b])
    op('pool', lambda e: e.memset(ones512[:], 1.0), [], [ones512])
    op('pool', lambda e: e.memset(m512[:], 1.0 / 512.0), [], [m512])
    op('pool', lambda e: e.memset(mhalf[:], -0.5), [], [mhalf])
    op('pool', lambda e: e.memset(bones[:], 0.0), [], [bones])
    op('pool', lambda e: e.memset(bones[0:64, 0:64], 1.0 / 64.0), [], [bones])
    op('pool', lambda e: e.memset(bones[64:128, 64:128], 1.0 / 64.0), [], [bones])
    op('pool', lambda e: e.affine_select(out=ident_f[:], in_=ones_f[:], pattern=[[-1, 128]], compare_op=ALU.is_equal, fill=0.0, base=0, channel_multiplier=1), [ones_f], [ident_f])
    op('pool', lambda e: e.tensor_copy(out=ident_b[:], in_=ident_f[:]), [ident_f], [ident_b])
    op('pool', lambda e: e.affine_select(out=tri_b[:], in_=ones_b[:], pattern=[[-1, 128]], compare_op=ALU.is_ge, fill=0.0, base=0, channel_multiplier=1), [ones_b], [tri_b])
    op('pool', lambda e: e.affine_select(out=shiftI[:], in_=ones_b[0:64, :], pattern=[[-1, 128]], compare_op=ALU.is_equal, fill=0.0, base=64, channel_multiplier=1), [ones_b], [shiftI])
    for i in range(4):
        op('pool', lambda e, i=i: e.affine_select(out=Mi[i][:], in_=ones512[:], pattern=[[1, 512]], compare_op=ALU.is_gt, fill=0.0, base=-128 * i, channel_multiplier=-1), [ones512], [Mi[i]])
    op('pool', lambda e: e.affine_select(out=Ms[:], in_=ones512[:, 0:256], pattern=[[0, 32], [1, 8]], compare_op=ALU.is_gt, fill=0.0, base=0, channel_multiplier=-1), [ones512], [Ms])
    dma('sp', colp[:], colp_d[:], [colp_d], [colp], colp)
    dma('sp', sbb[:], sbbias.t.broadcast_to([128, 8]), [sbbias], [sbb], sbb)
    dma('sp', vm[:], vmask[:], [vmask], [vm], vm)
    dma('sp', hfl[:], hflag[:], [hflag], [hfl], hfl)
    C_GNSB, C_GNCV, C_CLNG, C_CLNB, C_BDW, C_WDW, C_BFDW, C_WFDW = 0, 4, 8, 12, 16, 20, 144, 166
    def col(c): return colp[:, c:c + 1]

    if STOP == 'C': return bail()
    d.push()
    cT = d.sb("cT", [128, KC * 32], F32)
    sT = d.sb("sT", [128, KC * 32], F32)
    e5 = d.sb("e5", [128, KC * 32], F32)
    modsb = d.sb("modsb", [5, 6 * D], F32)
    bada = d.sb("bada", [5, 6 * D], F32)
    wa = [d.sb(f"wa{i}", [128, KC, 512], F32) for i in range(2)]
    dma('sp', cT[:], c5T.t.rearrange("p k r -> p (k r)"), [c5T], [cT], cT)
    dma('sp', bada[:], b_ada.t.broadcast_to([5, 6 * D]), [b_ada], [bada], bada)
    op('act', lambda e: e.activation(out=e5[:], in_=cT[:], func=AF.Exp, scale=-1.0), [cT], [e5])
    op('dve', lambda e: e.tensor_scalar(out=e5[:], in0=e5[:], scalar1=1.0, scalar2=None, op0=ALU.add), [e5], [e5])
    op('dve', lambda e: e.reciprocal(out=e5[:], in_=e5[:]), [e5], [e5])
    op('dve', lambda e: e.tensor_tensor(out=sT[:], in0=cT[:], in1=e5[:], op=ALU.mult), [cT, e5], [sT])
    w_ada_v = w_ada.t.rearrange("(k p) n -> p k n", p=128)
    for n in range(12):
        w = wa[n % 2]
        dma('sp', w[:], w_ada_v[:, :, n * 512:(n + 1) * 512], [w_ada], [w], w)
        pb = P[n % 2]
        for k in range(KC):
            op('pe', lambda e, k=k, w=w, pb=pb: e.matmul(pb[0:32, :], lhsT=sT[:, k * 32:(k + 1) * 32], rhs=w[:, k, :], start=(k == 0), stop=(k == KC - 1)), [sT, w], [pb])
        op('dve', lambda e, n=n, pb=pb: e.tensor_tensor(out=modsb[:, n * 512:(n + 1) * 512], in0=pb[0:5, :], in1=bada[:, n * 512:(n + 1) * 512], op=ALU.add), [pb, bada], [modsb])
    for a, b_ in ((1024, 3072), (4096, 6144)):
        op('dve', lambda e, a=a, b_=b_: e.tensor_scalar(out=modsb[:, a:b_], in0=modsb[:, a:b_], scalar1=1.0, scalar2=None, op0=ALU.add), [modsb], [modsb])
    dma('sp', mod_d[:], modsb[:], [modsb], [mod_d], modsb)
    d.pop()

    if STOP == 'A': return bail()
    def load_mod(tile, row, lo, npart=128, p0=0):
        dma('sp', tile[p0:p0 + npart, :], mod_d.t[row:row + 1, lo:lo + D].broadcast_to([npart, D]), [mod_d], [tile], tile)
    load_mod(G1, 0, 2048); load_mod(B2, 0, 3072); load_mod(A2, 0, 4096); load_mod(G2, 0, 5120)

    def modulate_transpose(xt, ntok, A, B, vcol, hb, t1, hT, c0):
        op('dve', lambda e: e.tensor_tensor(out=t1[0:ntok, :], in0=xt[0:ntok, :], in1=A[0:ntok, :], op=ALU.mult), [xt, A], [t1])
        if vcol is not None:
            op('dve', lambda e: e.scalar_tensor_tensor(out=hb[0:ntok, :], in0=B[0:ntok, :], scalar=vcol, in1=t1[0:ntok, :], op0=ALU.mult, op1=ALU.add), [B, t1, vm], [hb])
        else:
            op('dve', lambda e: e.tensor_tensor(out=hb[0:ntok, :], in0=t1[0:ntok, :], in1=B[0:ntok, :], op=ALU.add), [B, t1], [hb])
        for k in range(KC):
            op('pe', lambda e, k=k: e.transpose(out=PT[:, k * 128:k * 128 + ntok], in_=hb[0:ntok, k * 128:(k + 1) * 128], identity=ident_b[0:ntok, 0:ntok]), [hb, ident_b], [PT])
        op('act', lambda e: e.copy(out=hT[:, :, c0:c0 + ntok], in_=PT[:].rearrange("p (k t) -> p k t", k=KC)[:, :, 0:ntok]), [PT], [hT])

    def mm_acc(pb, pslice, lhs_fn, rhs_fn, nk, reads):
        for k in range(nk):
            op('pe', lambda e, k=k: e.matmul(pslice, lhsT=lhs_fn(k), rhs=rhs_fn(k), start=(k == 0), stop=(k == nk - 1)), reads, [pb])

    def layernorm_rows(src, ntok, gam, bet, dst, stats, mv, rstd):
        for hf in range(2):
            op('dve', lambda e, hf=hf: e.bn_stats(out=stats[0:ntok, hf * 6:(hf + 1) * 6], in_=src[0:ntok, hf * 512:(hf + 1) * 512]), [src], [stats])
        op('dve', lambda e: e.bn_aggr(out=mv[0:ntok, :], in_=stats[0:ntok, :]), [stats], [mv])
        op('dve', lambda e: e.tensor_scalar(out=rstd[0:ntok, :], in0=mv[0:ntok, 1:2], scalar1=EPS, scalar2=None, op0=ALU.add), [mv], [rstd])
        op('pool', lambda e: e.tensor_tensor(out=rstd[0:ntok, :], in0=rstd[0:ntok, :], in1=mhalf[0:ntok, 0:1], op=ALU.pow), [rstd, mhalf], [rstd])
        op('dve', lambda e: e.tensor_scalar(out=dst[0:ntok, :], in0=src[0:ntok, :], scalar1=mv[0:ntok, 0:1], scalar2=rstd[0:ntok, 0:1], op0=ALU.subtract, op1=ALU.mult), [src, mv, rstd], [dst])
        op('dve', lambda e: e.tensor_tensor(out=dst[0:ntok, :], in0=dst[0:ntok, :], in1=gam[0:ntok, :], op=ALU.mult), [dst, gam], [dst])
        op('dve', lambda e: e.tensor_tensor(out=dst[0:ntok, :], in0=dst[0:ntok, :], in1=bet[0:ntok, :], op=ALU.add), [dst, bet], [dst])

    def colstat_rstd(srcs, n, lhs, npart, pb, tmp, wk):
        for i, (b, ap) in enumerate(srcs):
            op('dve', lambda e, ap=ap, i=i: e.tensor_tensor(out=wk[i][0:npart, 0:n], in0=ap, in1=ap, op=ALU.mult), [b], [wk[i]])
        for i in range(len(srcs)):
            op('pe', lambda e, i=i: e.matmul(pb[0:npart, 0:n], lhsT=lhs, rhs=wk[i][0:npart, 0:n], start=(i == 0), stop=(i == len(srcs) - 1)), [wk[i], m512, bones], [pb])
        op('dve', lambda e: e.tensor_scalar(out=tmp[0:npart, 0:n], in0=pb[0:npart, 0:n], scalar1=EPS, scalar2=None, op0=ALU.add), [pb], [tmp])
        op('act', lambda e: e.activation(out=tmp[0:npart, 0:n], in_=tmp[0:npart, 0:n], func=AF.Ln), [tmp], [tmp])
        op('act', lambda e: e.activation(out=tmp[0:npart, 0:n], in_=tmp[0:npart, 0:n], func=AF.Exp, scale=-0.5), [tmp], [tmp])

    d.push()
    dwd = d.sb("dwd", [128, 124, 128], BF16)
    for i in range(124):
        op('dve' if i % 2 else 'pool', lambda e, i=i: e.tensor_scalar(out=dwd[:, i, :], in0=ident_f[:], scalar1=col(C_WDW + i), scalar2=None, op0=ALU.mult), [ident_f, colp], [dwd])
    wkf = dict(sq=[d.sb(f"csq{i}", [128, 512], F32) for i in range(4)], tmp=d.sb("ctmp", [128, 512], F32), sg=d.sb("csg", [128, 512], F32))
    d.push()
    uT_b = d.sb("uT_b", [128, 4, UOFF + NQ], BF16)
    op('pool', lambda e: e.memset(uT_b[:], 0.0), [], [uT_b])
    d.push()
    A1 = d.sb("A1", [128, D], F32); B1 = d.sb("B1", [128, D], F32)
    load_mod(B1, 0, 0); load_mod(A1, 0, 1024)
    winb = d.sb("winb", [128, KC, 2560], BF16)
    for k in range(KC):
        dma('pool', winb[:, k, :], w_in.t[k * 128:(k + 1) * 128, :], [w_in], [winb], winb)
    xt = [d.sb(f"xt{i}", [128, D], F32) for i in range(3)]
    t1 = d.sb("t1", [128, D], F32)
    hb = [d.sb(f"hb{i}", [128, D], BF16) for i in range(2)]
    hT = [d.sb(f"hT{i}", [128, KC, 512], BF16) for i in range(2)]
    kn_sb = [d.sb(f"kn_sb{i}", [128, 512], BF16) for i in range(2)]
    q_sb = [d.sb(f"q_sb{i}", [128, 512], BF16) for i in range(2)]
    v_sb = [d.sb(f"v_sb{i}", [128, 512], BF16) for i in range(2)]
    vf_sb = [d.sb(f"vf_sb{i}", [128, 512], F32) for i in range(2)]
    kf_sb = [d.sb(f"kf_sb{i}", [128, 512], F32) for i in range(2)]
    sig = d.sb("sig", [128, 512], F32)
    uf = [d.sb(f"uf{i}", [128, 512], F32) for i in range(4)]

    def proj_group(hTg, ncol, sbi, is_own, is_halo, c_lo):
        cs = slice(c_lo, c_lo + ncol)
        for pc in range(4):
            pb = P[pc % 2]
            mm_acc(pb, pb[:, 0:ncol], lambda k, pc=pc: winb[:, k, 512 + pc * 128:512 + (pc + 1) * 128], lambda k: hTg[:, k, cs], KC, [winb, hTg])
            ks = kn_sb[pc % 2]
            op('act', lambda e, pb=pb, ks=ks: e.activation(out=ks[:, 0:ncol], in_=pb[:, 0:ncol], func=AF.Identity, scale=-0.125), [pb], [ks])
            yield ('kn', pc, ks)
        if is_own or is_halo:
            for pc in range(0 if os.environ.get('SKIP_Q') else 4):
                pb = P[2 + pc % 2]
                mm_acc(pb, pb[:, 0:ncol], lambda k, pc=pc: winb[:, k, pc * 128:(pc + 1) * 128], lambda k: hTg[:, k, cs], KC, [winb, hTg])
                qs = q_sb[pc % 2]
                op('act' if os.environ.get('E1') else 'dve', lambda e, pb=pb, qs=qs: (e.copy if os.environ.get('E1') else e.tensor_copy)(out=qs[:, 0:ncol], in_=pb[:, 0:ncol]), [pb], [qs])
                yield ('q', pc, qs)
            for cc in range(0 if os.environ.get('SKIP_U') else 4):
                pa, pg = P[4], P[5]
                mm_acc(pa, pa[:, 0:ncol], lambda k, cc=cc: winb[:, k, 1536 + cc * 128:1536 + (cc + 1) * 128], lambda k: hTg[:, k, cs], KC, [winb, hTg])
                mm_acc(pg, pg[:, 0:ncol], lambda k, cc=cc: winb[:, k, 2048 + cc * 128:2048 + (cc + 1) * 128], lambda k: hTg[:, k, cs], KC, [winb, hTg])
                op('act', lambda e: e.activation(out=sig[:, 0:ncol], in_=pg[:, 0:ncol], func=AF.Sigmoid), [pg], [sig])
                op('dve', lambda e, cc=cc: e.tensor_tensor(out=uf[cc][:, 0:ncol], in0=pa[:, 0:ncol], in1=sig[:, 0:ncol], op=ALU.mult), [pa, sig], [uf[cc]])
                yield ('u', cc, uf[cc])

    xi = 0
    for sbi in range(16):
        if STOP == 'P1c' and sbi >= 1: break
        if STOP == 'P1d' and sbi not in (0, 11): continue
        if STOP == 'P1e' and sbi not in (0, 12): continue
        if STOP == 'P1f' and sbi not in (0, 15): continue
        is_own = sbi >= 12
        is_halo_sb = sbi == 11
        hTg = hT[sbi % 2]
        for bl in range(4):
            blk = sbi * 4 + bl
            x_t = xt[xi % 3]; xi += 1
            dma('sp', x_t[:], xs.t[blk * 128:(blk + 1) * 128, :], [xs], [x_t], x_t)
            modulate_transpose(x_t, 128, A1, B1, vm[:, blk:blk + 1], hb[blk % 2], t1, hTg, bl * 128)
        if STOP == 'P1a': break
        for bl in range(4):
            blk = sbi * 4 + bl
            pb = P[6]
            mm_acc(pb, pb[:, :], lambda k, bl=bl: hTg[:, k, bl * 128:(bl + 1) * 128], lambda k: winb[:, k, 1024:1536], KC, [winb, hTg])
            vs_ = v_sb[blk % 2]
            op('act', lambda e, vs_=vs_: e.copy(out=vs_[:], in_=pb[:]), [pb], [vs_])
            dma('sp', v_d.t[blk * 128:(blk + 1) * 128, :], vs_[:], [vs_], [v_d], vs_)
            if is_own and not os.environ.get('SKIP_OWNOUT'):
                vf = vf_sb[blk % 2]
                op('dve', lambda e, vf=vf: e.tensor_copy(out=vf[:], in_=pb[:]), [pb], [vf])
                r0 = (blk - 48) * 128
                dma('sp', v_p.t[r0:r0 + 128, :], vf[:], [vf], [v_p], vf, is_output=True)
                pk = P[3]
                mm_acc(pk, pk[:, :], lambda k, bl=bl: hTg[:, k, bl * 128:(bl + 1) * 128], lambda k: winb[:, k, 512:1024], KC, [winb, hTg])
                kf = kf_sb[blk % 2]
                op('dve', lambda e, kf=kf: e.tensor_copy(out=kf[:], in_=pk[:]), [pk], [kf])
                dma('sp', k_p.t[r0:r0 + 128, :], kf[:], [kf], [k_p], kf, is_output=True)
        if STOP == 'P1b': break
        if is_halo_sb:
            groups = [(512, 0, False, False), (128, 384, False, True)]
        else:
            groups = [(512, 0, is_own and not os.environ.get('SKIP_OWNPROJ'), False)]
        for gi, (ncol, c_lo, own_, halo_) in enumerate(groups):
            only_qu = (gi == 1)
            for kind, idx, buf in proj_group(hTg, ncol, sbi, own_, halo_, c_lo):
                if kind == 'kn':
                    if only_qu: continue
                    dma('sp', knT_d.t[idx, :, sbi * 512:sbi * 512 + ncol], buf[:, 0:ncol], [buf], [knT_d], buf)
                elif kind == 'q':
                    qc0 = 0 if halo_ else 128 + (sbi - 12) * 512
                    dma('sp', qT_d.t[idx, :, qc0:qc0 + ncol], buf[:, 0:ncol], [buf], [qT_d], buf)
                else:
                    uc0 = UOFF + (0 if halo_ else 128 + (sbi - 12) * 512)
                    op('dve' if os.environ.get('E2') else 'pool', lambda e, idx=idx, buf=buf, uc0=uc0, ncol=ncol: e.tensor_copy(out=uT_b[:, idx, uc0:uc0 + ncol], in_=buf[:, 0:ncol]), [buf], [uT_b])
                    if sbi == 15:
                        dma('sp', conv_p.t[:, idx * 128:(idx + 1) * 128].rearrange("t p -> p t"), buf[:, 482:512], [buf], [conv_p], buf, is_output=True, allow_slow_non_contiguous=True)
    d.pop()

    if STOP in ('P1', 'P1a', 'P1b', 'P1c', 'P1d', 'P1e', 'P1f'): return bail()
    def attn_chain(steps, nq, zmms, avmm, bias_ap, wk, prepA=None, npy=2):
        e_sb, sp_sb, e2_sb, a_sb, accf, accb = wk['e'], wk['sp'], wk['e2'], wk['a'], wk['accf'], wk['accb']
        ns = len(steps)
        def stageA(si):
            if prepA is not None: prepA(si)
            pz = P[si % 2]
            zmms(si, pz, True, True)
            es_ = e_sb[si % 3]; sp_ = sp_sb[si % 3]
            if bias_ap is not None:
                op('act', lambda e: e.activation(out=es_[:, 0:nq], in_=pz[:, 0:nq], func=AF.Exp, scale=-1.0, bias=bias_ap), [pz, sbb], [es_])
            else:
                op('act', lambda e: e.activation(out=es_[:, 0:nq], in_=pz[:, 0:nq], func=AF.Exp, scale=-1.0), [pz], [es_])
            op('act', lambda e: e.activation(out=sp_[:, 0:nq], in_=es_[:, 0:nq], func=AF.Ln, bias=1.0, scale=1.0), [es_], [sp_])
            mk = steps[si]['mask']
            if mk is not None:
                op('pool', lambda e: e.tensor_tensor(out=sp_[:, 0:nq], in0=sp_[:, 0:nq], in1=mk[:, 0:nq], op=ALU.mult), [sp_, mk], [sp_])
            if si < ns - 1:
                if si == 0:
                    op('pool', lambda e: e.tensor_copy(out=accf[:, 0:nq], in_=sp_[:, 0:nq]), [sp_], [accf])
                else:
                    op('pool', lambda e: e.tensor_tensor(out=accf[:, 0:nq], in0=accf[:, 0:nq], in1=sp_[:, 0:nq], op=ALU.add), [accf, sp_], [accf])
                ab2 = accb[si % 3]
                op('dve', lambda e: e.tensor_copy(out=ab2[:, 0:nq], in_=accf[:, 0:nq]), [accf], [ab2])
        def stageB1(si):
            first = si == 0
            py = P[2 + si % npy]
            es_ = e_sb[si % 3]; sp_ = sp_sb[si % 3]; e2_ = e2_sb[si % 2]; a_ = a_sb[si % 2]
            op('pe', lambda e: e.matmul(py[:, 0:nq], lhsT=tri_b[:], rhs=sp_[:, 0:nq], start=True, stop=first), [tri_b, sp_], [py])
            if not first:
                ab = accb[(si - 1) % 3]
                op('pe', lambda e: e.matmul(py[:, 0:nq], lhsT=ones_b[:], rhs=ab[:, 0:nq], start=False, stop=True), [ones_b, ab], [py])
            op('act', lambda e: e.activation(out=e2_[:, 0:nq], in_=py[:, 0:nq], func=AF.Exp, scale=-1.0), [py], [e2_])
            op('dve', lambda e: e.tensor_tensor(out=a_[:, 0:nq], in0=es_[:, 0:nq], in1=e2_[:, 0:nq], op=ALU.mult), [es_, e2_], [a_])
            mk = steps[si]['mask']
            if mk is not None:
                op('dve', lambda e: e.tensor_tensor(out=a_[:, 0:nq], in0=a_[:, 0:nq], in1=mk[:, 0:nq], op=ALU.mult), [a_, mk], [a_])
        def stageB2(si):
            avmm(si, a_sb[si % 2], si == 0, si == ns - 1)
        stageA(0)
        if ns > 1: stageA(1)
        for si in range(ns):
            stageB1(si)
            if si + 2 < ns: stageA(si + 2)
            if si >= 1: stageB2(si - 1)
        stageB2(ns - 1)

    d.push()
    knT2 = [d.sb(f"knT{i}", [128, NSLOT], BF16) for i in range(2)]
    vpair2 = [d.sb(f"vpair{i}", [128, NBLK, 128], BF16) for i in range(2)]
    qT2 = [d.sb(f"qT{i}", [128, NQ], BF16) for i in range(2)]
    wk = dict(e=[d.sb(f"e_sb{i}", [128, 512], F32) for i in range(3)],
              sp=[d.sb(f"sp_sb{i}", [128, 512], BF16) for i in range(3)],
              e2=[d.sb(f"e2_sb{i}", [128, 512], F32) for i in range(2)],
              a=[d.sb(f"a_sb{i}", [128, 512], BF16) for i in range(2)],
              accf=d.sb("accf", [128, 512], F32),
              accb=[d.sb(f"accb{i}", [128, 512], BF16) for i in range(3)])
    opair = d.sb("opair", [128, NQ], F32)
    sqw = [d.sb("sqw0", [128, 512], BF16)]
    rtmp = d.sb("rtmp", [128, 512], F32)
    mixs = d.sb("mixs", [128, NQ], BF16)
    v_dv = v_d.t.rearrange("(b p) c -> p b c", p=128)
    def load_pair(pc):
        knT = knT2[pc % 2]; vpair = vpair2[pc % 2]; qT = qT2[pc % 2]
        dma('sp', knT[:], knT_d.t[pc], [knT_d], [knT], knT)
        for q4 in range(4):
            dma('sp', vpair[:, q4 * 16:(q4 + 1) * 16, :], v_dv[:, q4 * 16:(q4 + 1) * 16, pc * 128:(pc + 1) * 128], [v_d], [vpair], vpair)
        dma('sp', qT[:], qT_d.t[pc], [qT_d], [qT], qT)
    load_pair(0)
    for pc in range(4):
        if pc + 1 < 4: load_pair(pc + 1)
        knT = knT2[pc % 2]; vpair = vpair2[pc % 2]; qT = qT2[pc % 2]
        for h in range(2):
            hs = slice(64 * h, 64 * h + 64)
            head = pc * 2 + h
            bias_ap = sbb[:, head:head + 1]
            glist = [(0, 128, 47, 1)] + [(128 + 512 * g, 512, 48 + 4 * g, 4) for g in range(4)]
            for (qc0, nq, diag0, ndiag) in glist:
                top = diag0 + ndiag - 1
                kbs = list(range(top, -1, -1))
                steps = [dict(mask=(Mi[kb - diag0] if kb >= diag0 else None)) for kb in kbs]
                po = P[4 + h]
                def zmms(si, pb, start, stop_unused, kbs=kbs, qc0=qc0, nq=nq, hs=hs, knT=knT, qT=qT):
                    kb = kbs[si]
                    op('pe', lambda e: e.matmul(pb[:, 0:nq], lhsT=knT[hs, kb * 128:(kb + 1) * 128], rhs=qT[hs, qc0:qc0 + nq], start=start, stop=True), [knT, qT], [pb])
                def avmm(si, a_, first, last, kbs=kbs, nq=nq, po=po, vpair=vpair):
                    kb = kbs[si]
                    op('pe', lambda e: e.matmul(po[:, 0:nq], lhsT=vpair[:, kb, :], rhs=a_[:, 0:nq], start=first, stop=last), [vpair, a_], [po])
                attn_chain(steps, nq, zmms, avmm, bias_ap, wk)
                op('act', lambda e, po=po, qc0=qc0, nq=nq, hs=hs: e.copy(out=opair[hs, qc0:qc0 + nq], in_=po[hs, 0:nq]), [po], [opair])
        for c0 in range(0, NQ, 512):
            n = min(512, NQ - c0)
            colstat_rstd([(opair, opair[:, c0:c0 + n])], n, bones[:], 128, P[6], rtmp, sqw)
            op('dve', lambda e, c0=c0, n=n, pc=pc: e.scalar_tensor_tensor(out=mixs[:, c0:c0 + n], in0=opair[:, c0:c0 + n], scalar=col(C_GNSB + pc), in1=rtmp[:, 0:n], op0=ALU.mult, op1=ALU.mult), [opair, colp, rtmp], [mixs])
        dma('sp', mixT_d.t[pc], mixs[:], [mixs], [mixT_d], mixs)
    d.pop()

    if STOP == 'P2': return bail()
    def conv_post(cvT, n, outfn, wkf, pb1):
        sq = wkf['sq']; tmp = wkf['tmp']; sg = wkf['sg']
        for cc in range(4):
            op('pe', lambda e, cc=cc: e.matmul(pb1[:, 0:n], lhsT=m512[:], rhs=cvT[:, cc, 0:n], start=(cc == 0), stop=(cc == 3)), [m512, cvT], [pb1])
        for cc in range(4):
            op('dve', lambda e, cc=cc: e.tensor_tensor(out=cvT[:, cc, 0:n], in0=cvT[:, cc, 0:n], in1=pb1[:, 0:n], op=ALU.subtract), [cvT, pb1], [cvT])
        colstat_rstd([(cvT, cvT[:, cc, 0:n]) for cc in range(4)], n, m512[:], 128, pb1, tmp, sq)
        for cc in range(4):
            op('dve', lambda e, cc=cc: e.tensor_tensor(out=cvT[:, cc, 0:n], in0=cvT[:, cc, 0:n], in1=tmp[:, 0:n], op=ALU.mult), [cvT, tmp], [cvT])
            op('dve', lambda e, cc=cc: e.tensor_scalar(out=cvT[:, cc, 0:n], in0=cvT[:, cc, 0:n], scalar1=col(C_CLNG + cc), scalar2=col(C_CLNB + cc), op0=ALU.mult, op1=ALU.add), [cvT, colp], [cvT])
            op('act', lambda e, cc=cc: e.activation(out=sg[:, 0:n], in_=cvT[:, cc, 0:n], func=AF.Sigmoid), [cvT], [sg])
            op('dve', lambda e, cc=cc: e.tensor_tensor(out=cvT[:, cc, 0:n], in0=cvT[:, cc, 0:n], in1=sg[:, 0:n], op=ALU.mult), [cvT, sg], [cvT])
        colstat_rstd([(cvT, cvT[:, cc, 0:n]) for cc in range(4)], n, m512[:], 128, pb1, tmp, sq)
        for cc in range(4):
            op('dve', lambda e, cc=cc: e.scalar_tensor_tensor(out=cvT[:, cc, 0:n], in0=cvT[:, cc, 0:n], scalar=col(C_GNCV + cc), in1=tmp[:, 0:n], op0=ALU.mult, op1=ALU.mult), [cvT, colp, tmp], [cvT])
            outfn(cc)

    d.push()
    cvw = [d.sb(f"cvw{i}", [128, 4, 512], F32) for i in range(2)]
    mixc = [d.sb(f"mixc{i}", [128, 4, 512], BF16) for i in range(2)]
    for ci, c0 in enumerate(range(0, NQ, 512)):
        n = min(512, NQ - c0)
        cv = cvw[ci % 2]; mx = mixc[ci % 2]
        for cc in range(4):
            pb = P[cc % 2]
            for k in range(31):
                op('pe', lambda e, k=k, cc=cc, pb=pb: e.matmul(pb[:, 0:n], lhsT=dwd[:, k * 4 + cc, :], rhs=uT_b[:, cc, c0 + 2 + k:c0 + 2 + k + n], start=(k == 0), stop=(k == 30)), [dwd, uT_b], [pb])
            op('act', lambda e, cc=cc, pb=pb: e.activation(out=cv[:, cc, 0:n], in_=pb[:, 0:n], func=AF.Identity, bias=col(C_BDW + cc), scale=1.0), [pb, colp], [cv])
        def outfn(cc, cv=cv, mx=mx, n=n):
            op('pool', lambda e: e.tensor_copy(out=mx[:, cc, 0:n], in_=cv[:, cc, 0:n]), [cv], [mx])
        conv_post(cv, n, outfn, wkf, P[2])
        for cc in range(4):
            dma('sp', mixT_d.t[4 + cc, :, c0:c0 + n], mx[:, cc, 0:n], [mx], [mixT_d], mx)

    d.pop()
    d.pop()
    if STOP == 'P3': return bail()
    d.push()
    A1s = d.sb("A1s", [32, D], F32); B1s = d.sb("B1s", [32, D], F32)
    for s_ in range(4):
        load_mod(B1s, 1 + s_, 0, 8, 8 * s_); load_mod(A1s, 1 + s_, 1024, 8, 8 * s_)
    winb2 = d.sb("winb2", [128, KC, 2560], BF16)
    for k in range(KC):
        dma('pool', winb2[:, k, :], w_in.t[k * 128:(k + 1) * 128, :], [w_in], [winb2], winb2)
    xts = d.sb("xts", [32, D], F32); t1s = d.sb("t1s", [32, D], F32); hbs = d.sb("hbs", [32, D], BF16)
    hTs = d.sb("hTs", [128, KC, 32], BF16)
    dma('sp', xts[:], x_s[:], [x_s], [xts], xts)
    modulate_transpose(xts, 32, A1s, B1s, None, hbs, t1s, hTs, 0)
    qT_n = d.sb("qT_n", [128, 4, 32], BF16)
    vnew = d.sb("vnew", [128, 4, 512], BF16)
    vtok = d.sb("vtok", [32, 512], F32); ktok = d.sb("ktok", [32, 512], F32)
    vtokb = d.sb("vtokb", [32, 512], BF16)
    us_f = d.sb("us_f", [128, 4, 32], F32)
    sgs = d.sb("sgs", [128, 32], F32)
    knT_f = d.sb("knT_f", [128, 4, 32], BF16)
    for pc in range(4):
        pb = P[pc % 2]
        mm_acc(pb, pb[:, 0:32], lambda k, pc=pc: winb2[:, k, 512 + pc * 128:512 + (pc + 1) * 128], lambda k: hTs[:, k, :], KC, [winb2, hTs])
        op('act', lambda e, pb=pb, pc=pc: e.activation(out=knT_f[:, pc, :], in_=pb[:, 0:32], func=AF.Identity, scale=-0.125), [pb], [knT_f])
        pq = P[2 + pc % 2]
        mm_acc(pq, pq[:, 0:32], lambda k, pc=pc: winb2[:, k, pc * 128:(pc + 1) * 128], lambda k: hTs[:, k, :], KC, [winb2, hTs])
        op('dve', lambda e, pq=pq, pc=pc: e.tensor_copy(out=qT_n[:, pc, :], in_=pq[:, 0:32]), [pq], [qT_n])
    pv = P[4]
    mm_acc(pv, pv[0:32, :], lambda k: hTs[:, k, :], lambda k: winb2[:, k, 1024:1536], KC, [winb2, hTs])
    op('dve', lambda e: e.tensor_copy(out=vtok[:], in_=pv[0:32, :]), [pv], [vtok])
    op('act', lambda e: e.copy(out=vtokb[:], in_=pv[0:32, :]), [pv], [vtokb])
    dma('sp', v_s[:], vtok[:], [vtok], [v_s], vtok, is_output=True)
    pk = P[5]
    mm_acc(pk, pk[0:32, :], lambda k: hTs[:, k, :], lambda k: winb2[:, k, 512:1024], KC, [winb2, hTs])
    op('dve', lambda e: e.tensor_copy(out=ktok[:], in_=pk[0:32, :]), [pk], [ktok])
    dma('sp', k_s[:], ktok[:], [ktok], [k_s], ktok, is_output=True)
    op('pool', lambda e: e.memset(vnew[:], 0.0), [], [vnew])
    for s_ in range(4):
        dma('sp', vnew[0:8, s_, :], vtokb[8 * s_:8 * s_ + 8, :], [vtokb], [vnew], vnew)
    for cc in range(4):
        pa, pg = P[0], P[1]
        mm_acc(pa, pa[:, 0:32], lambda k, cc=cc: winb2[:, k, 1536 + cc * 128:1536 + (cc + 1) * 128], lambda k: hTs[:, k, :], KC, [winb2, hTs])
        mm_acc(pg, pg[:, 0:32], lambda k, cc=cc: winb2[:, k, 2048 + cc * 128:2048 + (cc + 1) * 128], lambda k: hTs[:, k, :], KC, [winb2, hTs])
        op('act', lambda e: e.activation(out=sgs[:], in_=pg[:, 0:32], func=AF.Sigmoid), [pg], [sgs])
        op('dve', lambda e, cc=cc: e.tensor_tensor(out=us_f[:, cc, :], in0=pa[:, 0:32], in1=sgs[:], op=ALU.mult), [pa, sgs], [us_f])
    u_ext = d.sb("u_ext", [128, 4, 4, 38], BF16)
    stc = d.sb("stc", [120, 512], F32)
    dma('sp', stc[:], st_conv.t.rearrange("s t c -> (s t) c"), [st_conv], [stc], stc)
    for cc in range(4):
        pb = P[2 + cc % 2]
        op('pe', lambda e, cc=cc, pb=pb: e.transpose(out=pb[:, 0:120], in_=stc[:, cc * 128:(cc + 1) * 128], identity=ident_f[0:120, 0:120]), [stc, ident_f], [pb])
        op('dve', lambda e, cc=cc, pb=pb: e.tensor_copy(out=u_ext[:, cc, :, 0:30], in_=pb[:, 0:120].rearrange("p (s t) -> p s t", s=4)), [pb], [u_ext])
        op('pool', lambda e, cc=cc: e.tensor_copy(out=u_ext[:, cc, :, 30:38], in_=us_f[:, cc, :].rearrange("p (s t) -> p s t", s=4)), [us_f], [u_ext])
        for s_ in range(4):
            dma('sp', conv_s.t[s_, 22:30, cc * 128:(cc + 1) * 128].rearrange("t p -> p t"), us_f[:, cc, 8 * s_:8 * s_ + 8], [us_f], [conv_s], us_f, is_output=True, allow_slow_non_contiguous=True)
    cps = d.sb("cps", [88, 512], F32)
    for s_ in range(4):
        dma('sp', cps[22 * s_:22 * s_ + 22, :], st_conv.t[s_, 8:30, :], [st_conv], [cps], cps)
    for s_ in range(4):
        dma('sp', conv_s.t[s_, 0:22, :], cps[22 * s_:22 * s_ + 22, :], [cps], [conv_s], cps, is_output=True)
    cvs = d.sb("cvs", [128, 4, 32], F32)
    mixTs = d.sb("mixTs", [128, 8, 32], BF16)
    for cc in range(4):
        pb = P[cc % 2]
        for k in range(31):
            op('pe', lambda e, k=k, cc=cc, pb=pb: e.matmul(pb[:, 0:32], lhsT=dwd[:, k * 4 + cc, :], rhs=u_ext[:, cc, :, k:k + 8], start=(k == 0), stop=(k == 30)), [dwd, u_ext], [pb])
        op('act', lambda e, cc=cc, pb=pb: e.activation(out=cvs[:, cc, :], in_=pb[:, 0:32], func=AF.Identity, bias=col(C_BDW + cc), scale=1.0), [pb, colp], [cvs])
    def outfn_s(cc):
        op('pool', lambda e: e.tensor_copy(out=mixTs[:, 4 + cc, :], in_=cvs[:, cc, :]), [cvs], [mixTs])
    conv_post(cvs, 32, outfn_s, wkf, P[2])

    if STOP == 'S1': return bail()
    Qbd = d.sb("Qbd", [128, 4, 4, 16], BF16)
    op('pool', lambda e: e.memset(Qbd[:], 0.0), [], [Qbd])
    for pc in range(4):
        op('dve', lambda e, pc=pc: e.tensor_copy(out=Qbd[0:64, pc, :, 0:8], in_=qT_n[0:64, pc, :].rearrange("p (s t) -> p s t", s=4)), [qT_n], [Qbd])
        op('dve', lambda e, pc=pc: e.tensor_copy(out=Qbd[64:128, pc, :, 8:16], in_=qT_n[64:128, pc, :].rearrange("p (s t) -> p s t", s=4)), [qT_n], [Qbd])
    negb = d.sb("negb", [1, 256], F32)
    onesrow = d.sb("onesrow", [1, 128], F32)
    dma('sp', negb[:], brow[:], [brow], [negb], negb)
    op('dve', lambda e: e.tensor_scalar(out=negb[:], in0=negb[:], scalar1=-1.0, scalar2=None, op0=ALU.mult), [negb], [negb])
    op('pool', lambda e: e.memset(onesrow[:], 1.0), [], [onesrow])
    knT_ns = d.sb("knT_ns", [128, 4, 4, 128], BF16)
    op('pool', lambda e: e.memset(knT_ns[:], 0.0), [], [knT_ns])
    for s_ in range(4):
        op('dve', lambda e, s_=s_: e.tensor_copy(out=knT_ns[:, s_, :, 0:8], in_=knT_f[:, :, 8 * s_:8 * s_ + 8]), [knT_f], [knT_ns])
    pt_i = d.sb("pt_i", [128, 512], I32)
    ar_f = d.sb("ar_f", [128, 1], F32)
    offs = d.sb("offs", [128, 512], I32)
    dma('sp', pt_i[:], ptab.t.partition_broadcast(128), [ptab], [pt_i], pt_i)
    dma('sp', ar_f[:], arange[:], [arange], [ar_f], ar_f)
    op('dve', lambda e: e.tensor_scalar(out=offs[:], in0=pt_i[:], scalar1=128.0, scalar2=ar_f[:, 0:1], op0=ALU.mult, op1=ALU.add), [pt_i, ar_f], [offs])
    NKP = 3; NVP = 4
    kpg = [[d.sb(f"kpg{i}_{s_}", [128, 512], BF16) for s_ in range(4)] for i in range(NKP)]
    vpg = [[d.sb(f"vpg{i}_{s_}", [128, 512], BF16) for s_ in range(4)] for i in range(NVP)]
    kTp = [d.sb(f"kTp{i}", [128, 4, 4, 128], BF16) for i in range(2)]
    wks = dict(e=[d.sb(f"se_sb{i}", [128, 256], F32) for i in range(3)],
               sp=[d.sb(f"ssp_sb{i}", [128, 256], BF16) for i in range(3)],
               e2=[d.sb(f"se2_sb{i}", [128, 256], F32) for i in range(2)],
               a=[d.sb(f"sa_sb{i}", [128, 256], BF16) for i in range(2)],
               accf=d.sb("saccf", [128, 256], F32),
               accb=[d.sb(f"saccb{i}", [128, 256], BF16) for i in range(3)])
    steps = [dict(mask=Ms)] + [dict(mask=None) for _ in range(128)]
    state = {}
    def prep_pageK(si):
        pg = 128 - si
        i = si % NKP
        for s_ in range(4):
            cidx = s_ * 128 + pg
            dma('pool', kpg[i][s_][:], cache_k.t, [cache_k, offs], [kpg[i][s_]], kpg[i][s_], indirect=bass.IndirectOffsetOnAxis(ap=offs[:, cidx:cidx + 1], axis=0))
    def prep_pageV(si):
        pg = 128 - si
        iv = si % NVP
        for s_ in range(4):
            cidx = s_ * 128 + pg
            dma('pool', vpg[iv][s_][:], cache_v.t, [cache_v, offs], [vpg[iv][s_]], vpg[iv][s_], indirect=bass.IndirectOffsetOnAxis(ap=offs[:, cidx:cidx + 1], axis=0))
    def prep_kT(si):
        i = si % NKP
        kt = kTp[si % 2]
        for s_ in range(4):
            for pc in range(4):
                op('pe', lambda e, s_=s_, pc=pc: e.transpose(out=PT[:, pc * 128:(pc + 1) * 128], in_=kpg[i][s_][:, pc * 128:(pc + 1) * 128], identity=ident_b[:]), [kpg[i][s_], ident_b], [PT])
            op('dve', lambda e, s_=s_: e.tensor_scalar(out=kt[:, s_, :, :], in0=PT[:, 0:512].rearrange("p (c k) -> p c k", c=4), scalar1=-0.125, scalar2=None, op0=ALU.mult), [PT], [kt])
    def zmms_s(si, pb, start, stop_unused):
        kt = knT_ns if si == 0 else kTp[si % 2]
        op('pe', lambda e: e.matmul(pb[:, 0:256], lhsT=onesrow[:], rhs=negb[:], start=start, stop=False), [onesrow, negb], [pb])
        n = 0
        for s_ in range(4):
            for pc in range(4):
                n += 1
                c0 = s_ * 64 + pc * 16
                op('pe', lambda e, s_=s_, pc=pc, c0=c0, n=n: e.matmul(pb[:, c0:c0 + 16], lhsT=kt[:, s_, pc, :], rhs=Qbd[:, pc, s_, :], start=False, stop=(n == 16)), [kt, Qbd], [pb])
    po_seq = [P[3 + s_] for s_ in range(4)]
    def avmm_s(si, a_, first, last):
        for s_ in range(4):
            vsrc = vnew[:, s_, :] if si == 0 else vpg[si % NVP][s_][:]
            vb = vnew if si == 0 else vpg[si % NVP][s_]
            op('pe', lambda e, vsrc=vsrc, s_=s_: e.matmul(po_seq[s_][0:64, :], lhsT=a_[:, s_ * 64:(s_ + 1) * 64], rhs=vsrc, start=first, stop=last), [vb, a_], [po_seq[s_]])
    prep_pageK(1); prep_pageK(2)
    def prepA_s(si):
        if si == 0: return
        if si + 2 <= 128: prep_pageK(si + 2)
        prep_pageV(si)
        prep_kT(si)
    attn_chain(steps, 256, zmms_s, avmm_s, None, wks, prepA=prepA_s, npy=1)
    osb = [d.sb(f"osb{i}", [64, 512], F32) for i in range(2)]
    OT = d.sb("OT", [128, 4, 4, 8], F32)
    ortmp = d.sb("ortmp", [128, 128], F32)
    osq = [d.sb("osq", [128, 128], BF16)]
    psel = [P[0], P[1]]
    for s_ in range(4):
        ob = osb[s_ % 2]
        op('act', lambda e, s_=s_, ob=ob: e.copy(out=ob[:], in_=po_seq[s_][0:64, :]), [po_seq[s_]], [ob])
        pb = psel[s_ % 2]
        for pc in range(4):
            op('pe', lambda e, pc=pc, ob=ob, pb=pb: e.matmul(pb[:, pc * 16:(pc + 1) * 16], lhsT=ob[:, pc * 128:(pc + 1) * 128], rhs=ident_f[0:64, pc * 16:(pc + 1) * 16], start=True, stop=True), [ob, ident_f], [pb])
        pbv = pb[:, 0:64].rearrange("p (c j) -> p c j", c=4)
        op('dve', lambda e, s_=s_, pbv=pbv: e.tensor_copy(out=OT[0:64, :, s_, :], in_=pbv[0:64, :, 0:8]), [pb], [OT])
        op('dve', lambda e, s_=s_, pbv=pbv: e.tensor_copy(out=OT[64:128, :, s_, :], in_=pbv[64:128, :, 8:16]), [pb], [OT])
    OTf = OT[:, :, :, :].rearrange("p c s t -> p (c s t)")
    colstat_rstd([(OT, OTf)], 128, bones[:], 128, P[2], ortmp, osq)
    for pc in range(4):
        op('dve', lambda e, pc=pc: e.scalar_tensor_tensor(out=mixTs[:, pc, :], in0=OTf[:, pc * 32:(pc + 1) * 32], scalar=col(C_GNSB + pc), in1=ortmp[:, pc * 32:(pc + 1) * 32], op0=ALU.mult, op1=ALU.mult), [OT, colp, ortmp], [mixTs])
    mixTs_keep = d.dram("mixTs_d", [128, 8, 32], BF16, OUT if os.environ.get("DBG") else "Internal")
    dma('sp', mixTs_keep[:], mixTs[:], [mixTs], [mixTs_keep], mixTs)
    d.pop()
    d.pop()

    if STOP == 'S2': return bail()
    d.push()
    gts_halo = d.sb("gts_halo", [128, NFC, 8], F32)
    d.push()
    stf = d.sb("stf", [8, DFF], F32)
    dma('sp', stf[:], st_ffn.t.rearrange("s t c -> (s t) c"), [st_ffn], [stf], stf)
    for fc in range(NFC):
        pb = P[fc % 2]
        op('pe', lambda e, fc=fc, pb=pb: e.transpose(out=pb[:, 0:8], in_=stf[:, fc * 128:(fc + 1) * 128], identity=ident_f[0:8, 0:8]), [stf, ident_f], [pb])
        op('dve', lambda e, fc=fc, pb=pb: e.tensor_copy(out=gts_halo[:, fc, :], in_=pb[:, 0:8]), [pb], [gts_halo])
    d.pop()
    woutb = d.sb("woutb", [128, 8, D], BF16)
    for k in range(8):
        dma('pool', woutb[:, k, :], w_out.t[k * 128:(k + 1) * 128, :], [w_out], [woutb], woutb)
    wdnb = d.sb("wdnb", [128, NFC, D], BF16)
    for fc in range(NFC):
        dma('pool', wdnb[:, fc, :], w_down.t[fc * 128:(fc + 1) * 128, :], [w_down], [wdnb], wdnb)
    lnt = [d.sb(f"lnt{i}", [128, D], F32) for i in range(4)]
    for i in range(4):
        dma('sp', lnt[i][:], lnp.t[i:i + 1, :].broadcast_to([128, D]), [lnp], [lnt[i]], lnt[i])
    xt4 = [d.sb(f"xq{i}", [128, D], F32) for i in range(1)]
    tmp4 = d.sb("tmp4", [128, D], F32)
    x1buf = d.sb("x1buf", [128, 4, D], F32)
    hb4 = d.sb("hb4", [128, D], BF16)
    h2T = d.sb("h2T", [128, KC, 512], BF16)
    mix_sb = d.sb("mix_sb", [128, 8, 512], BF16)
    wupg = [d.sb(f"wupg{i}", [128, KC, 128], BF16) for i in range(2)]
    wupv = [d.sb(f"wupv{i}", [128, KC, 128], BF16) for i in range(2)]
    gt_ext = [d.sb(f"gt_ext{i}", [128, 520], F32) for i in range(2)]
    gc = d.sb("gc", [128, 512], F32)
    ge = d.sb("ge", [128, 512], F32)
    fT = d.sb("fT", [128, NFC, 512], BF16)
    ybuf = [d.sb(f"ybuf{i}", [128, D], F32) for i in range(1)]
    stats = d.sb("stats", [128, 12], F32); mv = d.sb("mv", [128, 2], F32); rstd = d.sb("rstd", [128, 1], F32)
    w_up_v = w_up.t.rearrange("(k p) n -> p k n", p=128)
    A2s = d.sb("A2s", [32, D], F32); B2s = d.sb("B2s", [32, D], F32); G1s = d.sb("G1s", [32, D], F32); G2s = d.sb("G2s", [32, D], F32)
    for s_ in range(4):
        load_mod(G1s, 1 + s_, 2048, 8, 8 * s_); load_mod(B2s, 1 + s_, 3072, 8, 8 * s_)
        load_mod(A2s, 1 + s_, 4096, 8, 8 * s_); load_mod(G2s, 1 + s_, 5120, 8, 8 * s_)
    yi = 0

    def token_front(blocks, mixsrc, ncols_total, g1t, a2t, b2t):
        for bi, (ntok, x_ap, xdep, mc0, slot) in enumerate(blocks):
            xq = xt4[0]
            dma('sp', xq[0:ntok, :], x_ap, [xdep], [xq], xq)
            for hf in range(2):
                pb = P[hf]
                mm_acc(pb, pb[0:ntok, :], lambda k: mixsrc[:, k, mc0:mc0 + ntok], lambda k, hf=hf: woutb[:, k, hf * 512:(hf + 1) * 512], 8, [mixsrc, woutb])
                op('dve', lambda e, hf=hf, pb=pb: e.tensor_tensor(out=tmp4[0:ntok, hf * 512:(hf + 1) * 512], in0=pb[0:ntok, :], in1=g1t[0:ntok, hf * 512:(hf + 1) * 512], op=ALU.mult), [pb, g1t], [tmp4])
            op('dve', lambda e: e.scalar_tensor_tensor(out=tmp4[0:ntok, :], in0=xq[0:ntok, :], scalar=ALPHA, in1=tmp4[0:ntok, :], op0=ALU.mult, op1=ALU.add), [xq, tmp4], [tmp4])
            x1 = x1buf
            layernorm_rows_slot(tmp4, ntok, lnt[0], lnt[1], slot)
            modulate_transpose_slot(ntok, slot, a2t, b2t, bi * 128)

    def layernorm_rows_slot(src, ntok, gam, bet, slot):
        for hf in range(2):
            op('dve', lambda e, hf=hf: e.bn_stats(out=stats[0:ntok, hf * 6:(hf + 1) * 6], in_=src[0:ntok, hf * 512:(hf + 1) * 512]), [src], [stats])
        op('dve', lambda e: e.bn_aggr(out=mv[0:ntok, :], in_=stats[0:ntok, :]), [stats], [mv])
        op('dve', lambda e: e.tensor_scalar(out=rstd[0:ntok, :], in0=mv[0:ntok, 1:2], scalar1=EPS, scalar2=None, op0=ALU.add), [mv], [rstd])
        op('pool', lambda e: e.tensor_tensor(out=rstd[0:ntok, :], in0=rstd[0:ntok, :], in1=mhalf[0:ntok, 0:1], op=ALU.pow), [rstd, mhalf], [rstd])
        op('dve', lambda e: e.tensor_scalar(out=x1buf[0:ntok, slot, :], in0=src[0:ntok, :], scalar1=mv[0:ntok, 0:1], scalar2=rstd[0:ntok, 0:1], op0=ALU.subtract, op1=ALU.mult), [src, mv, rstd], [x1buf])
        op('dve', lambda e: e.tensor_tensor(out=x1buf[0:ntok, slot, :], in0=x1buf[0:ntok, slot, :], in1=gam[0:ntok, :], op=ALU.mult), [gam], [x1buf])
        op('dve', lambda e: e.tensor_tensor(out=x1buf[0:ntok, slot, :], in0=x1buf[0:ntok, slot, :], in1=bet[0:ntok, :], op=ALU.add), [bet], [x1buf])

    def modulate_transpose_slot(ntok, slot, a2t, b2t, c0):
        op('dve', lambda e: e.tensor_tensor(out=tmp4[0:ntok, :], in0=x1buf[0:ntok, slot, :], in1=a2t[0:ntok, :], op=ALU.mult), [x1buf, a2t], [tmp4])
        op('dve', lambda e: e.tensor_tensor(out=hb4[0:ntok, :], in0=tmp4[0:ntok, :], in1=b2t[0:ntok, :], op=ALU.add), [tmp4, b2t], [hb4])
        for k in range(KC):
            op('pe', lambda e, k=k: e.transpose(out=PT[:, k * 128:k * 128 + ntok], in_=hb4[0:ntok, k * 128:(k + 1) * 128], identity=ident_b[0:ntok, 0:ntok]), [hb4, ident_b], [PT])
        op('act', lambda e: e.copy(out=h2T[:, :, c0:c0 + ntok], in_=PT[:].rearrange("p (k t) -> p k t", k=KC)[:, :, 0:ntok]), [PT], [h2T])

    def ffn_mid(n, halo_src, nseq, do_val, cs=None):
        tl = n // nseq
        c_lo = 0 if cs is None else cs
        for fc in range(NFC):
            wg = wupg[fc % 2]; wv = wupv[fc % 2]
            dma('pool', wg[:], w_up_v[:, :, DFF + fc * 128:DFF + (fc + 1) * 128], [w_up], [wg], wg)
            if do_val:
                dma('pool', wv[:], w_up_v[:, :, fc * 128:(fc + 1) * 128], [w_up], [wv], wv)
            pg = P[2 + fc % 2]; pvv = P[4 + fc % 2]
            mm_acc(pg, pg[:, 0:n], lambda k: wg[:, k, :], lambda k: h2T[:, k, c_lo:c_lo + n], KC, [wg, h2T])
            if do_val:
                mm_acc(pvv, pvv[:, 0:n], lambda k: wv[:, k, :], lambda k: h2T[:, k, c_lo:c_lo + n], KC, [wv, h2T])
            gx = gt_ext[fc % 2]
            gxv = gx[:, 0:nseq * (tl + 2)].rearrange("p (s t) -> p s t", s=nseq)
            yield ('gate', fc, pg, gx, gxv, pvv)

    def conv3_gelu(fc, gxv, nseq, tl, n, pvv):
        gcv = gc[:, 0:n].rearrange("p (s t) -> p s t", s=nseq)
        op('dve', lambda e: e.tensor_scalar(out=gcv, in0=gxv[:, :, 0:tl], scalar1=col(C_WFDW + 0 * NFC + fc), scalar2=col(C_BFDW + fc), op0=ALU.mult, op1=ALU.add), [gt_ext[fc % 2], colp], [gc])
        for j in (1, 2):
            op('dve', lambda e, j=j: e.scalar_tensor_tensor(out=gcv, in0=gxv[:, :, j:j + tl], scalar=col(C_WFDW + j * NFC + fc), in1=gcv, op0=ALU.mult, op1=ALU.add), [gt_ext[fc % 2], colp, gc], [gc])
        op('act', lambda e: e.activation(out=ge[:, 0:n], in_=gc[:, 0:n], func=AF.Gelu), [gc], [ge])
        op('dve', lambda e: e.tensor_tensor(out=fT[:, fc, 0:n], in0=ge[:, 0:n], in1=pvv[:, 0:n], op=ALU.mult), [ge, pvv], [fT])

    def token_back(blocks, g2t, out_t):
        nonlocal yi
        for bi, (ntok, slot, out_ap) in enumerate(blocks):
            for hf in range(2):
                pb = P[hf]
                mm_acc(pb, pb[0:ntok, :], lambda k: fT[:, k, bi * 128:bi * 128 + ntok], lambda k, hf=hf: wdnb[:, k, hf * 512:(hf + 1) * 512], NFC, [fT, wdnb])
                op('dve', lambda e, hf=hf, pb=pb: e.tensor_tensor(out=tmp4[0:ntok, hf * 512:(hf + 1) * 512], in0=pb[0:ntok, :], in1=g2t[0:ntok, hf * 512:(hf + 1) * 512], op=ALU.mult), [pb, g2t], [tmp4])
            op('dve', lambda e: e.scalar_tensor_tensor(out=tmp4[0:ntok, :], in0=x1buf[0:ntok, slot, :], scalar=ALPHA, in1=tmp4[0:ntok, :], op0=ALU.mult, op1=ALU.add), [x1buf, tmp4], [tmp4])
            yb = ybuf[0]; yi += 1
            layernorm_rows(tmp4, ntok, lnt[2], lnt[3], yb, stats, mv, rstd)
            dma('sp', out_ap, yb[0:ntok, :], [yb], [out_t], yb, is_output=True)

    dma('sp', mix_sb[:, :, 0:128], mixT_d.t[:, :, 0:128].rearrange("c p t -> p c t"), [mixT_d], [mix_sb], mix_sb)
    token_front([(128, xs.t[47 * 128:48 * 128, :], xs, 0, 0)], mix_sb, 128, G1, A2, B2)
    for (kind, fc, pg, gx, gxv, pvv) in ffn_mid(2, None, 1, False, cs=126):
        op('dve', lambda e, fc=fc, pg=pg: e.tensor_scalar(out=gthalo[:, fc, :], in0=pg[:, 0:2], scalar1=hfl[:, 0:1], scalar2=None, op0=ALU.mult), [pg, hfl], [gthalo])
    for sbo in range(4):
        c0 = 128 + sbo * 512
        dma('sp', mix_sb[:], mixT_d.t[:, :, c0:c0 + 512].rearrange("c p t -> p c t"), [mixT_d], [mix_sb], mix_sb)
        blocks = [(128, xs.t[(48 + sbo * 4 + bl) * 128:(49 + sbo * 4 + bl) * 128, :], xs, bl * 128, bl) for bl in range(4)]
        token_front(blocks, mix_sb, 512, G1, A2, B2)
        for (kind, fc, pg, gx, gxv, pvv) in ffn_mid(512, None, 1, True):
            op('pool', lambda e, fc=fc, gx=gx: e.tensor_copy(out=gx[:, 0:2], in_=gthalo[:, fc, :]), [gthalo], [gx])
            op('act', lambda e, pg=pg, gx=gx: e.copy(out=gx[:, 2:514], in_=pg[:, 0:512]), [pg], [gx])
            op('pool', lambda e, fc=fc, gx=gx: e.tensor_copy(out=gthalo[:, fc, :], in_=gx[:, 512:514]), [gx], [gthalo])
            conv3_gelu(fc, gxv, 1, 512, 512, pvv)
        token_back([(128, bl, y_p.t[(sbo * 4 + bl) * 128:(sbo * 4 + bl + 1) * 128, :]) for bl in range(4)], G2, y_p)
    for fc in range(NFC):
        dma('sp', ffn_p.t[:, fc * 128:(fc + 1) * 128].rearrange("t p -> p t"), gthalo[:, fc, :], [gthalo], [ffn_p], gthalo, is_output=True, allow_slow_non_contiguous=True)
    mixTs2 = d.sb("mixTs2", [128, 8, 32], BF16)
    dma('sp', mixTs2[:], mixTs_keep[:], [mixTs_keep], [mixTs2], mixTs2)
    token_front([(32, x_s[:], x_s, 0, 0)], mixTs2, 32, G1s, A2s, B2s)
    gsn = d.sb("gsn", [128, NFC, 8], F32)
    for (kind, fc, pg, gx, gxv, pvv) in ffn_mid(32, None, 4, True):
        op('pool', lambda e, fc=fc, gxv=gxv: e.tensor_copy(out=gxv[:, :, 0:2], in_=gts_halo[:, fc, :].rearrange("p (s t) -> p s t", s=4)), [gts_halo], [gx])
        op('act', lambda e, pg=pg, gxv=gxv: e.copy(out=gxv[:, :, 2:10], in_=pg[:, 0:32].rearrange("p (s t) -> p s t", s=4)), [pg], [gx])
        op('pool', lambda e, fc=fc, gxv=gxv: e.tensor_copy(out=gsn[:, fc, :].rearrange("p (s t) -> p s t", s=4), in_=gxv[:, :, 8:10]), [gx], [gsn])
        conv3_gelu(fc, gxv, 4, 8, 32, pvv)
    token_back([(32, 0, y_s[:])], G2s, y_s)
    for fc in range(NFC):
        for s_ in range(4):
            dma('sp', ffn_s.t[s_, :, fc * 128:(fc + 1) * 128].rearrange("t p -> p t"), gsn[:, fc, 2 * s_:2 * s_ + 2], [gsn], [ffn_s], gsn, is_output=True, allow_slow_non_contiguous=True)
    d.finish()
    d.stacks.pop().close()
    d.stacks.pop().close()
    return nc


_NC = None


def make_in_maps(inp):
    f32 = np.float32
    x_prompt = np.asarray(inp['x_prompt'], f32); x_sample = np.asarray(inp['x_sample'], f32)
    ck = np.ascontiguousarray(np.asarray(inp['cache_k'], f32)[0].reshape(5120 * 128, 512))
    cv = np.ascontiguousarray(np.asarray(inp['cache_v'], f32)[0].reshape(5120 * 128, 512))
    page_table = np.asarray(inp['page_table'], np.int32)
    g = lambda k: np.asarray(inp[k], f32)[0]
    w_dw = g('w_dw'); w_fdw = g('w_fdw')
    def cols(v, n):
        return np.ascontiguousarray(v.reshape(n, 128).T)
    colp = np.concatenate([
        cols(g('gn_sb'), 4), cols(g('gn_conv'), 4), cols(g('cln_g'), 4), cols(g('cln_b'), 4), cols(g('b_dw'), 4),
        np.ascontiguousarray(w_dw.reshape(31, 4, 128).transpose(2, 0, 1).reshape(128, 124)),
        cols(g('b_fdw'), NFC),
        np.ascontiguousarray(w_fdw.reshape(3, NFC, 128).transpose(2, 0, 1).reshape(128, 66)),
    ], axis=1).astype(f32)
    assert colp.shape == (128, NCOLP)
    sbb = g('sb_bias').reshape(1, 8)
    brow = np.ascontiguousarray(np.broadcast_to(sbb.reshape(1, 1, 8, 1), (1, 4, 8, 8)).reshape(1, 256))
    gsb_hd = np.ascontiguousarray(g('gn_sb').reshape(8, 64).T)
    lnp = np.stack([g('ln1_g'), g('ln1_b'), g('ln2_g'), g('ln2_b')]).astype(f32)
    shared = dict(cache_k=ck, cache_v=cv, arange=np.arange(128, dtype=f32).reshape(128, 1),
                  w_ada=g('w_ada'), b_ada=g('b_ada').reshape(1, -1), w_in=g('w_in'), sbbias=sbb, brow=brow,
                  colp=colp, gsb_hd=gsb_hd, w_out=g('w_out'), lnp=lnp, w_up=g('w_up'), w_down=g('w_down'))
    in_maps = []
    for c in range(8):
        b, j = c // 4, c % 4
        nreal = OWN * (j + 1)
        xs = np.zeros((NSLOT, D), f32)
        xs[NSLOT - nreal:] = x_prompt[b, :nreal]
        valid = np.zeros(NSLOT, f32); valid[NSLOT - nreal:] = 1.0
        vmask = np.ascontiguousarray(valid.reshape(NBLK, 128).T)
        hflag = np.full((128, 1), 1.0 if j > 0 else 0.0, f32)
        c5 = np.concatenate([np.asarray(inp['c_prompt'], f32)[b:b + 1], np.asarray(inp['c_sample'], f32)[4 * c:4 * c + 4]], 0)
        c5p = np.zeros((32, D), f32); c5p[0:5] = c5
        c5T = np.ascontiguousarray(c5p.reshape(32, KC, 128).transpose(2, 1, 0))
        m = dict(shared)
        m.update(xs=xs, vmask=vmask, hflag=hflag, c5T=c5T,
                 x_s=np.ascontiguousarray(x_sample[4 * c:4 * c + 4].reshape(32, D)),
                 ptab=np.ascontiguousarray(page_table[4 * c:4 * c + 4].reshape(-1)),
                 st_conv=np.ascontiguousarray(np.asarray(inp['state_conv'], f32)[0, 4 * c:4 * c + 4]),
                 st_ffn=np.ascontiguousarray(np.asarray(inp['state_ffn'], f32)[0, 4 * c:4 * c + 4]))
        in_maps.append(m)
    return in_maps


def kernel(**inp):
    global _NC
    f32 = np.float32
    in_maps = make_in_maps(inp)
    if _NC is None:
        _NC = build_nc()
    res = run_bass_kernel_spmd(_NC, in_maps, core_ids=list(range(8)))
    R = res.results
    y_prompt = np.zeros((2, 8192, D), f32); k_prompt = np.zeros((1, 2, 8192, 8, 64), f32); v_prompt = np.zeros_like(k_prompt)
    conv_prompt = np.zeros((1, 2, 30, 512), f32); ffn_prompt = np.zeros((1, 2, 2, DFF), f32)
    y_sample = np.zeros((32, 8, D), f32); k_sample = np.zeros((1, 32, 8, 8, 64), f32); v_sample = np.zeros_like(k_sample)
    conv_sample = np.zeros((1, 32, 30, 512), f32); ffn_sample = np.zeros((1, 32, 2, DFF), f32)
    for c in range(8):
        b, j = c // 4, c % 4
        r = R[c]
        y_prompt[b, OWN * j:OWN * (j + 1)] = r['y_p']
        k_prompt[0, b, OWN * j:OWN * (j + 1)] = r['k_p'].reshape(OWN, 8, 64)
        v_prompt[0, b, OWN * j:OWN * (j + 1)] = r['v_p'].reshape(OWN, 8, 64)
        if j == 3:
            conv_prompt[0, b] = r['conv_p']; ffn_prompt[0, b] = r['ffn_p']
        y_sample[4 * c:4 * c + 4] = r['y_s'].reshape(4, 8, D)
        k_sample[0, 4 * c:4 * c + 4] = r['k_s'].reshape(4, 8, 8, 64)
        v_sample[0, 4 * c:4 * c + 4] = r['v_s'].reshape(4, 8, 8, 64)
        conv_sample[0, 4 * c:4 * c + 4] = r['conv_s']; ffn_sample[0, 4 * c:4 * c + 4] = r['ffn_s']
    return (y_prompt, y_sample, k_prompt, v_prompt, conv_prompt, ffn_prompt, k_sample, v_sample, conv_sample, ffn_sample)
```
